# Optimizing a Trainium2 kernel written in Bass

```python
import jax, jax.numpy as jnp
from jax import lax
import numpy as np

D_MODEL = 1024
BATCH = 4
SEQ = 4096
DEPTH = 1
DEC_BATCH = 128
DEC_SEQ = 4
PAST_LEN = 8192
PAGE_SIZE = 128

MIX_WIDTH = D_MODEL
ATT_WIDTH = MIX_WIDTH // 2
HG_WIDTH = MIX_WIDTH - ATT_WIDTH
ATT_HEAD_DIM = 64
ATT_HEADS = ATT_WIDTH // ATT_HEAD_DIM
DILATIONS = ((128, 1), (512, 4), (2048, 16))
ATT_WINDOW_MAX = 2048
ATT_BLOCK = 128
ROPE_THETA = 10000.0
HG_EXPAND = 128
HG_HEADS = HG_WIDTH // HG_EXPAND
HG_DK = HG_EXPAND
HG_DV = HG_WIDTH // HG_HEADS
HG_CHUNK = 32
D_FF = 2816
IN_COLS = 3 * ATT_WIDTH + 4 * HG_WIDTH
IN_SPLITS = [ATT_WIDTH, 2 * ATT_WIDTH, 3 * ATT_WIDTH,
             3 * ATT_WIDTH + HG_WIDTH, 3 * ATT_WIDTH + 2 * HG_WIDTH, 3 * ATT_WIDTH + 3 * HG_WIDTH]
DN_ALPHA = (2.0 * DEPTH) ** 0.25
DN_BETA = (8.0 * DEPTH) ** -0.25
LN_EPS = 1e-5
NEG_INF = -1e30

kernel_name = "hymba_longnet_hgrn2_macaron_deepnorm_step"


def layer_norm(x, g, b):
    xf = x.astype(jnp.float32)
    mu = jnp.mean(xf, axis=-1, keepdims=True)
    var = jnp.mean(jnp.square(xf - mu), axis=-1, keepdims=True)
    return ((xf - mu) * lax.rsqrt(var + LN_EPS)).astype(x.dtype) * g + b


def head_rms(o):
    of = o.astype(jnp.float32)
    return (of * lax.rsqrt(jnp.mean(of * of, axis=-1, keepdims=True) + LN_EPS)).astype(o.dtype)


def swiglu(x, w_gate, w_up, w_down):
    return (jax.nn.silu(x @ w_gate) * (x @ w_up)) @ w_down


def rope(x, pos):
    half = x.shape[-1] // 2
    inv = ROPE_THETA ** (-jnp.arange(half, dtype=jnp.float32) / half)
    ang = pos.astype(jnp.float32)[:, None] * inv[None, :]
    cos = jnp.cos(ang)[:, None, :].astype(x.dtype)
    sin = jnp.sin(ang)[:, None, :].astype(x.dtype)
    x1, x2 = x[..., :half], x[..., half:]
    return jnp.concatenate([x1 * cos - x2 * sin, x2 * cos + x1 * sin], axis=-1)


def project(x, w_in, lb, pos):
    b, t, _ = x.shape
    z = x @ w_in
    qa, ka, va, qh, fh, ih, gh = jnp.split(z, IN_SPLITS, axis=-1)
    qa = rope(qa.reshape(b, t, ATT_HEADS, ATT_HEAD_DIM), pos)
    ka = rope(ka.reshape(b, t, ATT_HEADS, ATT_HEAD_DIM), pos)
    va = va.reshape(b, t, ATT_HEADS, ATT_HEAD_DIM)
    zf = fh.astype(jnp.float32).reshape(b, t, HG_HEADS, HG_DK)
    lbh = lb.reshape(HG_HEADS, HG_DK)
    logf = jnp.log(lbh + (1.0 - lbh) * jax.nn.sigmoid(zf))
    kh = (1.0 - lbh) * jax.nn.sigmoid(-zf)
    qh = jax.nn.silu(qh).reshape(b, t, HG_HEADS, HG_DK)
    ih = ih.reshape(b, t, HG_HEADS, HG_DV)
    return (qa, ka, va), (qh, kh, ih, logf), gh


def dilated_branch_prompt(q, k, v, window, dil):
    b, t, h, e = q.shape
    w_sub = window // dil
    span = dil * ATT_BLOCK
    t_pad = -(-t // span) * span
    n_blk = t_pad // span

    def to_sub(a):
        a = jnp.pad(a, ((0, 0), (0, t_pad - t), (0, 0), (0, 0)))
        a = a.reshape(b, t_pad // dil, dil, h, e).transpose(0, 2, 1, 3, 4)
        return a.reshape(b, dil, n_blk, ATT_BLOCK, h, e)

    qs, ks, vs = to_sub(q), to_sub(k), to_sub(v)
    prev = ((0, 0), (0, 0), (1, 0), (0, 0), (0, 0), (0, 0))
    kb = jnp.concatenate([jnp.pad(ks, prev)[:, :, :-1], ks], axis=3)
    vb = jnp.concatenate([jnp.pad(vs, prev)[:, :, :-1], vs], axis=3)
    qi = jnp.arange(ATT_BLOCK)[:, None]
    kc = jnp.arange(2 * ATT_BLOCK)[None, :]
    dist = qi + ATT_BLOCK - kc
    rel = (dist >= 0) & (dist <= w_sub)
    has_prev = (jnp.arange(n_blk) > 0)[:, None, None] | (kc >= ATT_BLOCK)[None]
    mask = rel[None] & has_prev
    s = jnp.einsum('bdnqhe,bdnkhe->bdnhqk', qs, kb).astype(jnp.float32) * (e ** -0.5)
    s = jnp.where(mask[None, None, :, None], s, NEG_INF)
    lse = jax.nn.logsumexp(s, axis=-1)
    p = jnp.exp(s - lse[..., None]).astype(v.dtype)
    o = jnp.einsum('bdnhqk,bdnkhe->bdnqhe', p, vb)
    o = o.reshape(b, dil, t_pad // dil, h, e).transpose(0, 2, 1, 3, 4).reshape(b, t_pad, h, e)[:, :t]
    lse = lse.transpose(0, 1, 2, 4, 3).reshape(b, dil, t_pad // dil, h)
    lse = lse.transpose(0, 2, 1, 3).reshape(b, t_pad, h)[:, :t]
    return o, lse


def dilated_branch_sample(q, k_all, v_all, past_rows, window, dil):
    s_len, e = q.shape[1], q.shape[-1]
    n_keys = window // dil + 1
    idx = past_rows + jnp.arange(s_len)[:, None] - dil * jnp.arange(n_keys)[None, :]
    valid = idx >= 0
    idx = jnp.maximum(idx, 0)
    kg = k_all[:, idx]
    vg = v_all[:, idx]
    s = jnp.einsum('bshe,bsjhe->bshj', q, kg).astype(jnp.float32) * (e ** -0.5)
    s = jnp.where(valid[None, :, None, :], s, NEG_INF)
    lse = jax.nn.logsumexp(s, axis=-1)
    p = jnp.exp(s - lse[..., None]).astype(v_all.dtype)
    o = jnp.einsum('bshj,bsjhe->bshe', p, vg)
    return o, lse


def combine_branches(branches):
    lse = jnp.stack([l for _, l in branches], axis=0)
    wts = jax.nn.softmax(lse, axis=0)
    o = sum(wts[i][..., None] * branches[i][0].astype(jnp.float32) for i in range(len(branches)))
    return o.astype(branches[0][0].dtype)


def hgrn2_chunked(q, k, v, logf, s0):
    b, t, h, dk = q.shape
    dv = v.shape[-1]
    t_pad = -(-t // HG_CHUNK) * HG_CHUNK
    n_ch = t_pad // HG_CHUNK

    def chunks(a):
        a = jnp.pad(a.astype(jnp.float32), ((0, 0), (0, t_pad - t), (0, 0), (0, 0)))
        return a.reshape(b, n_ch, HG_CHUNK, h, a.shape[-1])

    qc, kc, vc, gc = chunks(q), chunks(k), chunks(v), chunks(logf)
    cum = jnp.cumsum(gc, axis=2)
    cum_end = cum[:, :, -1:]
    q_dec = qc * jnp.exp(cum)
    k_dec = kc * jnp.exp(-cum)
    k_end = kc * jnp.exp(cum_end - cum)
    causal = jnp.tril(jnp.ones((HG_CHUNK, HG_CHUNK), dtype=bool))
    a = jnp.einsum('bnihk,bnjhk->bnhij', q_dec, k_dec)
    a = jnp.where(causal, a, 0.0)
    o_intra = jnp.einsum('bnhij,bnjhv->bnihv', a, vc)
    u = jnp.einsum('bnjhk,bnjhv->bnhkv', k_end, vc)
    decay = jnp.exp(cum_end[:, :, 0])

    def step(state, inp):
        dec, upd = inp
        return dec[..., None] * state + upd, state

    s_final, s_start = lax.scan(step, s0.astype(jnp.float32),
                                (decay.transpose(1, 0, 2, 3), u.transpose(1, 0, 2, 3, 4)))
    s_start = s_start.transpose(1, 0, 2, 3, 4)
    o_inter = jnp.einsum('bnihk,bnhkv->bnihv', q_dec, s_start)
    o = (o_intra + o_inter).reshape(b, t_pad, h, dv)[:, :t]
    return o.astype(v.dtype), s_final.astype(s0.dtype)


def mix_out(o_att, o_hg, gh, hg_norm_g, w_out):
    b, t = o_att.shape[:2]
    o_hg = head_rms(o_hg).reshape(b, t, HG_WIDTH) * hg_norm_g * jax.nn.silu(gh)
    cat = jnp.concatenate([o_att.reshape(b, t, ATT_WIDTH), o_hg.astype(o_att.dtype)], axis=-1)
    return cat @ w_out


def setup_inputs(seed: int = 0) -> dict:
    key = jax.random.key(seed)
    ks = jax.random.split(key, 24)
    win_buf = min(ATT_WINDOW_MAX, PAST_LEN)

    def nrm(k, shape, scale):
        return jax.random.normal(k, shape, jnp.float32) * scale

    col_scale = jnp.ones((IN_COLS,), jnp.float32)
    col_scale = col_scale.at[2 * ATT_WIDTH:3 * ATT_WIDTH].set(DN_BETA)
    col_scale = col_scale.at[3 * ATT_WIDTH + 2 * HG_WIDTH:3 * ATT_WIDTH + 3 * HG_WIDTH].set(DN_BETA)
    return {
        "x_prompt": nrm(ks[0], (BATCH, SEQ, D_MODEL), 1.0),
        "x_sample": nrm(ks[1], (DEC_BATCH, DEC_SEQ, D_MODEL), 1.0),
        "cache_k": nrm(ks[2], (DEPTH, DEC_BATCH, win_buf, ATT_HEADS, ATT_HEAD_DIM), 1.0),
        "cache_v": nrm(ks[3], (DEPTH, DEC_BATCH, win_buf, ATT_HEADS, ATT_HEAD_DIM), DN_BETA),
        "state_hgrn": nrm(ks[4], (DEPTH, DEC_BATCH, HG_HEADS, HG_DK, HG_DV), 0.5),
        "ffn1_w_gate": nrm(ks[5], (DEPTH, D_MODEL, D_FF), D_MODEL ** -0.5),
        "ffn1_w_up": nrm(ks[6], (DEPTH, D_MODEL, D_FF), D_MODEL ** -0.5),
        "ffn1_w_down": nrm(ks[7], (DEPTH, D_FF, D_MODEL), DN_BETA * D_FF ** -0.5),
        "ln1_g": 1.0 + nrm(ks[8], (DEPTH, D_MODEL), 0.1),
        "ln1_b": nrm(ks[9], (DEPTH, D_MODEL), 0.02),
        "w_in": nrm(ks[10], (DEPTH, D_MODEL, IN_COLS), D_MODEL ** -0.5) * col_scale,
        "hg_lower_bound": nrm(ks[11], (DEPTH + 1, HG_WIDTH), 0.1),
        "hg_norm_g": 1.0 + nrm(ks[12], (DEPTH, HG_WIDTH), 0.1),
        "w_out": nrm(ks[13], (DEPTH, MIX_WIDTH, D_MODEL), DN_BETA * MIX_WIDTH ** -0.5),
        "ln2_g": 1.0 + nrm(ks[14], (DEPTH, D_MODEL), 0.1),
        "ln2_b": nrm(ks[15], (DEPTH, D_MODEL), 0.02),
        "ffn2_w_gate": nrm(ks[16], (DEPTH, D_MODEL, D_FF), D_MODEL ** -0.5),
        "ffn2_w_up": nrm(ks[17], (DEPTH, D_MODEL, D_FF), D_MODEL ** -0.5),
        "ffn2_w_down": nrm(ks[18], (DEPTH, D_FF, D_MODEL), DN_BETA * D_FF ** -0.5),
        "ln3_g": 1.0 + nrm(ks[19], (DEPTH, D_MODEL), 0.1),
        "ln3_b": nrm(ks[20], (DEPTH, D_MODEL), 0.02),
    }


def reference(x_prompt, x_sample, cache_k, cache_v, state_hgrn,
              ffn1_w_gate, ffn1_w_up, ffn1_w_down, ln1_g, ln1_b,
              w_in, hg_lower_bound, hg_norm_g, w_out, ln2_g, ln2_b,
              ffn2_w_gate, ffn2_w_up, ffn2_w_down, ln3_g, ln3_b):
    lower_bounds = jnp.cumsum(jax.nn.softmax(hg_lower_bound.astype(jnp.float32), axis=0), axis=0)
    t_p = x_prompt.shape[1]
    t_s = x_sample.shape[1]
    past_rows = cache_k.shape[2]
    win_p = min(ATT_WINDOW_MAX, t_p)
    pos_p = jnp.arange(t_p, dtype=jnp.int32)
    pos_s = PAST_LEN + jnp.arange(t_s, dtype=jnp.int32)
    xp, xs = x_prompt, x_sample
    kp_l, vp_l, sp_l, ks_l, vs_l, ss_l = [], [], [], [], [], []
    for l in range(DEPTH):
        xp = layer_norm(DN_ALPHA * xp + 0.5 * swiglu(xp, ffn1_w_gate[l], ffn1_w_up[l], ffn1_w_down[l]),
                        ln1_g[l], ln1_b[l])
        xs = layer_norm(DN_ALPHA * xs + 0.5 * swiglu(xs, ffn1_w_gate[l], ffn1_w_up[l], ffn1_w_down[l]),
                        ln1_g[l], ln1_b[l])

        (qa, ka, va), hg, gh = project(xp, w_in[l], lower_bounds[l], pos_p)
        o_att = combine_branches([dilated_branch_prompt(qa, ka, va, w, d) for (w, d) in DILATIONS])
        s0 = jnp.zeros((xp.shape[0], HG_HEADS, HG_DK, HG_DV), state_hgrn.dtype)
        o_hg, s_p = hgrn2_chunked(hg[0], hg[1], hg[2], hg[3], s0)
        mix_p = mix_out(o_att, o_hg, gh, hg_norm_g[l], w_out[l])
        kp_l.append(ka[:, t_p - win_p:])
        vp_l.append(va[:, t_p - win_p:])
        sp_l.append(s_p)

        (qa_s, ka_s, va_s), hg_s, gh_s = project(xs, w_in[l], lower_bounds[l], pos_s)
        k_all = jnp.concatenate([cache_k[l].astype(ka_s.dtype), ka_s], axis=1)
        v_all = jnp.concatenate([cache_v[l].astype(va_s.dtype), va_s], axis=1)
        o_att_s = combine_branches([dilated_branch_sample(qa_s, k_all, v_all, past_rows, w, d)
                                    for (w, d) in DILATIONS])
        o_hg_s, s_s = hgrn2_chunked(hg_s[0], hg_s[1], hg_s[2], hg_s[3], state_hgrn[l])
        mix_s = mix_out(o_att_s, o_hg_s, gh_s, hg_norm_g[l], w_out[l])
        ks_l.append(k_all[:, -past_rows:])
        vs_l.append(v_all[:, -past_rows:])
        ss_l.append(s_s)

        xp = layer_norm(DN_ALPHA * xp + mix_p, ln2_g[l], ln2_b[l])
        xs = layer_norm(DN_ALPHA * xs + mix_s, ln2_g[l], ln2_b[l])

        xp = layer_norm(DN_ALPHA * xp + 0.5 * swiglu(xp, ffn2_w_gate[l], ffn2_w_up[l], ffn2_w_down[l]),
                        ln3_g[l], ln3_b[l])
        xs = layer_norm(DN_ALPHA * xs + 0.5 * swiglu(xs, ffn2_w_gate[l], ffn2_w_up[l], ffn2_w_down[l]),
                        ln3_g[l], ln3_b[l])
    return (xp, xs, jnp.stack(kp_l), jnp.stack(vp_l), jnp.stack(sp_l),
            jnp.stack(ks_l), jnp.stack(vs_l), jnp.stack(ss_l))
```

```python
import numpy as np
from contextlib import ExitStack
import concourse.bass as bass
import concourse.mybir as mybir
from concourse.bass_utils import run_bass_kernel_spmd

F32 = mybir.dt.float32
BF16 = mybir.dt.bfloat16
AF = mybir.ActivationFunctionType
ALU = mybir.AluOpType

NCORES = 8
D = 1024
FF = 2816
NK = FF // 128
L = 2048
NS = 64
NTOK = 2 * L + NS
NMAIN = L + NS
WINC = 4608
ALPHA = 2.0 ** 0.25
EPS = 1e-5
NCST = 1080
import os
NSQ = int(os.environ.get("KNSQ", "16"))
KLVL = int(os.environ.get("KLVL", "99"))
C_MCUR, C_MPREV, C_MPREVH, C_M64, C_RMASK, C_RMASKS, C_FLAG = 0, 128, 256, 384, 448, 960, 1024
C_LB, C_GN, C_MULT, C_MNEW, C_MS4 = 1025, 1033, 1037, 1065, 1069
W_QA, W_KA, W_VA, W_QH, W_FH, W_IH, W_GH, W_QS, W_KS = 0, 512, 1024, 1536, 2048, 2560, 3072, 3584, 4096

ENGS = ("pe", "act", "dve", "pool", "sp")
PSUM_TOK = {"tp", "pg", "pu", "pd", "pz", "psc", "po", "pub", "pa", "pob", "tpb", "pos"}


def I(name, *a, **k):
    return lambda e: getattr(e, name)(*a, **k)


class Op:
    __slots__ = ("eng", "fn", "deps", "dma", "sig", "sem", "val")

    def __init__(self, eng, fn, dma):
        self.eng, self.fn, self.dma = eng, fn, dma
        self.deps, self.sig, self.sem, self.val = [], dma, None, 0


class Prog:
    def __init__(self, nc, stack):
        self.nc = nc
        self.esem = {e: stack.enter_context(nc.semaphore("es_" + e)) for e in ENGS}
        self.ecount = {e: 0 for e in ENGS}
        self.dsems = {q: [stack.enter_context(nc.semaphore("ds_%s%d" % (q, i))) for i in range(n)]
                      for q, n in (("sp", 24), ("pool", 12), ("act", 6))}
        self.dval = {}
        self.drr = {q: 0 for q in self.dsems}
        self.reset()

    def reset(self):
        self.ops = {e: [] for e in ENGS}
        self.last_w, self.readers = {}, {}
        self.gen_deps = {}
        self.dlast = {}

    def op(self, eng, fn, r=(), w=(), dma=False):
        o = Op(eng, fn, dma)
        deps = []
        for t in r:
            if t in self.last_w:
                deps.extend(self.last_w[t])
            if t[0] in PSUM_TOK:
                rd = self.readers.get(t)
                if rd:
                    deps.extend(x for en, x in rd[0].items() if en != eng)
        append_w = set()
        for t in w:
            rd = self.readers.get(t)
            has_rd = bool(rd and (rd[0] or rd[1]))
            lw = self.last_w.get(t)
            if dma and lw and not has_rd and all(x.dma for x in lw):
                append_w.add(t)
                deps.extend(self.gen_deps.get(t, ()))
                continue
            gd = []
            if lw:
                gd.extend(lw)
            if rd:
                gd.extend(rd[0].values())
                gd.extend(rd[1])
            self.gen_deps[t] = gd
            deps.extend(gd)
        for t in r:
            rd = self.readers.setdefault(t, ({}, []))
            if dma:
                rd[1].append(o)
            else:
                rd[0][eng] = o
        for t in w:
            if t in append_w:
                self.last_w[t].append(o)
            else:
                self.last_w[t] = [o]
                self.readers[t] = ({}, [])
        if dma:
            ring = self.dsems[eng]
            s = ring[self.drr[eng] % len(ring)]
            self.drr[eng] += 1
            if s in self.dlast:
                deps.append(self.dlast[s])
            self.dlast[s] = o
            o.sem = s
            self.dval[s] = self.dval.get(s, 0) + 16
            o.val = self.dval[s]
        seen = set()
        for d in deps:
            if d is o or id(d) in seen:
                continue
            seen.add(id(d))
            if d.eng == "pe" and eng == "pe" and not d.dma:
                continue
            d.sig = True
            o.deps.append(d)
        self.ops[eng].append(o)
        return o

    def dma(self, q, out, in_, r=(), w=()):
        return self.op(q, I("dma_start", out=out, in_=in_), r=r, w=w, dma=True)

    def flush(self, name="ph"):
        nc = self.nc
        finals = []
        for e in ENGS:
            lst = self.ops[e]
            comp = [o for o in lst if not o.dma]
            if comp:
                comp[-1].sig = True
            finals.extend(o for o in lst if o.dma)
        for e in ENGS:
            for o in self.ops[e]:
                if not o.dma and o.sig:
                    self.ecount[e] += 1
                    o.sem, o.val = self.esem[e], self.ecount[e]
        last_comp = {}
        for e in ENGS:
            comp = [o for o in self.ops[e] if not o.dma]
            if comp:
                last_comp[e] = comp[-1]
        dma_final = {}
        for o in finals:
            dma_final[o.sem] = max(dma_final.get(o.sem, 0), o.val)

        def emit(eng_name):
            def body(eng):
                seen = {}
                for o in self.ops[eng_name]:
                    for d in o.deps:
                        if seen.get(d.sem, 0) < d.val:
                            eng.wait_ge(d.sem, d.val)
                            seen[d.sem] = d.val
                    ins = o.fn(eng)
                    if o.sig:
                        ins.then_inc(o.sem, 16 if o.dma else 1)
                for e2, lo in last_comp.items():
                    if seen.get(lo.sem, 0) < lo.val:
                        eng.wait_ge(lo.sem, lo.val)
                for s, v in dma_final.items():
                    if seen.get(s, 0) < v:
                        eng.wait_ge(s, v)
            return body

        with nc.named_scope(name), nc.Block() as block:
            block.tensor(emit("pe"))
            block.scalar(emit("act"))
            block.vector(emit("dve"))
            block.gpsimd(emit("pool"))
            block.sync(emit("sp"))
        self.reset()


def ffn_pieces():
    return [(0, 6), (6, 12), (12, 17), (17, 22)]


def piece_of(k):
    for i, (a, b) in enumerate(ffn_pieces()):
        if a <= k < b:
            return i
    raise ValueError


def build_program(PHASES=("A", "B1", "B2", "B3", "C")):
    nc = bass.Bass("TRN2", target_bir_lowering=False)

    def din(name, shape):
        return nc.dram_tensor(name, shape, F32, kind="ExternalInput").ap()

    def dout(name, shape):
        return nc.dram_tensor(name, shape, F32, kind="ExternalOutput").ap()

    xin = din("xin", [NTOK, D])
    ck = din("ck", [NSQ, 2048, 512])
    cv = din("cv", [NSQ, 2048, 512])
    st = din("st", [64 * 128, 128])
    w1g, w1u, w1d = din("w1g", [D + 1, FF])[0:D, :], din("w1u", [D + 1, FF])[0:D, :], din("w1d", [FF + 1, D])[0:FF, :]
    w2g, w2u, w2d = din("w2g", [D + 1, FF])[0:D, :], din("w2u", [D + 1, FF])[0:D, :], din("w2d", [FF + 1, D])[0:FF, :]
    win = din("win", [D + 1, WINC])[0:D, :]
    wout = din("wout", [D + 1, D])[0:D, :]
    lnv = din("lnv", [6, D])
    cst = din("cst", [128, NCST])
    rope = din("rope", [2, 128, NTOK])

    y_o = dout("y", [NMAIN, D])
    kT_o = dout("kTo", [512, L])
    v_o = dout("vo", [L, 512])
    s_o = dout("so", [128, 4, 128])
    nk_o = dout("nk", [NSQ, 2048, 512])
    nv_o = dout("nv", [NSQ, 2048, 512])
    ns_o = dout("ns", [128, 64, 128])

    import os
    if os.environ.get("KDBG"):
        x1s = dout("x1s", [NTOK, D])
        x2s = dout("x2s", [NMAIN, D])
    else:
        x1s = nc.dram_tensor("x1s", [NTOK, D], F32).ap()
        x2s = nc.dram_tensor("x2s", [NMAIN, D], F32).ap()
    winb = nc.dram_tensor("winb", [D, WINC], BF16).ap()
    woutb = nc.dram_tensor("woutb", [D, D], BF16).ap()
    cats = nc.dram_tensor("cats", [64, 8, NMAIN], BF16).ap()

    gstack = ExitStack()
    with gstack:
        P = Prog(nc, gstack)

        def sb(stack, name, shape, dt=F32):
            return stack.enter_context(nc.sbuf_tensor(name, shape, dt))

        def ps(stack, name, shape, dt=F32):
            return stack.enter_context(nc.psum_tensor(name, shape, dt))

        def make_ident(stk, tagp):
            identf = sb(stk, "identf" + tagp, [128, 128], F32)
            P.op("pool", I("memset", identf[:], 0.0), w=[("identf",)])
            P.op("pool", I("affine_select", out=identf[:], in_=identf[:], pattern=[[-1, 128]],
                                                   compare_op=ALU.not_equal, fill=1.0, base=0,
                                                   channel_multiplier=1), w=[("identf",)])
            return identf

        def load_T(src, row0, T, xt, tp, xT, identf, tagp):
            nt = (T + 127) // 128
            for t in range(nt):
                tt = min(128, T - 128 * t)
                r0 = row0 + 128 * t
                bi = load_T.cnt % 2
                load_T.cnt += 1
                P.dma("sp", xt[:tt, bi, :], src[r0:r0 + tt, :], w=[("xt", bi)])
                for cg in range(2):
                    bk = load_T.tcnt % 2
                    load_T.tcnt += 1
                    for j in range(4):
                        c = 4 * cg + j
                        P.op("pe", I("transpose",
                            tp[bk][:, j * 128:j * 128 + tt], xt[:tt, bi, c * 128:(c + 1) * 128], identf[:tt, :tt]),
                            r=[("xt", bi), ("identf",)], w=[("tp", bk)])
                    P.op("act", I("copy",
                        out=xT[:, 4 * cg:4 * cg + 4, t * 128:t * 128 + tt],
                        in_=tp[bk][:, :].rearrange("p (j n) -> p j n", j=4)[:, :, :tt]),
                        r=[("tp", bk)], w=[("xT", t)])
        load_T.cnt = 0
        load_T.tcnt = 0

        def ln_epilogue(T, nt, src, row0, dst, drow0, pd_groups, yb, gb, stats, mv, rstd, emit_mm):
            for t in range(nt):
                tt = min(128, T - 128 * t)
                P.dma("sp", yb[:tt, t, :], src[row0 + 128 * t:row0 + 128 * t + tt, :], w=[("yb", t)])
                for hf in range(2):
                    bk = ln_epilogue.cnt % 2
                    ln_epilogue.cnt += 1
                    pdt, pdtok = pd_groups[bk]
                    emit_mm(t, hf, pdt, pdtok, tt)
                    P.op("dve", I("scalar_tensor_tensor",
                        out=yb[:tt, t, hf * 512:(hf + 1) * 512], in0=yb[:tt, t, hf * 512:(hf + 1) * 512],
                        scalar=ALPHA, in1=pdt[:tt, :], op0=ALU.mult, op1=ALU.add),
                        r=[pdtok], w=[("yb", t)])
                    P.op("dve", I("bn_stats",
                        out=stats[:tt, t, hf, :], in_=yb[:tt, t, hf * 512:(hf + 1) * 512]),
                        r=[("yb", t)], w=[("stats", t)])
                P.op("dve", I("bn_aggr",
                    out=mv[:tt, t, :], in_=stats[:tt, t, :, :].rearrange("p a b -> p (a b)")),
                    r=[("stats", t)], w=[("mv",)])
            pp = min(128, T)
            P.op("act", I("activation", out=rstd[:pp, 0:nt], in_=mv[:pp, 0:nt, 1], func=AF.Sqrt,
                                               bias=epsb[:pp, :], scale=1.0),
                 r=[("mv",)], w=[("rstd",)])
            P.op("dve", I("reciprocal", out=rstd[:pp, 0:nt], in_=rstd[:pp, 0:nt]), w=[("rstd",)])
            for t in range(nt):
                tt = min(128, T - 128 * t)
                P.op("dve", I("tensor_scalar",
                    out=yb[:tt, t, :], in0=yb[:tt, t, :], scalar1=mv[:tt, t, 0:1], scalar2=rstd[:tt, t:t + 1],
                    op0=ALU.subtract, op1=ALU.mult), r=[("mv",), ("rstd",)], w=[("yb", t)])
                P.op("pool", I("tensor_tensor",
                    out=yb[:tt, t, :], in0=yb[:tt, t, :], in1=gb[:tt, 0, :], op=ALU.mult),
                    r=[("gb",)], w=[("yb", t)])
                P.op("pool", I("tensor_tensor",
                    out=yb[:tt, t, :], in0=yb[:tt, t, :], in1=gb[:tt, 1, :], op=ALU.add),
                    r=[("gb",)], w=[("yb", t)])
                for (dd, dr) in (dst if isinstance(dst, list) else [(dst, drow0)]):
                    P.dma("sp", dd[dr + 128 * t:dr + 128 * t + tt, :], yb[:tt, t, :], r=[("yb", t)])
        ln_epilogue.cnt = 0

        def load_gb(gb, grow):
            P.dma("sp", gb[:, 0, :], lnv[grow:grow + 1, :].partition_broadcast(128), w=[("gb",)])
            P.dma("sp", gb[:, 1, :], lnv[grow + 1:grow + 2, :].partition_broadcast(128), w=[("gb",)])

        epsb = sb(gstack, "epsb", [128, 1], F32)
        P.op("pool", I("memset", epsb[:], EPS), w=[("epsb",)])

        def ffn_phase(tagp, src, groups, dsts, wgd, wud, wdd, grow, extra=None):
            with ExitStack() as stk:
                wg = sb(stk, "wg" + tagp, [128, 8, FF], BF16)
                wu = sb(stk, "wu" + tagp, [128, 8, FF], BF16)
                wd = sb(stk, "wd" + tagp, [128, NK, D], BF16)
                xt = sb(stk, "xt" + tagp, [128, 2, D], F32)
                xT = sb(stk, "xT" + tagp, [128, 8, 512], BF16)
                hT = sb(stk, "hT" + tagp, [128, NK, 512], BF16)
                sg = sb(stk, "sg" + tagp, [128, 2, 512], F32)
                yb = sb(stk, "yb" + tagp, [128, 4, D], F32)
                gb = sb(stk, "gb" + tagp, [128, 2, D], F32)
                stats = sb(stk, "stats" + tagp, [128, 4, 2, 6], F32)
                mv = sb(stk, "mv" + tagp, [128, 4, 2], F32)
                rstd = sb(stk, "rstd" + tagp, [128, 4], F32)
                tp = [ps(stk, "tp%d%s" % (i, tagp), [128, 512], F32) for i in range(2)]
                pg = [ps(stk, "pg%d%s" % (i, tagp), [128, 512], F32) for i in range(2)]
                pu = [ps(stk, "pu%d%s" % (i, tagp), [128, 512], F32) for i in range(2)]
                pd = [ps(stk, "pd%d%s" % (i, tagp), [128, 512], F32) for i in range(2)]
                identf = make_ident(stk, tagp)
                wgv = wgd.rearrange("(c p) n -> p c n", p=128)
                wuv = wud.rearrange("(c p) n -> p c n", p=128)
                wdv = wdd.rearrange("(k p) n -> p k n", p=128)
                for i, (a, b) in enumerate(ffn_pieces()):
                    P.dma("pool", wg[:, :, a * 128:b * 128], wgv[:, :, a * 128:b * 128], w=[("wg", i)])
                    P.dma("pool", wu[:, :, a * 128:b * 128], wuv[:, :, a * 128:b * 128], w=[("wu", i)])
                for i, (a, b) in enumerate(ffn_pieces()):
                    P.dma("pool", wd[:, a:b, :], wdv[:, a:b, :], w=[("wd", i)])
                load_gb(gb, grow)

                def gate_up(T):
                    nt = (T + 127) // 128
                    xr = [("xT", t) for t in range(nt)]
                    for k in range(NK):
                        bk = k % 2
                        pc = piece_of(k)
                        for c in range(8):
                            P.op("pe", I("matmul",
                                pg[bk][:, :T], wg[:, c, k * 128:(k + 1) * 128], xT[:, c, :T],
                                start=(c == 0), stop=(c == 7)), r=xr + [("wg", pc)], w=[("pg", bk)])
                        for c in range(8):
                            P.op("pe", I("matmul",
                                pu[bk][:, :T], wu[:, c, k * 128:(k + 1) * 128], xT[:, c, :T],
                                start=(c == 0), stop=(c == 7)), r=xr + [("wu", pc)], w=[("pu", bk)])
                        P.op("act", I("activation", out=sg[:, bk, :T], in_=pg[bk][:, :T], func=AF.Silu),
                             r=[("pg", bk)], w=[("sg", bk)])
                        P.op("dve", I("scalar_tensor_tensor",
                            out=hT[:, k, :T], in0=sg[:, bk, :T], scalar=0.5, in1=pu[bk][:, :T],
                            op0=ALU.mult, op1=ALU.mult), r=[("sg", bk), ("pu", bk)], w=[("hT", k)])

                def down_mm(t, hf, pdt, pdtok, tt):
                    for k in range(NK):
                        P.op("pe", I("matmul",
                            pdt[:tt, :], hT[:, k, t * 128:t * 128 + tt], wd[:, k, hf * 512:(hf + 1) * 512],
                            start=(k == 0), stop=(k == NK - 1)),
                            r=[("hT", k), ("wd", piece_of(k))], w=[pdtok])

                load_T(src, groups[0][0], groups[0][1], xt, tp, xT, identf, tagp)
                for gi, (row0, T) in enumerate(groups):
                    nt = (T + 127) // 128
                    gate_up(T)
                    if gi + 1 < len(groups):
                        load_T(src, groups[gi + 1][0], groups[gi + 1][1], xt, tp, xT, identf, tagp)
                    if extra is not None:
                        extra(gi)
                    ln_epilogue(T, nt, src, row0, dsts(gi), None, [(pd[0], ("pd", 0)), (pd[1], ("pd", 1))], yb, gb, stats, mv, rstd, down_mm)
                P.flush("ffn" + tagp)

        groupsA = [(512 * g, 512) for g in range(8)] + [(2 * L, NS)]

        def cast_weights(gi):
            if gi == 0:
                for i in range(4):
                    P.dma("pool", winb[i * 256:(i + 1) * 256, :], win[i * 256:(i + 1) * 256, :])
                P.dma("pool", woutb[:, :], wout[:, :])
            if gi < 8:
                for b in (2 * gi, 2 * gi + 1):
                    if b >= NSQ:
                        continue
                    P.dma("act", nk_o[b, 0:2044, :], ck[b, 4:2048, :])
                    P.dma("act", nv_o[b, 0:2044, :], cv[b, 4:2048, :])

        ffn_phase("A", xin, groupsA, lambda gi: [(x1s, groupsA[gi][0])], w1g, w1u, w1d, 0, extra=cast_weights)

        winv = winb.rearrange("(c p) n -> p c n", p=128)
        if "B1" not in PHASES:
            return nc

        with ExitStack() as stk:
            xt = sb(stk, "xtB", [128, 2, D], F32)
            xT = sb(stk, "xTB", [128, 8, 512], BF16)
            wi = sb(stk, "wiB", [128, 2, 8, 512], BF16)
            kT = sb(stk, "kTB", [128, 4, 2 * L], BF16)
            qT = sb(stk, "qTB", [128, 4, 512], BF16)
            V16 = sb(stk, "V16", [128, 32, 8, 65], BF16)
            V4 = sb(stk, "V4", [128, 2, 4, 8, 65], BF16)
            V1 = sb(stk, "V1", [128, 2, 4, 8, 65], BF16)
            cs = sb(stk, "csB", [128, 2, 512], F32)
            t1 = sb(stk, "t1B", [128, 2, 512], F32)
            t2 = sb(stk, "t2B", [128, 2, 512], F32)
            kf = sb(stk, "kfB", [128, 2, 512], F32)
            vf = sb(stk, "vfB", [128, 2, 512], F32)
            pT = sb(stk, "pTB", [128, 4, 512], BF16)
            pE = sb(stk, "pEB", [128, 4, 512], BF16)
            oacc = sb(stk, "oaccB", [65, 2, 512], F32)
            rden = sb(stk, "rdenB", [65, 2, 512], F32)
            catA = sb(stk, "catAB", [64, 8, 512], BF16)
            cstb = sb(stk, "cstbB", [128, 448], BF16)
            onesf = sb(stk, "onesfB", [128, 64], F32)
            onesh = sb(stk, "oneshB", [128, 64], BF16)
            rhi = sb(stk, "rhiB", [65, 2, 512], BF16)
            rlo = sb(stk, "rloB", [65, 2, 512], BF16)
            identf = make_ident(stk, "B")
            tp = [ps(stk, "tpB%d" % i, [128, 512], F32) for i in range(2)]
            pz = [ps(stk, "pzB%d" % i, [128, 512], F32) for i in range(2)]
            psc = [ps(stk, "pscB%d" % i, [128, 512], F32) for i in range(2)]
            po = [ps(stk, "poB%d" % i, [128, 512], F32) for i in range(2)]

            P.dma("pool", cstb[:, :], cst[:, 0:448], w=[("cstb",)])
            P.op("pool", I("memset", onesf[:], 1.0), w=[("onesf",)])
            P.op("pool", I("memset", onesh[:], 1.0), w=[("onesf",)])
            P.op("pool", I("memset", V16[:, :, :, 64:65], 1.0), w=[("V16i",)])
            P.op("pool", I("memset", V4[:, :, :, :, 64:65], 1.0), w=[("V4i",)])
            P.op("pool", I("memset", V1[:, :, :, :, 64:65], 1.0), w=[("V1i",)])
            mcur = cstb[:, C_MCUR:C_MCUR + 128]
            mprev = cstb[:, C_MPREV:C_MPREV + 128]
            mprevh = cstb[:, C_MPREVH:C_MPREVH + 128]

            wcnt = [0]

            def load_wblock(col0):
                bi = wcnt[0] % 2
                wcnt[0] += 1
                P.dma("sp", wi[:, bi, :, :], winv[:, :, col0:col0 + 512], w=[("wi", bi)])
                return bi

            zc = [0]
            sc_cnt = [0]
            po_cnt = [0]
            pt_cnt = [0]

            for g in range(8):
                main = g >= 4
                gm = g - 4
                tok0 = 512 * g
                load_T(x1s, tok0, 512, xt, tp, xT, identf, "B")
                P.dma("sp", cs[:, 0, :], rope[0, :, tok0:tok0 + 512], w=[("cs",)])
                P.dma("sp", cs[:, 1, :], rope[1, :, tok0:tok0 + 512], w=[("cs",)])
                xr = [("xT", t) for t in range(4)]

                def rope_proj(col, colsw, dst_bf, dst_tok, dst_f32=None):
                    bi = load_wblock(col)
                    bs = load_wblock(colsw)
                    for cq in range(4):
                        za, zb = zc[0] % 2, (zc[0] + 1) % 2
                        zc[0] += 2
                        for (bw, zz) in ((bi, za), (bs, zb)):
                            for c in range(8):
                                P.op("pe", I("matmul",
                                    pz[zz][:, :], wi[:, bw, c, cq * 128:(cq + 1) * 128], xT[:, c, :],
                                    start=(c == 0), stop=(c == 7)), r=xr + [("wi", bw)], w=[("pz", zz)])
                        tb = cq % 2
                        P.op("dve", I("tensor_tensor",
                            out=t1[:, tb, :], in0=pz[za][:, :], in1=cs[:, 0, :], op=ALU.mult),
                            r=[("pz", za), ("cs",)], w=[("t1", tb)])
                        P.op("dve", I("tensor_tensor",
                            out=t2[:, tb, :], in0=pz[zb][:, :], in1=cs[:, 1, :], op=ALU.mult),
                            r=[("pz", zb), ("cs",)], w=[("t2", tb)])
                        if dst_f32 is not None:
                            P.op("pool", I("tensor_tensor",
                                out=kf[:, tb, :], in0=t1[:, tb, :], in1=t2[:, tb, :], op=ALU.add),
                                r=[("t1", tb), ("t2", tb)], w=[("kf", tb)])
                            P.op("act", I("copy", out=dst_bf(cq), in_=kf[:, tb, :]),
                                 r=[("kf", tb)], w=[dst_tok])
                            P.dma("sp", dst_f32(cq), kf[:, tb, :], r=[("kf", tb)])
                        else:
                            P.op("pool", I("tensor_tensor",
                                out=dst_bf(cq), in0=t1[:, tb, :], in1=t2[:, tb, :], op=ALU.add),
                                r=[("t1", tb), ("t2", tb)], w=[dst_tok])

                if main:
                    rope_proj(W_KA, W_KS, lambda cq: kT[:, cq, tok0:tok0 + 512], ("kT",),
                              dst_f32=lambda cq: kT_o[cq * 128:(cq + 1) * 128, gm * 512:(gm + 1) * 512])
                    rope_proj(W_QA, W_QS, lambda cq: qT[:, cq, :], ("qT",))
                else:
                    rope_proj(W_KA, W_KS, lambda cq: kT[:, cq, tok0:tok0 + 512], ("kT",))

                bv = load_wblock(W_VA)
                slot = g % 2

                def vmm(lhs_fn, M):
                    zz = zc[0] % 2
                    zc[0] += 1
                    for c in range(8):
                        P.op("pe", I("matmul",
                            pz[zz][:M, :], lhs_fn(c), wi[:, bv, c, :], start=(c == 0), stop=(c == 7)),
                            r=xr + [("wi", bv)], w=[("pz", zz)])
                    return zz

                for blk in range(4):
                    zz = vmm(lambda c, blk=blk: xT[:, c, blk * 128:(blk + 1) * 128], 128)
                    if main:
                        fb = blk % 2
                        P.op("dve", I("tensor_copy", out=vf[:, fb, :], in_=pz[zz][:, :]),
                             r=[("pz", zz)], w=[("vf", fb)])
                        P.op("pool", I("tensor_copy",
                            out=V1[:, slot, blk, :, 0:64], in_=vf[:, fb, :].rearrange("p (h e) -> p h e", h=8)),
                            r=[("vf", fb), ("V1i",)], w=[("V",)])
                        P.dma("sp", v_o[gm * 512 + blk * 128:gm * 512 + (blk + 1) * 128, :], vf[:, fb, :],
                              r=[("vf", fb)])
                    else:
                        P.op("act", I("copy",
                            out=V1[:, slot, blk, :, 0:64], in_=pz[zz][:, :].rearrange("p (h e) -> p h e", h=8)),
                            r=[("pz", zz), ("V1i",)], w=[("V",)])
                if g >= 3:
                    for r4 in range(4):
                        zz = vmm(lambda c, r4=r4: xT[:, c, r4:512:4], 128)
                        P.op("act", I("copy",
                            out=V4[:, slot, r4, :, 0:64], in_=pz[zz][:, :].rearrange("p (h e) -> p h e", h=8)),
                            r=[("pz", zz), ("V4i",)], w=[("V",)])
                span = g // 4
                gq = g % 4
                for r16 in range(16):
                    zz = vmm(lambda c, r16=r16: xT[:, c, r16:512:16], 32)
                    P.op("act", I("copy",
                        out=V16[32 * gq:32 * gq + 32, span * 16 + r16, :, 0:64],
                        in_=pz[zz][0:32, :].rearrange("p (h e) -> p h e", h=8)),
                        r=[("pz", zz), ("V16i",)], w=[("V",)])
                import os as _os
                if not main or _os.environ.get("KNOATT"):
                    continue

                jobs = []
                m_cp = cstb[:, 0:256].rearrange("p (t k) -> p t k", t=2)
                m_cph = cstb[:, 0:384].rearrange("p (t k) -> p t k", t=3)[:, 0:3:2, :]

                def make_job(pairs, mask_ops, pv_list, tail):
                    st = {}

                    def S_stage():
                        sb_ = sc_cnt[0] % 2
                        sc_cnt[0] += 1
                        col = 0
                        for (k_ap, M, q_ap, N) in pairs:
                            P.op("pe", I("matmul", psc[sb_][:M, col:col + N], k_ap, q_ap, start=True, stop=True),
                                 r=[("kT",), ("qT",)], w=[("psc", sb_)])
                            col += N
                        pi = pt_cnt[0] % 4
                        pt_cnt[0] += 1
                        st["pi"] = pi
                        P.op("act", I("activation", out=pE[:, pi, :col], in_=psc[sb_][:, :col], func=AF.Exp, scale=0.125),
                             r=[("psc", sb_)], w=[("pE", pi)])
                        for (rows, c0, ncol, view, m_ap) in mask_ops:
                            P.op("pool", I("tensor_tensor", out=view(pT[:rows, pi, c0:c0 + ncol]),
                                           in0=view(pE[:rows, pi, c0:c0 + ncol]), in1=m_ap, op=ALU.mult),
                                 r=[("pE", pi), ("cstb",)], w=[("pT", pi)])

                    def PV_stage():
                        pi = st["pi"]
                        for (o_ap, v_ap, rows, c0, n, st_, sp_, potok) in pv_list:
                            P.op("pe", I("matmul", o_ap, v_ap, pT[:rows, pi, c0:c0 + n], start=st_, stop=sp_),
                                 r=[("pT", pi), ("V",)], w=[potok])
                        if tail is not None:
                            tail()
                    jobs.append((S_stage, PV_stage))

                for h in range(8):
                    c4, pb = h // 2, (h % 2) * 64
                    ob = h % 2
                    qh = qT[pb:pb + 64, c4, :]
                    kh = kT[pb:pb + 64, c4, :]
                    v22 = lambda a: a.rearrange("p (q t k) -> p q t k", q=2, t=2)
                    pob = po_cnt[0] % 2
                    po_cnt[0] += 1
                    for half in range(2):
                        pairs, pv = [], []
                        for qi_, qb in enumerate((2 * half, 2 * half + 1)):
                            tq = tok0 + 128 * qb
                            qa = qh[:, qb * 128:(qb + 1) * 128]
                            vprev = V1[:, 1 - slot, 3, h, :] if qb == 0 else V1[:, slot, qb - 1, h, :]
                            pairs.append((kh[:, tq:tq + 128], 128, qa, 128))
                            pairs.append((kh[:, tq - 128:tq], 128, qa, 128))
                            oq = po[pob][:65, qb * 128:(qb + 1) * 128]
                            pv.append((oq, V1[:, slot, qb, h, :], 128, 256 * qi_, 128, True, False, ("po", pob)))
                            pv.append((oq, vprev, 128, 256 * qi_ + 128, 128, False, True, ("po", pob)))
                        if gm == 0 and half == 0:
                            v12 = lambda a: a.rearrange("p (t k) -> p t k", t=2)
                            mops = [(128, 0, 256, v12, m_cph), (128, 256, 256, v12, m_cp)]
                        else:
                            mops = [(128, 0, 512, v22, m_cp.unsqueeze(1).broadcast_to([128, 2, 2, 128]))]
                        tail = None
                        if half == 1:
                            def tail(pob=pob, ob=ob):
                                P.op("act", I("copy", out=oacc[:, ob, :], in_=po[pob][:65, :]),
                                     r=[("po", pob)], w=[("oacc", ob)])
                        make_job(pairs, mops, pv, tail)
                    pob = po_cnt[0] % 2
                    po_cnt[0] += 1
                    for half in range(2):
                        pairs, pv = [], []
                        for qi_, r4 in enumerate((2 * half, 2 * half + 1)):
                            qa = qh[:, r4:512:4]
                            pairs.append((kh[:, tok0 + r4:tok0 + 512:4], 128, qa, 128))
                            pairs.append((kh[:, tok0 - 512 + r4:tok0:4], 128, qa, 128))
                            oq = po[pob][:65, r4 * 128:(r4 + 1) * 128]
                            pv.append((oq, V4[:, slot, r4, h, :], 128, 256 * qi_, 128, True, False, ("po", pob)))
                            pv.append((oq, V4[:, 1 - slot, r4, h, :], 128, 256 * qi_ + 128, 128, False, True, ("po", pob)))
                        mm = m_cph if gm == 0 else m_cp
                        mops = [(128, 0, 512, v22, mm.unsqueeze(1).broadcast_to([128, 2, 2, 128]))]
                        tail = None
                        if half == 1:
                            def tail(pob=pob, ob=ob):
                                P.op("dve", I("tensor_tensor",
                                    out=oacc[:, ob, :].rearrange("p (i r) -> p r i", r=4),
                                    in0=oacc[:, ob, :].rearrange("p (i r) -> p r i", r=4),
                                    in1=po[pob][:65, :].rearrange("p (r i) -> p r i", r=4), op=ALU.add),
                                    r=[("po", pob)], w=[("oacc", ob)])
                        make_job(pairs, mops, pv, tail)
                    pob = po_cnt[0] % 2
                    po_cnt[0] += 1
                    Mc = 32 * (gm + 1)
                    v8 = lambda a: a.rearrange("p (r i) -> p r i", r=8)
                    for quarter in range(2):
                        pairs, pv = [], []
                        for j in range(8):
                            r16 = 8 * quarter + j
                            pairs.append((kh[:, r16:L:16], 128, qh[:, r16:512:16], 32))
                        for j in range(8):
                            r16 = 8 * quarter + j
                            pairs.append((kh[:, L + r16:L + r16 + 16 * (Mc - 1) + 1:16], Mc, qh[:, r16:512:16], 32))
                        for j in range(8):
                            r16 = 8 * quarter + j
                            oq = po[pob][:65, r16 * 32:(r16 + 1) * 32]
                            pv.append((oq, V16[:, r16, h, :], 128, 32 * j, 32, True, False, ("po", pob)))
                            pv.append((oq, V16[:Mc, 16 + r16, h, :], Mc, 256 + 32 * j, 32, False, True, ("po", pob)))
                        mops = [(128, 0, 256, v8, mprevh[:, 32 * gm:32 * gm + 32].unsqueeze(1).broadcast_to([128, 8, 32])),
                                (Mc, 256, 256, v8, mcur[:Mc, 32 * gm:32 * gm + 32].unsqueeze(1).broadcast_to([Mc, 8, 32]))]
                        tail = None
                        if quarter == 1:
                            def tail(pob=pob, ob=ob, h=h):
                                P.op("dve", I("tensor_tensor",
                                    out=oacc[:, ob, :].rearrange("p (i r) -> p r i", r=16),
                                    in0=oacc[:, ob, :].rearrange("p (i r) -> p r i", r=16),
                                    in1=po[pob][:65, :].rearrange("p (r i) -> p r i", r=16), op=ALU.add),
                                    r=[("po", pob)], w=[("oacc", ob)])
                                P.op("dve", I("reciprocal", out=rden[64:65, ob, :], in_=oacc[64:65, ob, :]),
                                     r=[("oacc", ob)], w=[("rden", ob)])
                                zz = zc[0] % 2
                                zc[0] += 1
                                P.op("dve", I("tensor_copy", out=rhi[64:65, ob, :], in_=rden[64:65, ob, :]),
                                     r=[("rden", ob)], w=[("rhi", ob)])
                                P.op("dve", I("tensor_tensor", out=rlo[64:65, ob, :], in0=rden[64:65, ob, :],
                                              in1=rhi[64:65, ob, :], op=ALU.subtract),
                                     r=[("rden", ob), ("rhi", ob)], w=[("rlo", ob)])
                                P.op("pe", I("matmul", pz[zz][:64, :], onesh[64:65, 0:64], rhi[64:65, ob, :],
                                             start=True, stop=False), r=[("rhi", ob), ("onesf",)], w=[("pz", zz)])
                                P.op("pe", I("matmul", pz[zz][:64, :], onesh[64:65, 0:64], rlo[64:65, ob, :],
                                             start=False, stop=True), r=[("rlo", ob), ("onesf",)], w=[("pz", zz)])
                                P.op("dve", I("tensor_tensor", out=catA[:, h, :], in0=oacc[0:64, ob, :],
                                              in1=pz[zz][:64, :], op=ALU.mult),
                                     r=[("pz", zz), ("oacc", ob)], w=[("catA",)])
                        make_job(pairs, mops, pv, tail)
                prev_pv = None
                for (S_stage, PV_stage) in jobs:
                    S_stage()
                    if prev_pv is not None:
                        prev_pv()
                    prev_pv = PV_stage
                prev_pv()
                P.dma("sp", cats[:, :, gm * 512:(gm + 1) * 512], catA[:, :, :], r=[("catA",)])
            P.flush("B1")

        if "B2" not in PHASES:
            P.flush() if any(P.ops[e] for e in ENGS) else None
            return nc
        woAv = woutb[0:512, :].rearrange("(h p) n -> p h n", p=64)
        woHv = woutb[512:1024, :].rearrange("(k p) n -> p k n", p=128)

        def hgrn_common(stk, tagp, T):
            d = {}
            d["kk"] = sb(stk, "kk" + tagp, [128, 4, T], F32)
            d["ecum"] = sb(stk, "ecum" + tagp, [128, 4, T], F32)
            d["tmpa"] = sb(stk, "tmpa" + tagp, [128, 2, T], F32)
            d["tmpb"] = sb(stk, "tmpb" + tagp, [128, 2, T], F32)
            d["tmpc"] = sb(stk, "tmpc" + tagp, [128, 2, T], F32)
            d["kdec"] = sb(stk, "kdec" + tagp, [128, 4, T], BF16)
            d["kendT"] = sb(stk, "kendT" + tagp, [128, 4, T], BF16)
            d["qdec"] = sb(stk, "qdec" + tagp, [128, 4, T], BF16)
            d["gs"] = sb(stk, "gs" + tagp, [128, 4, T], BF16)
            d["osb"] = sb(stk, "osb" + tagp, [128, 2, T], F32)
            d["osq"] = sb(stk, "osq" + tagp, [128, 2, T], BF16)
            d["catH"] = sb(stk, "catH" + tagp, [128, 4, T], BF16)
            d["oml"] = sb(stk, "oml" + tagp, [128, 2, 4], F32)
            d["onesb"] = sb(stk, "onesb" + tagp, [128, 128], BF16)
            d["cstf"] = sb(stk, "cstf" + tagp, [128, NCST], F32)
            d["cstb"] = sb(stk, "cstbh" + tagp, [128, NCST], BF16)
            d["identb"] = sb(stk, "identb" + tagp, [128, 128], BF16)
            cstf, oml = d["cstf"], d["oml"]
            P.dma("sp", cstf[:, :], cst[:, :], w=[("cstf",)])
            P.dma("pool", d["cstb"][:, :], cst[:, :], w=[("cstbh",)])
            P.op("pool", I("memset", d["onesb"][:], 1.0), w=[("onesb",)])
            P.op("dve", I("tensor_tensor", out=oml[:, 0, :], in0=cstf[:, C_LB + 4:C_LB + 8],
                                                  in1=cstf[:, C_LB:C_LB + 4], op=ALU.subtract),
                 r=[("cstf",)], w=[("oml",)])
            P.op("act", I("activation", out=oml[:, 0, :], in_=oml[:, 0, :], func=AF.Sigmoid), w=[("oml",)])
            P.op("dve", I("tensor_scalar", out=oml[:, 1, :], in0=oml[:, 0, :],
                                                  scalar1=cstf[:, C_FLAG:C_FLAG + 1], scalar2=None, op0=ALU.mult),
                 r=[("cstf",)], w=[("oml",)])
            return d

        def hgrn_gates(d, T, xT, xr, wi, load_wblock, pz, zc, halo, main, rmask_ap, chunk):
            kk, ecum, tmpa, tmpb, tmpc = d["kk"], d["ecum"], d["tmpa"], d["tmpb"], d["tmpc"]
            kdec, kendT, qdec, gs, oml = d["kdec"], d["kendT"], d["qdec"], d["gs"], d["oml"]
            nch = T // chunk

            def proj(bw, hh):
                zz = zc[0] % 2
                zc[0] += 1
                for c in range(8):
                    P.op("pe", I("matmul",
                        pz[zz][:, :T], wi[:, bw, c, hh * 128:(hh + 1) * 128], xT[:, c, :T],
                        start=(c == 0), stop=(c == 7)), r=xr + [("wi", bw)], w=[("pz", zz)])
                return zz

            bw = load_wblock(W_FH)
            for hh in range(4):
                zz = proj(bw, hh)
                tb = hh % 2
                P.op("act", I("activation", out=tmpa[:, tb, :], in_=pz[zz][:, :T],
                                                                  func=AF.Sigmoid, scale=-1.0),
                     r=[("pz", zz)], w=[("tmpa", tb)])
                P.op("dve", I("tensor_scalar",
                    out=kk[:, hh, :], in0=tmpa[:, tb, :], scalar1=oml[:, 1 if halo else 0, hh:hh + 1],
                    scalar2=None, op0=ALU.mult), r=[("tmpa", tb), ("oml",)], w=[("kk", hh)])
                P.op("act", I("activation", out=tmpb[:, tb, :], in_=kk[:, hh, :], func=AF.Ln,
                                                                  scale=-1.0, bias=oneb[:, :]),
                     r=[("kk", hh)], w=[("tmpb", tb)])
                P.op("dve", I("tensor_tensor_scan",
                    out=tmpc[:, tb, :], data0=rmask_ap, data1=tmpb[:, tb, :], initial=0.0,
                    op0=ALU.mult, op1=ALU.add), r=[("tmpb", tb), ("cstf",)], w=[("tmpc", tb)])
                P.op("act", I("activation", out=ecum[:, hh, :], in_=tmpc[:, tb, :], func=AF.Exp),
                     r=[("tmpc", tb)], w=[("ecum", hh)])
                P.op("act", I("activation", out=tmpa[:, tb, :], in_=tmpc[:, tb, :], func=AF.Exp, scale=-1.0),
                     r=[("tmpc", tb)], w=[("tmpa", tb)])
                P.op("dve", I("tensor_tensor",
                    out=kdec[:, hh, :], in0=kk[:, hh, :], in1=tmpa[:, tb, :], op=ALU.mult),
                    r=[("kk", hh), ("tmpa", tb)], w=[("kdec", hh)])
                P.op("dve", I("tensor_tensor",
                    out=kendT[:, hh, :].rearrange("p (n c) -> p n c", c=chunk),
                    in0=kdec[:, hh, :].rearrange("p (n c) -> p n c", c=chunk),
                    in1=ecum[:, hh, chunk - 1:T:chunk].unsqueeze(2).broadcast_to([128, nch, chunk]), op=ALU.mult),
                    r=[("kdec", hh), ("ecum", hh)], w=[("kendT", hh)])
            if main:
                bw = load_wblock(W_QH)
                for hh in range(4):
                    zz = proj(bw, hh)
                    tb = hh % 2
                    P.op("act", I("activation", out=tmpb[:, tb, :], in_=pz[zz][:, :T], func=AF.Silu),
                         r=[("pz", zz)], w=[("tmpb", tb)])
                    P.op("dve", I("tensor_tensor",
                        out=qdec[:, hh, :], in0=tmpb[:, tb, :], in1=ecum[:, hh, :], op=ALU.mult),
                        r=[("tmpb", tb), ("ecum", hh)], w=[("qdec", hh)])
                bw = load_wblock(W_GH)
                for hh in range(4):
                    zz = proj(bw, hh)
                    P.op("act", I("activation", out=gs[:, hh, :], in_=pz[zz][:, :T], func=AF.Silu),
                         r=[("pz", zz)], w=[("gs", hh)])

        def hgrn_out(d, T, hh, o_psum, o_tok, pz, zc):
            osb, osq, catH, gs, cstf, onesb, tmpa = d["osb"], d["osq"], d["catH"], d["gs"], d["cstf"], d["onesb"], d["tmpa"]
            ob = hh % 2
            P.op("dve", I("tensor_copy", out=osb[:, ob, :], in_=o_psum), r=[o_tok], w=[("osb", ob)])
            P.op("pool", I("tensor_tensor", out=osq[:, ob, :], in0=osb[:, ob, :], in1=osb[:, ob, :], op=ALU.mult),
                 r=[("osb", ob)], w=[("osq", ob)])
            zz = zc[0] % 2
            zc[0] += 1
            P.op("pe", I("matmul", pz[zz][:, :T], onesb[:, :], osq[:, ob, :], start=True, stop=True),
                 r=[("osq", ob), ("onesb",)], w=[("pz", zz)])
            P.op("act", I("activation", out=tmpa[:, ob, :], in_=pz[zz][:, :T], func=AF.Sqrt,
                                               scale=1.0 / 128.0, bias=epsb[:, :]),
                 r=[("pz", zz)], w=[("tmpa", ob)])
            P.op("dve", I("reciprocal", out=tmpa[:, ob, :], in_=tmpa[:, ob, :]), w=[("tmpa", ob)])
            P.op("dve", I("tensor_tensor", out=osb[:, ob, :], in0=osb[:, ob, :], in1=tmpa[:, ob, :], op=ALU.mult),
                 r=[("tmpa", ob)], w=[("osb", ob)])
            P.op("dve", I("scalar_tensor_tensor",
                out=catH[:, hh, :], in0=osb[:, ob, :], scalar=cstf[:, C_GN + hh:C_GN + hh + 1], in1=gs[:, hh, :],
                op0=ALU.mult, op1=ALU.mult), r=[("osb", ob), ("gs", hh), ("cstf",)], w=[("catH",)])

        oneb = sb(gstack, "oneb", [128, 1], F32)
        P.op("pool", I("memset", oneb[:], 1.0), w=[("oneb",)])

        with ExitStack() as stk:
            xt = sb(stk, "xtC", [128, 2, D], F32)
            xT = sb(stk, "xTC", [128, 8, 512], BF16)
            wi = sb(stk, "wiC", [128, 2, 8, 512], BF16)
            d = hgrn_common(stk, "C", 512)
            kend_tm = sb(stk, "kendtm", [128, 4, 512], BF16)
            vh_tm = sb(stk, "vhtm", [128, 4, 512], BF16)
            S = sb(stk, "Sst", [128, 4, 128], F32)
            Sbf = sb(stk, "Sbf", [128, 4, 8, 128], BF16)
            ATm = sb(stk, "ATm", [128, 4, 4, 64], BF16)
            catAs = sb(stk, "catAs", [64, 8, 512], BF16)
            woA = sb(stk, "woA", [64, 8, D], BF16)
            woH = sb(stk, "woH", [128, 4, D], BF16)
            yb = sb(stk, "ybC", [128, 4, D], F32)
            gb = sb(stk, "gbC", [128, 2, D], F32)
            stats = sb(stk, "statsC", [128, 4, 2, 6], F32)
            mv = sb(stk, "mvC", [128, 4, 2], F32)
            rstd = sb(stk, "rstdC", [128, 4], F32)
            identf = make_ident(stk, "C")
            identb = d["identb"]
            tp = [ps(stk, "tpC%d" % i, [128, 512], F32) for i in range(2)]
            tpb = ps(stk, "tpbC", [128, 1024], BF16)
            pz = [ps(stk, "pzC%d" % i, [128, 512], F32) for i in range(2)]
            pub = ps(stk, "pubC", [128, 512], F32)
            pa = ps(stk, "paC", [128, 512], F32)
            pob = ps(stk, "pobC", [128, 512], F32)
            P.op("act", I("copy", out=identb[:], in_=identf[:]), r=[("identf",)], w=[("identb",)])
            P.op("pool", I("memset", S[:], 0.0), w=[("S",)])
            P.dma("sp", woA[:, :, :], woAv, w=[("woA",)])
            P.dma("sp", woH[:, :, :], woHv, w=[("woH",)])
            load_gb(gb, 2)
            cstf, cstb = d["cstf"], d["cstb"]
            wcnt = [0]

            def load_wblock(col0):
                bi = wcnt[0] % 2
                wcnt[0] += 1
                P.dma("sp", wi[:, bi, :, :], winv[:, :, col0:col0 + 512], w=[("wi", bi)])
                return bi

            zc = [0]
            for g in range(8):
                main = g >= 4
                gm = g - 4
                tok0 = 512 * g
                load_T(x1s, tok0, 512, xt, tp, xT, identf, "C")
                xr = [("xT", t) for t in range(4)]
                hgrn_gates(d, 512, xT, xr, wi, load_wblock, pz, zc, not main, main,
                           cstf[:, C_RMASK:C_RMASK + 512], 64)
                kendT, kdec, qdec, ecum = d["kendT"], d["kdec"], d["qdec"], d["ecum"]
                for t in range(4):
                    for hh in range(4):
                        P.op("pe", I("transpose",
                            tpb[:, hh * 128:(hh + 1) * 128], kendT[:, hh, t * 128:(t + 1) * 128], identb[:, :]),
                            r=[("kendT", hh), ("identb",)], w=[("tpb",)])
                    P.op("act", I("copy", out=kend_tm[:, t, :], in_=tpb[:, 0:512]), r=[("tpb",)], w=[("kendtm",)])
                bw = load_wblock(W_IH)
                for t in range(4):
                    zz = zc[0] % 2
                    zc[0] += 1
                    for c in range(8):
                        P.op("pe", I("matmul",
                            pz[zz][:, :], xT[:, c, t * 128:(t + 1) * 128], wi[:, bw, c, :],
                            start=(c == 0), stop=(c == 7)), r=xr + [("wi", bw)], w=[("pz", zz)])
                    P.op("act", I("copy", out=vh_tm[:, t, :], in_=pz[zz][:, :]),
                         r=[("pz", zz)], w=[("vhtm",)])
                for ch in range(8):
                    t, hf = ch // 2, ch % 2
                    pr = slice(64 * hf, 64 * hf + 64)
                    for hh in range(4):
                        P.op("pe", I("matmul",
                            pub[:, hh * 128:(hh + 1) * 128], kend_tm[pr, t, hh * 128:(hh + 1) * 128],
                            vh_tm[pr, t, hh * 128:(hh + 1) * 128], start=True, stop=True),
                            r=[("kendtm",), ("vhtm",)], w=[("pub",)])
                    if main:
                        P.op("pool", I("tensor_copy", out=Sbf[:, :, ch, :], in_=S[:, :, :]),
                             r=[("S",)], w=[("Sbf",)])
                    for hh in range(4):
                        P.op("dve", I("scalar_tensor_tensor",
                            out=S[:, hh, :], in0=S[:, hh, :], scalar=ecum[:, hh, ch * 64 + 63:ch * 64 + 64],
                            in1=pub[:, hh * 128:(hh + 1) * 128], op0=ALU.mult, op1=ALU.add),
                            r=[("pub",), ("ecum", hh)], w=[("S",)])
                if not main:
                    continue
                for t in range(4):
                    for hf in range(2):
                        ch = 2 * t + hf
                        pr = slice(64 * hf, 64 * hf + 64)
                        for hh in range(4):
                            P.op("pe", I("matmul",
                                pa[pr, hh * 64:(hh + 1) * 64], kdec[:, hh, ch * 64:(ch + 1) * 64],
                                qdec[:, hh, ch * 64:(ch + 1) * 64], start=True, stop=True),
                                r=[("kdec", hh), ("qdec", hh)], w=[("pa",)])
                    P.op("dve", I("tensor_tensor",
                        out=ATm[:, t, :, :], in0=pa[:, 0:256].rearrange("p (h i) -> p h i", h=4),
                        in1=cstb[:, C_M64:C_M64 + 64].unsqueeze(1).broadcast_to([128, 4, 64]), op=ALU.mult),
                        r=[("pa",), ("cstbh",)], w=[("ATm",)])
                for hh in range(4):
                    for ch in range(8):
                        t, hf = ch // 2, ch % 2
                        pr = slice(64 * hf, 64 * hf + 64)
                        P.op("pe", I("matmul",
                            pob[:, ch * 64:(ch + 1) * 64], Sbf[:, hh, ch, :], qdec[:, hh, ch * 64:(ch + 1) * 64],
                            start=True, stop=False), r=[("Sbf",), ("qdec", hh)], w=[("pob",)])
                        P.op("pe", I("matmul",
                            pob[:, ch * 64:(ch + 1) * 64], vh_tm[pr, t, hh * 128:(hh + 1) * 128], ATm[pr, t, hh, :],
                            start=False, stop=True), r=[("vhtm",), ("ATm",)], w=[("pob",)])
                    hgrn_out(d, 512, hh, pob[:, :], ("pob",), pz, zc)
                P.dma("sp", catAs[:, :, :], cats[:, :, gm * 512:(gm + 1) * 512], w=[("catAs",)])
                catH = d["catH"]

                def wout_mm(t, hf, pdt, pdtok, tt):
                    for h in range(8):
                        P.op("pe", I("matmul",
                            pdt[:tt, :], catAs[:, h, t * 128:t * 128 + tt], woA[:, h, hf * 512:(hf + 1) * 512],
                            start=(h == 0), stop=False), r=[("catAs",), ("woA",)], w=[pdtok])
                    for hh in range(4):
                        P.op("pe", I("matmul",
                            pdt[:tt, :], catH[:, hh, t * 128:t * 128 + tt], woH[:, hh, hf * 512:(hf + 1) * 512],
                            start=False, stop=(hh == 3)), r=[("catH",), ("woH",)], w=[pdtok])

                ln_epilogue(512, 4, x1s, tok0, [(x2s, gm * 512)], None, [(pz[0], ("pz", 0)), (pz[1], ("pz", 1))],
                            yb, gb, stats, mv, rstd, wout_mm)
            P.dma("sp", s_o[:, :, :], S[:, :, :], r=[("S",)])
            P.flush("B2")
        if "B3" not in PHASES:
            return nc
        with ExitStack() as stk:
            T = NS
            xt = sb(stk, "xtS", [128, 2, D], F32)
            xT = sb(stk, "xTS", [128, 8, T], BF16)
            wi = sb(stk, "wiS", [128, 2, 8, 512], BF16)
            d = hgrn_common(stk, "S", T)
            cs = sb(stk, "csS", [128, 2, T], F32)
            t1 = sb(stk, "t1S", [128, 2, T], F32)
            t2 = sb(stk, "t2S", [128, 2, T], F32)
            kfs = sb(stk, "kfsS", [128, 4, T], F32)
            kTs = sb(stk, "kTsS", [128, 4, T], BF16)
            qTs = sb(stk, "qTsS", [128, 4, T], BF16)
            knew = sb(stk, "knewS", [T, 512], F32)
            vnew = sb(stk, "vnewS", [T, 512], F32)
            vnb = sb(stk, "vnbS", [4, 2, 8, 65], BF16)
            kc = sb(stk, "kcS", [128, 2, 7, 512], F32)
            vc = sb(stk, "vcS", [128, 1, 7, 512], F32)
            kcT = sb(stk, "kcTS", [128, 4, 7 * 128], BF16)
            vcb = sb(stk, "vcbS", [128, 7, 8, 65], BF16)
            pexp = sb(stk, "pexpS", [128, 7, 8, 4], BF16)
            pTs = sb(stk, "pTsS", [128, 7, 8, 4], BF16)
            pexn = sb(stk, "pexnS", [4, 8, 4], BF16)
            pTn = sb(stk, "pTnS", [4, 8, 4], BF16)
            oS = sb(stk, "oSS", [65, 16, 32], F32)
            rdn = sb(stk, "rdnS", [65, 512], F32)
            onesf = sb(stk, "onesfS", [128, 64], F32)
            oneshS = sb(stk, "oneshS", [128, 64], BF16)
            rhiS = sb(stk, "rhiS", [65, 512], BF16)
            rloS = sb(stk, "rloS", [65, 512], BF16)
            catAs = sb(stk, "catAsS", [64, 8, T], BF16)
            iT = sb(stk, "iTS", [128, 4, T], BF16)
            S0f = sb(stk, "S0fS", [128, 64, 128], F32)
            S0b = sb(stk, "S0bS", [128, 2, 4, 128], BF16)
            ksvs = sb(stk, "ksvsS", [4, 2, 8, 128], BF16)
            ATs = sb(stk, "ATsS", [4, 64, 4], BF16)
            woA = sb(stk, "woAS", [64, 8, D], BF16)
            woH = sb(stk, "woHS", [128, 4, D], BF16)
            yb = sb(stk, "ybS", [128, 1, D], F32)
            gb = sb(stk, "gbS", [128, 2, D], F32)
            stats = sb(stk, "statsS", [128, 1, 2, 6], F32)
            mv = sb(stk, "mvS", [128, 1, 2], F32)
            rstd = sb(stk, "rstdS", [128, 1], F32)
            identf = make_ident(stk, "S")
            identb = d["identb"]
            cstf, cstb = d["cstf"], d["cstb"]
            tp = [ps(stk, "tpS%d" % i, [128, 512], F32) for i in range(2)]
            tpb = ps(stk, "tpbS", [128, 1024], BF16)
            pz = [ps(stk, "pzS%d" % i, [128, 512], F32) for i in range(2)]
            psc = ps(stk, "pscS", [128, 512], F32)
            pos = ps(stk, "posS", [128, 512], F32)
            pub = ps(stk, "pubS", [128, 512], F32)
            P.op("act", I("copy", out=identb[:], in_=identf[:]), r=[("identf",)], w=[("identb",)])
            P.op("pool", I("memset", onesf[:], 1.0), w=[("onesf",)])
            P.op("pool", I("memset", oneshS[:], 1.0), w=[("onesf",)])
            P.op("pool", I("memset", vcb[:, :, :, 64:65], 1.0), w=[("vcbi",)])
            P.op("pool", I("memset", vnb[:, :, :, 64:65], 1.0), w=[("vnbi",)])
            P.dma("sp", woA[:, :, :], woAv, w=[("woA",)])
            P.dma("sp", woH[:, :, :], woHv, w=[("woH",)])
            load_gb(gb, 2)
            stv = st.rearrange("(bh k) v -> k bh v", k=128)
            for i in range(4):
                P.dma("sp", S0f[:, 16 * i:16 * i + 16, :], stv[:, 16 * i:16 * i + 16, :], w=[("S0f", i)])
            P.dma("sp", cs[:, 0, :], rope[0, :, 2 * L:2 * L + T], w=[("cs",)])
            P.dma("sp", cs[:, 1, :], rope[1, :, 2 * L:2 * L + T], w=[("cs",)])
            wcnt = [0]

            def load_wblock(col0):
                bi = wcnt[0] % 2
                wcnt[0] += 1
                P.dma("sp", wi[:, bi, :, :], winv[:, :, col0:col0 + 512], w=[("wi", bi)])
                return bi

            zc = [0]
            load_T(x1s, 2 * L, T, xt, tp, xT, identf, "S")
            xr = [("xT", 0)]

            def rope_proj_s(col, colsw, dst_bf, dst_tok, f32dst):
                bi = load_wblock(col)
                bs = load_wblock(colsw)
                for cq in range(4):
                    za, zb = zc[0] % 2, (zc[0] + 1) % 2
                    zc[0] += 2
                    for (bw, zz) in ((bi, za), (bs, zb)):
                        for c in range(8):
                            P.op("pe", I("matmul",
                                pz[zz][:, :T], wi[:, bw, c, cq * 128:(cq + 1) * 128], xT[:, c, :],
                                start=(c == 0), stop=(c == 7)), r=xr + [("wi", bw)], w=[("pz", zz)])
                    tb = cq % 2
                    P.op("dve", I("tensor_tensor",
                        out=t1[:, tb, :], in0=pz[za][:, :T], in1=cs[:, 0, :], op=ALU.mult),
                        r=[("pz", za), ("cs",)], w=[("t1", tb)])
                    P.op("dve", I("tensor_tensor",
                        out=t2[:, tb, :], in0=pz[zb][:, :T], in1=cs[:, 1, :], op=ALU.mult),
                        r=[("pz", zb), ("cs",)], w=[("t2", tb)])
                    if f32dst is not None:
                        P.op("pool", I("tensor_tensor",
                            out=f32dst[:, cq, :], in0=t1[:, tb, :], in1=t2[:, tb, :], op=ALU.add),
                            r=[("t1", tb), ("t2", tb)], w=[("kfs",)])
                        P.op("act", I("copy", out=dst_bf[:, cq, :], in_=f32dst[:, cq, :]),
                             r=[("kfs",)], w=[dst_tok])
                    else:
                        P.op("pool", I("tensor_tensor",
                            out=dst_bf[:, cq, :], in0=t1[:, tb, :], in1=t2[:, tb, :], op=ALU.add),
                            r=[("t1", tb), ("t2", tb)], w=[dst_tok])

            rope_proj_s(W_KA, W_KS, kTs, ("kTs",), kfs)
            rope_proj_s(W_QA, W_QS, qTs, ("qTs",), None)
            for cq in range(4):
                P.op("pe", I("transpose", tp[0][:T, cq * 128:(cq + 1) * 128], kfs[:, cq, :], identf[:, :]),
                     r=[("kfs",), ("identf",)], w=[("tp", 0)])
            P.op("act", I("copy", out=knew[:, :], in_=tp[0][:T, :]), r=[("tp", 0)], w=[("knew",)])
            bv = load_wblock(W_VA)
            zz = zc[0] % 2
            zc[0] += 1
            for c in range(8):
                P.op("pe", I("matmul", pz[zz][:T, :], xT[:, c, :], wi[:, bv, c, :],
                                                          start=(c == 0), stop=(c == 7)),
                     r=xr + [("wi", bv)], w=[("pz", zz)])
            P.op("act", I("copy", out=vnew[:, :], in_=pz[zz][:T, :]), r=[("pz", zz)], w=[("vnew",)])
            for b in range(NSQ):
                P.dma("sp", nk_o[b, 2044:2048, :], knew[4 * b:4 * b + 4, :], r=[("knew",)])
                P.dma("sp", nv_o[b, 2044:2048, :], vnew[4 * b:4 * b + 4, :], r=[("vnew",)], w=[("nvrow", b)])

            mult = cstb[:, C_MULT:C_MULT + 28].rearrange("p (t s) -> p t s", t=7)
            for b in range(NSQ):
                bf = b % 2
                P.dma("sp", kc[:, bf, 0:4, :], ck[b, 1536:2048, :].rearrange("(t p) n -> p t n", p=128), w=[("kc", bf)])
                P.dma("pool", vnb[:, bf, :, 0:64], nv_o[b, 2044:2048, :].rearrange("s (h e) -> s h e", h=8),
                      r=[("nvrow", b), ("vnbi",)], w=[("vnb", bf)])
                P.dma("sp", vc[:, 0, 0:4, :], cv[b, 1536:2048, :].rearrange("(t p) n -> p t n", p=128), w=[("vc", 0)])
                for t in range(3):
                    for r in range(4):
                        P.dma("sp", kc[32 * r:32 * r + 32, bf, 4 + t, :], ck[b, t * 512 + r:(t + 1) * 512:16, :],
                              w=[("kc", bf)])
                        P.dma("sp", vc[32 * r:32 * r + 32, 0, 4 + t, :], cv[b, t * 512 + r:(t + 1) * 512:16, :],
                              w=[("vc", 0)])
                for c4 in range(4):
                    for tg in range(2):
                        tiles = list(range(4 * tg, min(7, 4 * tg + 4)))
                        bk = (2 * c4 + tg) % 2
                        for j, tl in enumerate(tiles):
                            P.op("pe", I("transpose",
                                tp[bk][:, j * 128:(j + 1) * 128], kc[:, bf, tl, c4 * 128:(c4 + 1) * 128], identf[:, :]),
                                r=[("kc", bf), ("identf",)], w=[("tp", bk)])
                        n = len(tiles) * 128
                        P.op("act", I("copy",
                            out=kcT[:, c4, tg * 512:tg * 512 + n], in_=tp[bk][:, :n]),
                            r=[("tp", bk)], w=[("kcT",)])
                for tl in range(7):
                    P.op("pool", I("tensor_copy",
                        out=vcb[:, tl, :, 0:64], in_=vc[:, 0, tl, :].rearrange("p (h e) -> p h e", h=8)),
                        r=[("vc", 0), ("vcbi",)], w=[("vcb",)])
                for par, bank, btok in ((0, psc, ("psc",)), (1, pub, ("pub",))):
                    pb = par * 64
                    for tl in range(7):
                        for h2 in range(4):
                            c4 = h2
                            P.op("pe", I("matmul",
                                bank[:, (tl * 4 + h2) * 4:(tl * 4 + h2) * 4 + 4], kcT[pb:pb + 64, c4, tl * 128:(tl + 1) * 128],
                                qTs[pb:pb + 64, c4, 4 * b:4 * b + 4], start=True, stop=True),
                                r=[("kcT",), ("qTs",)], w=[btok])
                    for h2 in range(4):
                        c4 = h2
                        P.op("pe", I("matmul",
                            bank[0:4, 112 + h2 * 4:112 + h2 * 4 + 4], kTs[pb:pb + 64, c4, 4 * b:4 * b + 4],
                            qTs[pb:pb + 64, c4, 4 * b:4 * b + 4], start=True, stop=True),
                            r=[("kTs",), ("qTs",)], w=[btok])
                    P.op("act", I("activation", out=pexp[:, :, par:8:2, :],
                                  in_=bank[:, 0:112].rearrange("p (t h s) -> p t h s", t=7, h=4),
                                  func=AF.Exp, scale=0.125), r=[btok], w=[("pexp",)])
                    P.op("act", I("activation", out=pexn[:, par:8:2, :],
                                  in_=bank[0:4, 112:128].rearrange("p (h s) -> p h s", h=4),
                                  func=AF.Exp, scale=0.125), r=[btok], w=[("pexn",)])
                P.op("dve", I("tensor_tensor",
                    out=pTs[:, :, :, :], in0=pexp[:, :, :, :],
                    in1=mult.unsqueeze(2).broadcast_to([128, 7, 8, 4]), op=ALU.mult),
                    r=[("pexp",), ("cstbh",)], w=[("pTs",)])
                P.op("dve", I("tensor_tensor",
                    out=pTn[:, :, :], in0=pexn[:, :, :],
                    in1=cstb[0:4, C_MNEW:C_MNEW + 4].unsqueeze(1).broadcast_to([4, 8, 4]), op=ALU.mult),
                    r=[("pexn",), ("cstbh",)], w=[("pTn",)])
                for h in range(8):
                    for tl in range(7):
                        P.op("pe", I("matmul",
                            pos[0:65, h * 4:h * 4 + 4], vcb[:, tl, h, :], pTs[:, tl, h, :],
                            start=(tl == 0), stop=False), r=[("vcb",), ("pTs",)], w=[("pos",)])
                    P.op("pe", I("matmul",
                        pos[0:65, h * 4:h * 4 + 4], vnb[0:4, bf, h, :], pTn[0:4, h, :], start=False, stop=True),
                        r=[("vnb", bf), ("pTn",)], w=[("pos",)])
                P.op("act", I("copy", out=oS[:, b, :], in_=pos[0:65, 0:32]), r=[("pos",)], w=[("oS",)])
            oSf = oS[:, :, :].rearrange("p b x -> p (b x)")
            P.op("dve", I("reciprocal", out=rdn[64:65, :], in_=oSf[64:65, :]), r=[("oS",)], w=[("rdn",)])
            P.op("dve", I("tensor_copy", out=rhiS[64:65, :], in_=rdn[64:65, :]), r=[("rdn",)], w=[("rhiS",)])
            P.op("dve", I("tensor_tensor", out=rloS[64:65, :], in0=rdn[64:65, :], in1=rhiS[64:65, :], op=ALU.subtract),
                 r=[("rdn",), ("rhiS",)], w=[("rloS",)])
            P.op("pe", I("matmul", pz[0][:64, :], oneshS[64:65, 0:64], rhiS[64:65, :], start=True, stop=False),
                 r=[("rhiS",), ("onesf",)], w=[("pz", 0)])
            P.op("pe", I("matmul", pz[0][:64, :], oneshS[64:65, 0:64], rloS[64:65, :], start=False, stop=True),
                 r=[("rloS",), ("onesf",)], w=[("pz", 0)])
            P.op("dve", I("tensor_tensor",
                out=catAs[:, :, :].rearrange("p h (b s) -> p b h s", s=4),
                in0=oS[0:64, :, :].rearrange("p b (h s) -> p b h s", s=4),
                in1=pz[0][:64, :].rearrange("p (b h s) -> p b h s", h=8, s=4), op=ALU.mult),
                r=[("pz", 0), ("oS",)], w=[("catAs",)])

            hgrn_gates(d, T, xT, xr, wi, load_wblock, pz, zc, False, True, cstf[:, C_RMASKS:C_RMASKS + T], 4)
            kendT, kdec, qdec, ecum = d["kendT"], d["kdec"], d["qdec"], d["ecum"]
            bw = load_wblock(W_IH)
            for hh in range(4):
                zz = zc[0] % 2
                zc[0] += 1
                for c in range(8):
                    P.op("pe", I("matmul",
                        pz[zz][:, :T], wi[:, bw, c, hh * 128:(hh + 1) * 128], xT[:, c, :],
                        start=(c == 0), stop=(c == 7)), r=xr + [("wi", bw)], w=[("pz", zz)])
                P.op("act", I("copy", out=iT[:, hh, :], in_=pz[zz][:, :T]),
                     r=[("pz", zz)], w=[("iT",)])
            for b in range(NSQ):
                for hh in range(4):
                    P.op("pe", I("matmul",
                        psc[0:4, (b * 4 + hh) * 4:(b * 4 + hh) * 4 + 4], kdec[:, hh, 4 * b:4 * b + 4],
                        qdec[:, hh, 4 * b:4 * b + 4], start=True, stop=True),
                        r=[("kdec", hh), ("qdec", hh)], w=[("psc",)])
            P.op("dve", I("tensor_tensor",
                out=ATs[:, :, :], in0=psc[0:4, 0:256].rearrange("p (x i) -> p x i", i=4),
                in1=cstb[0:4, C_MS4:C_MS4 + 4].unsqueeze(1).broadcast_to([4, 64, 4]), op=ALU.mult),
                r=[("psc",), ("cstbh",)], w=[("ATs",)])
            for b in range(NSQ):
                bf = b % 2
                for hh in range(4):
                    P.op("pe", I("transpose",
                        tpb[0:4, hh * 128:(hh + 1) * 128], kendT[:, hh, 4 * b:4 * b + 4], identb[:, :]),
                        r=[("kendT", hh), ("identb",)], w=[("tpb",)])
                for hh in range(4):
                    P.op("pe", I("transpose",
                        tpb[0:4, (4 + hh) * 128:(5 + hh) * 128], iT[:, hh, 4 * b:4 * b + 4], identb[:, :]),
                        r=[("iT",), ("identb",)], w=[("tpb",)])
                P.op("act", I("copy", out=ksvs[:, bf, :, :].rearrange("p x n -> p (x n)"), in_=tpb[0:4, :]),
                     r=[("tpb",)], w=[("ksvs", bf)])
                P.op("pool", I("tensor_copy", out=S0b[:, bf, :, :], in_=S0f[:, 4 * b:4 * b + 4, :]),
                     r=[("S0f", b // 4)], w=[("S0b", bf)])
                for hh in range(4):
                    bh = 4 * b + hh
                    P.op("pe", I("matmul",
                        pos[:, bh * 4:bh * 4 + 4], S0b[:, bf, hh, :], qdec[:, hh, 4 * b:4 * b + 4],
                        start=True, stop=False), r=[("S0b", bf), ("qdec", hh)], w=[("pos",)])
                    P.op("pe", I("matmul",
                        pos[:, bh * 4:bh * 4 + 4], ksvs[0:4, bf, 4 + hh, :], ATs[0:4, bh, :],
                        start=False, stop=True), r=[("ksvs", bf), ("ATs",)], w=[("pos",)])
                for hh in range(4):
                    P.op("pe", I("matmul",
                        pub[:, hh * 128:(hh + 1) * 128], ksvs[0:4, bf, hh, :], ksvs[0:4, bf, 4 + hh, :],
                        start=True, stop=True), r=[("ksvs", bf)], w=[("pub",)])
                for hh in range(4):
                    bh = 4 * b + hh
                    P.op("dve", I("scalar_tensor_tensor",
                        out=S0f[:, bh, :], in0=S0f[:, bh, :], scalar=ecum[:, hh, 4 * b + 3:4 * b + 4],
                        in1=pub[:, hh * 128:(hh + 1) * 128], op0=ALU.mult, op1=ALU.add),
                        r=[("pub",), ("ecum", hh), ("S0b", bf)], w=[("S0f", b // 4)])
            for i in range(4):
                P.dma("sp", ns_o[:, 16 * i:16 * i + 16, :], S0f[:, 16 * i:16 * i + 16, :], r=[("S0f", i)])
            for hh in range(4):
                hgrn_out(d, T, hh, pos[:, 0:256].rearrange("p (b h s) -> p b h s", h=4, s=4)[:, :, hh, :],
                         ("pos",), pz, zc)
            catH = d["catH"]

            def wout_mm(t, hf, pdt, pdtok, tt):
                for h in range(8):
                    P.op("pe", I("matmul",
                        pdt[:tt, :], catAs[:, h, :], woA[:, h, hf * 512:(hf + 1) * 512],
                        start=(h == 0), stop=False), r=[("catAs",), ("woA",)], w=[pdtok])
                for hh in range(4):
                    P.op("pe", I("matmul",
                        pdt[:tt, :], catH[:, hh, :], woH[:, hh, hf * 512:(hf + 1) * 512],
                        start=False, stop=(hh == 3)), r=[("catH",), ("woH",)], w=[pdtok])

            ln_epilogue(T, 1, x1s, 2 * L, [(x2s, L)], None, [(pz[0], ("pz", 0)), (pz[1], ("pz", 1))],
                        yb, gb, stats, mv, rstd, wout_mm)
            P.flush("B3")

        if "C" not in PHASES:
            return nc
        groupsC = [(512 * g, 512) for g in range(4)] + [(L, NS)]
        ffn_phase("C2", x2s, groupsC, lambda gi: [(y_o, groupsC[gi][0])], w2g, w2u, w2d, 4)
    return nc


def _rope_tables(positions):
    half = 32
    inv = (np.float32(10000.0) ** (-np.arange(half, dtype=np.float32) / np.float32(half))).astype(np.float32)
    ang = positions.astype(np.float32)[None, :] * inv[:, None]
    cos = np.cos(ang).astype(np.float32)
    sin = np.sin(ang).astype(np.float32)
    cosT = np.tile(cos, (4, 1))
    sinS = np.concatenate([-sin, sin, -sin, sin], axis=0)
    return cosT, sinS


def _consts(half_flag, hg_lower_bound, hg_norm_g):
    c = np.zeros((128, NCST), np.float32)
    k = np.arange(128)[:, None]
    q = np.arange(128)[None, :]
    c[:, C_MCUR:C_MCUR + 128] = (k <= q)
    c[:, C_MPREV:C_MPREV + 128] = (k >= q)
    c[:, C_MPREVH:C_MPREVH + 128] = (k >= q) * float(half_flag)
    c[:, C_M64:C_M64 + 64] = ((np.arange(128)[:, None] % 64) <= np.arange(64)[None, :])
    c[:, C_RMASK:C_RMASK + 512] = (np.arange(512) % 64 != 0)[None, :]
    c[:, C_RMASKS:C_RMASKS + 64] = (np.arange(64) % 4 != 0)[None, :]
    c[:, C_FLAG] = float(half_flag)
    lb = hg_lower_bound.reshape(2, 4, 128)
    c[:, C_LB:C_LB + 8] = lb.transpose(2, 0, 1).reshape(128, 8)
    c[:, C_GN:C_GN + 4] = hg_norm_g.reshape(4, 128).T
    p = np.arange(128)
    mult = np.zeros((128, 7, 4), np.float32)
    for t in range(7):
        if t < 4:
            row = 1536 + 128 * t + p
        else:
            row = 16 * ((t - 4) * 32 + p % 32) + p // 32
        for s_ in range(4):
            mult[:, t, s_] = ((row >= 1920 + s_).astype(np.float32) + ((row >= 1536) & (row % 4 == s_))
                              + (row % 16 == s_))
    c[:, C_MULT:C_MULT + 28] = mult.reshape(128, 28)
    sp = np.arange(4)[:, None]
    sq = np.arange(4)[None, :]
    c[0:4, C_MNEW:C_MNEW + 4] = (sp < sq) * 1.0 + (sp == sq) * 3.0
    c[0:4, C_MS4:C_MS4 + 4] = (sp <= sq)
    return c


_PERM = None


def _swap_perm():
    idx = np.arange(512).reshape(8, 2, 32)
    return idx[:, ::-1, :].reshape(512)


def kernel(x_prompt, x_sample, cache_k, cache_v, state_hgrn,
           ffn1_w_gate, ffn1_w_up, ffn1_w_down, ln1_g, ln1_b,
           w_in, hg_lower_bound, hg_norm_g, w_out, ln2_g, ln2_b,
           ffn2_w_gate, ffn2_w_up, ffn2_w_down, ln3_g, ln3_b):
    f = lambda a: np.ascontiguousarray(np.asarray(a, dtype=np.float32))
    x_prompt, x_sample = f(x_prompt), f(x_sample)
    cache_k, cache_v, state_hgrn = np.asarray(cache_k), np.asarray(cache_v), np.asarray(state_hgrn)
    win0 = f(w_in)[0]
    perm = _swap_perm()
    win_ext = np.ascontiguousarray(np.concatenate(
        [win0, win0[:, 0:512][:, perm], win0[:, 512:1024][:, perm]], axis=1))
    lnv = np.ascontiguousarray(np.stack([f(ln1_g)[0], f(ln1_b)[0], f(ln2_g)[0], f(ln2_b)[0], f(ln3_g)[0], f(ln3_b)[0]]))
    shared = {
        "w1g": f(ffn1_w_gate)[0], "w1u": f(ffn1_w_up)[0], "w1d": f(ffn1_w_down)[0],
        "w2g": f(ffn2_w_gate)[0], "w2u": f(ffn2_w_up)[0], "w2d": f(ffn2_w_down)[0],
        "win": win_ext, "wout": f(w_out)[0], "lnv": lnv,
    }
    hlb, hng = f(hg_lower_bound), f(hg_norm_g)[0]
    in_maps = []
    for c in range(NCORES):
        b, half = c // 2, c % 2
        xs = x_sample[16 * c:16 * c + 16].reshape(NS, D)
        if half == 0:
            halo = np.zeros((L, D), np.float32)
            pos_h = np.zeros(L, np.int64)
        else:
            halo = x_prompt[b, 0:L]
            pos_h = np.arange(0, L)
        xin = np.concatenate([halo, x_prompt[b, half * L:(half + 1) * L], xs], axis=0)
        pos = np.concatenate([pos_h, np.arange(half * L, (half + 1) * L), 8192 + (np.arange(NS) % 4)])
        cosT, sinS = _rope_tables(pos)
        m = {}
        for kk, vv in shared.items():
            if kk == "lnv":
                m[kk] = vv
            else:
                m[kk] = np.concatenate([vv, np.full((1, vv.shape[1]), float(c), np.float32)], axis=0)
        m["xin"] = np.ascontiguousarray(xin)
        m["ck"] = np.ascontiguousarray(cache_k[0, 16 * c:16 * c + 16].reshape(16, 2048, 512), dtype=np.float32)
        m["cv"] = np.ascontiguousarray(cache_v[0, 16 * c:16 * c + 16].reshape(16, 2048, 512), dtype=np.float32)
        m["st"] = np.ascontiguousarray(state_hgrn[0, 16 * c:16 * c + 16].reshape(64 * 128, 128), dtype=np.float32)
        m["cst"] = _consts(half, hlb, hng)
        m["rope"] = np.ascontiguousarray(np.stack([cosT, sinS]))
        in_maps.append(m)
    import os
    if os.environ.get("KDBG"):
        ncr = int(os.environ.get("KCORES", "8"))
        nc = build_program(tuple(os.environ["KDBG"].split(",")))
        ims = in_maps[:ncr]
        for m in ims:
            m["ck"], m["cv"] = m["ck"][:NSQ], m["cv"][:NSQ]
        res = run_bass_kernel_spmd(nc, ims, core_ids=list(range(ncr)), trace=bool(os.environ.get("KTRACE")))
        return res, in_maps
    nc = build_program()
    res = run_bass_kernel_spmd(nc, in_maps, core_ids=list(range(NCORES)))
    R = res.results
    y_prompt = np.empty((4, 4096, D), np.float32)
    y_sample = np.empty((128, 4, D), np.float32)
    nkp = np.empty((1, 4, 2048, 8, 64), np.float32)
    nvp = np.empty((1, 4, 2048, 8, 64), np.float32)
    nsp = np.empty((1, 4, 4, 128, 128), np.float32)
    nks = np.empty((1, 128, 2048, 8, 64), np.float32)
    nvs = np.empty((1, 128, 2048, 8, 64), np.float32)
    nss = np.empty((1, 128, 4, 128, 128), np.float32)
    for c in range(NCORES):
        b, half = c // 2, c % 2
        r = R[c]
        y_prompt[b, half * L:(half + 1) * L] = r["y"][0:L]
        y_sample[16 * c:16 * c + 16] = r["y"][L:L + NS].reshape(16, 4, D)
        if half == 1:
            nkp[0, b] = r["kTo"].T.reshape(2048, 8, 64)
            nvp[0, b] = r["vo"].reshape(2048, 8, 64)
            nsp[0, b] = r["so"].transpose(1, 0, 2)
        nks[0, 16 * c:16 * c + 16] = r["nk"].reshape(16, 2048, 8, 64)
        nvs[0, 16 * c:16 * c + 16] = r["nv"].reshape(16, 2048, 8, 64)
        nss[0, 16 * c:16 * c + 16] = r["ns"].reshape(128, 16, 4, 128).transpose(1, 2, 0, 3)
    return (y_prompt, y_sample, nkp, nvp, nsp, nks, nvs, nss)
```

```python
import numpy as np
from contextlib import ExitStack
import concourse.bass as bass
import concourse.mybir as mybir
from concourse.bass_utils import run_bass_kernel_spmd

F32 = mybir.dt.float32
BF16 = mybir.dt.bfloat16
AF = mybir.ActivationFunctionType
ALU = mybir.AluOpType

NCORES = 8
D = 1024
FF = 2816
NK = FF // 128
L = 2048
NS = 64
NTOK = 2 * L + NS
NMAIN = L + NS
WINC = 4608
ALPHA = 2.0 ** 0.25
EPS = 1e-5
NCST = 1080
import os
NSQ = int(os.environ.get("KNSQ", "16"))
KLVL = int(os.environ.get("KLVL", "99"))
C_MCUR, C_MPREV, C_MPREVH, C_M64, C_RMASK, C_RMASKS, C_FLAG = 0, 128, 256, 384, 448, 960, 1024
C_LB, C_GN, C_MULT, C_MNEW, C_MS4 = 1025, 1033, 1037, 1065, 1069
W_QA, W_KA, W_VA, W_QH, W_FH, W_IH, W_GH, W_QS, W_KS = 0, 512, 1024, 1536, 2048, 2560, 3072, 3584, 4096

ENGS = ("pe", "act", "dve", "pool", "sp")
PSUM_TOK = {"tp", "pg", "pu", "pd", "pz", "psc", "po", "pub", "pa", "pob", "tpb", "pos"}


def I(name, *a, **k):
    return lambda e: getattr(e, name)(*a, **k)


class Op:
    __slots__ = ("eng", "fn", "deps", "dma", "sig", "sem", "val")

    def __init__(self, eng, fn, dma):
        self.eng, self.fn, self.dma = eng, fn, dma
        self.deps, self.sig, self.sem, self.val = [], dma, None, 0


class Prog:
    def __init__(self, nc, stack):
        self.nc = nc
        self.esem = {e: stack.enter_context(nc.semaphore("es_" + e)) for e in ENGS}
        self.ecount = {e: 0 for e in ENGS}
        self.dsems = {q: [stack.enter_context(nc.semaphore("ds_%s%d" % (q, i))) for i in range(n)]
                      for q, n in (("sp", 24), ("pool", 12), ("act", 6))}
        self.dval = {}
        self.drr = {q: 0 for q in self.dsems}
        self.reset()

    def reset(self):
        self.ops = {e: [] for e in ENGS}
        self.last_w, self.readers = {}, {}
        self.gen_deps = {}
        self.dlast = {}

    def op(self, eng, fn, r=(), w=(), dma=False):
        o = Op(eng, fn, dma)
        deps = []
        for t in r:
            if t in self.last_w:
                deps.extend(self.last_w[t])
            if t[0] in PSUM_TOK:
                rd = self.readers.get(t)
                if rd:
                    deps.extend(x for en, x in rd[0].items() if en != eng)
        append_w = set()
        for t in w:
            rd = self.readers.get(t)
            has_rd = bool(rd and (rd[0] or rd[1]))
            lw = self.last_w.get(t)
            if dma and lw and not has_rd and all(x.dma for x in lw):
                append_w.add(t)
                deps.extend(self.gen_deps.get(t, ()))
                continue
            gd = []
            if lw:
                gd.extend(lw)
            if rd:
                gd.extend(rd[0].values())
                gd.extend(rd[1])
            self.gen_deps[t] = gd
            deps.extend(gd)
        for t in r:
            rd = self.readers.setdefault(t, ({}, []))
            if dma:
                rd[1].append(o)
            else:
                rd[0][eng] = o
        for t in w:
            if t in append_w:
                self.last_w[t].append(o)
            else:
                self.last_w[t] = [o]
                self.readers[t] = ({}, [])
        if dma:
            ring = self.dsems[eng]
            s = ring[self.drr[eng] % len(ring)]
            self.drr[eng] += 1
            if s in self.dlast:
                deps.append(self.dlast[s])
            self.dlast[s] = o
            o.sem = s
            self.dval[s] = self.dval.get(s, 0) + 16
            o.val = self.dval[s]
        seen = set()
        for d in deps:
            if d is o or id(d) in seen:
                continue
            seen.add(id(d))
            if d.eng == "pe" and eng == "pe" and not d.dma:
                continue
            d.sig = True
            o.deps.append(d)
        self.ops[eng].append(o)
        return o

    def dma(self, q, out, in_, r=(), w=()):
        return self.op(q, I("dma_start", out=out, in_=in_), r=r, w=w, dma=True)

    def flush(self, name="ph"):
        nc = self.nc
        finals = []
        for e in ENGS:
            lst = self.ops[e]
            comp = [o for o in lst if not o.dma]
            if comp:
                comp[-1].sig = True
            finals.extend(o for o in lst if o.dma)
        for e in ENGS:
            for o in self.ops[e]:
                if not o.dma and o.sig:
                    self.ecount[e] += 1
                    o.sem, o.val = self.esem[e], self.ecount[e]
        last_comp = {}
        for e in ENGS:
            comp = [o for o in self.ops[e] if not o.dma]
            if comp:
                last_comp[e] = comp[-1]
        dma_final = {}
        for o in finals:
            dma_final[o.sem] = max(dma_final.get(o.sem, 0), o.val)

        def emit(eng_name):
            def body(eng):
                seen = {}
                for o in self.ops[eng_name]:
                    for d in o.deps:
                        if seen.get(d.sem, 0) < d.val:
                            eng.wait_ge(d.sem, d.val)
                            seen[d.sem] = d.val
                    ins = o.fn(eng)
                    if o.sig:
                        ins.then_inc(o.sem, 16 if o.dma else 1)
                for e2, lo in last_comp.items():
                    if seen.get(lo.sem, 0) < lo.val:
                        eng.wait_ge(lo.sem, lo.val)
                for s, v in dma_final.items():
                    if seen.get(s, 0) < v:
                        eng.wait_ge(s, v)
            return body

        with nc.named_scope(name), nc.Block() as block:
            block.tensor(emit("pe"))
            block.scalar(emit("act"))
            block.vector(emit("dve"))
            block.gpsimd(emit("pool"))
            block.sync(emit("sp"))
        self.reset()


def ffn_pieces():
    return [(0, 6), (6, 12), (12, 17), (17, 22)]


def piece_of(k):
    for i, (a, b) in enumerate(ffn_pieces()):
        if a <= k < b:
            return i
    raise ValueError


def build_program(PHASES=("A", "B1", "B2", "B3", "C")):
    nc = bass.Bass("TRN2", target_bir_lowering=False)

    def din(name, shape):
        return nc.dram_tensor(name, shape, F32, kind="ExternalInput").ap()

    def dout(name, shape):
        return nc.dram_tensor(name, shape, F32, kind="ExternalOutput").ap()

    xin = din("xin", [NTOK, D])
    ck = din("ck", [NSQ, 2048, 512])
    cv = din("cv", [NSQ, 2048, 512])
    st = din("st", [64 * 128, 128])
    w1g, w1u, w1d = din("w1g", [D + 1, FF])[0:D, :], din("w1u", [D + 1, FF])[0:D, :], din("w1d", [FF + 1, D])[0:FF, :]
    w2g, w2u, w2d = din("w2g", [D + 1, FF])[0:D, :], din("w2u", [D + 1, FF])[0:D, :], din("w2d", [FF + 1, D])[0:FF, :]
    win = din("win", [D + 1, WINC])[0:D, :]
    wout = din("wout", [D + 1, D])[0:D, :]
    lnv = din("lnv", [6, D])
    cst = din("cst", [128, NCST])
    rope = din("rope", [2, 128, NTOK])

    y_o = dout("y", [NMAIN, D])
    kT_o = dout("kTo", [512, L])
    v_o = dout("vo", [L, 512])
    s_o = dout("so", [128, 4, 128])
    nk_o = dout("nk", [NSQ, 2048, 512])
    nv_o = dout("nv", [NSQ, 2048, 512])
    ns_o = dout("ns", [128, 64, 128])

    import os
    if os.environ.get("KDBG"):
        x1s = dout("x1s", [NTOK, D])
        x2s = dout("x2s", [NMAIN, D])
    else:
        x1s = nc.dram_tensor("x1s", [NTOK, D], F32).ap()
        x2s = nc.dram_tensor("x2s", [NMAIN, D], F32).ap()
    winb = nc.dram_tensor("winb", [D, WINC], BF16).ap()
    woutb = nc.dram_tensor("woutb", [D, D], BF16).ap()
    cats = nc.dram_tensor("cats", [64, 8, NMAIN], BF16).ap()
    vscr = nc.dram_tensor("vscr", [2 * L, 520], BF16).ap()

    gstack = ExitStack()
    with gstack:
        P = Prog(nc, gstack)

        def sb(stack, name, shape, dt=F32):
            return stack.enter_context(nc.sbuf_tensor(name, shape, dt))

        def ps(stack, name, shape, dt=F32):
            return stack.enter_context(nc.psum_tensor(name, shape, dt))

        def make_ident(stk, tagp):
            identf = sb(stk, "identf" + tagp, [128, 128], F32)
            P.op("pool", I("memset", identf[:], 0.0), w=[("identf",)])
            P.op("pool", I("affine_select", out=identf[:], in_=identf[:], pattern=[[-1, 128]],
                                                   compare_op=ALU.not_equal, fill=1.0, base=0,
                                                   channel_multiplier=1), w=[("identf",)])
            return identf

        def load_T(src, row0, T, xt, tp, xT, identf, tagp):
            nt = (T + 127) // 128
            for t in range(nt):
                tt = min(128, T - 128 * t)
                r0 = row0 + 128 * t
                bi = load_T.cnt % 2
                load_T.cnt += 1
                P.dma("sp", xt[:tt, bi, :], src[r0:r0 + tt, :], w=[("xt", bi)])
                for cg in range(2):
                    bk = load_T.tcnt % 2
                    load_T.tcnt += 1
                    for j in range(4):
                        c = 4 * cg + j
                        P.op("pe", I("transpose",
                            tp[bk][:, j * 128:j * 128 + tt], xt[:tt, bi, c * 128:(c + 1) * 128], identf[:tt, :tt]),
                            r=[("xt", bi), ("identf",)], w=[("tp", bk)])
                    P.op("act", I("copy",
                        out=xT[:, 4 * cg:4 * cg + 4, t * 128:t * 128 + tt],
                        in_=tp[bk][:, :].rearrange("p (j n) -> p j n", j=4)[:, :, :tt]),
                        r=[("tp", bk)], w=[("xT", t)])
        load_T.cnt = 0
        load_T.tcnt = 0

        def ln_epilogue(T, nt, src, row0, dst, drow0, pd_groups, yb, gb, stats, mv, rstd, emit_mm):
            for t in range(nt):
                tt = min(128, T - 128 * t)
                P.dma("sp", yb[:tt, t, :], src[row0 + 128 * t:row0 + 128 * t + tt, :], w=[("yb", t)])
                for hf in range(2):
                    bk = ln_epilogue.cnt % 2
                    ln_epilogue.cnt += 1
                    pdt, pdtok = pd_groups[bk]
                    emit_mm(t, hf, pdt, pdtok, tt)
                    P.op("dve", I("scalar_tensor_tensor",
                        out=yb[:tt, t, hf * 512:(hf + 1) * 512], in0=yb[:tt, t, hf * 512:(hf + 1) * 512],
                        scalar=ALPHA, in1=pdt[:tt, :], op0=ALU.mult, op1=ALU.add),
                        r=[pdtok], w=[("yb", t)])
                    P.op("dve", I("bn_stats",
                        out=stats[:tt, t, hf, :], in_=yb[:tt, t, hf * 512:(hf + 1) * 512]),
                        r=[("yb", t)], w=[("stats", t)])
                P.op("dve", I("bn_aggr",
                    out=mv[:tt, t, :], in_=stats[:tt, t, :, :].rearrange("p a b -> p (a b)")),
                    r=[("stats", t)], w=[("mv",)])
            pp = min(128, T)
            P.op("act", I("activation", out=rstd[:pp, 0:nt], in_=mv[:pp, 0:nt, 1], func=AF.Sqrt,
                                               bias=epsb[:pp, :], scale=1.0),
                 r=[("mv",)], w=[("rstd",)])
            P.op("dve", I("reciprocal", out=rstd[:pp, 0:nt], in_=rstd[:pp, 0:nt]), w=[("rstd",)])
            for t in range(nt):
                tt = min(128, T - 128 * t)
                P.op("dve", I("tensor_scalar",
                    out=yb[:tt, t, :], in0=yb[:tt, t, :], scalar1=mv[:tt, t, 0:1], scalar2=rstd[:tt, t:t + 1],
                    op0=ALU.subtract, op1=ALU.mult), r=[("mv",), ("rstd",)], w=[("yb", t)])
                P.op("pool", I("tensor_tensor",
                    out=yb[:tt, t, :], in0=yb[:tt, t, :], in1=gb[:tt, 0, :], op=ALU.mult),
                    r=[("gb",)], w=[("yb", t)])
                P.op("pool", I("tensor_tensor",
                    out=yb[:tt, t, :], in0=yb[:tt, t, :], in1=gb[:tt, 1, :], op=ALU.add),
                    r=[("gb",)], w=[("yb", t)])
                for (dd, dr) in (dst if isinstance(dst, list) else [(dst, drow0)]):
                    P.dma("sp", dd[dr + 128 * t:dr + 128 * t + tt, :], yb[:tt, t, :], r=[("yb", t)])
        ln_epilogue.cnt = 0

        def load_gb(gb, grow):
            P.dma("sp", gb[:, 0, :], lnv[grow:grow + 1, :].partition_broadcast(128), w=[("gb",)])
            P.dma("sp", gb[:, 1, :], lnv[grow + 1:grow + 2, :].partition_broadcast(128), w=[("gb",)])

        epsb = sb(gstack, "epsb", [128, 1], F32)
        P.op("pool", I("memset", epsb[:], EPS), w=[("epsb",)])

        def ffn_phase(tagp, src, groups, dsts, wgd, wud, wdd, grow, extra=None):
            with ExitStack() as stk:
                wg = sb(stk, "wg" + tagp, [128, 8, FF], BF16)
                wu = sb(stk, "wu" + tagp, [128, 8, FF], BF16)
                wd = sb(stk, "wd" + tagp, [128, NK, D], BF16)
                xt = sb(stk, "xt" + tagp, [128, 2, D], F32)
                xT = sb(stk, "xT" + tagp, [128, 8, 512], BF16)
                hT = sb(stk, "hT" + tagp, [128, NK, 512], BF16)
                sg = sb(stk, "sg" + tagp, [128, 2, 512], F32)
                yb = sb(stk, "yb" + tagp, [128, 4, D], F32)
                gb = sb(stk, "gb" + tagp, [128, 2, D], F32)
                stats = sb(stk, "stats" + tagp, [128, 4, 2, 6], F32)
                mv = sb(stk, "mv" + tagp, [128, 4, 2], F32)
                rstd = sb(stk, "rstd" + tagp, [128, 4], F32)
                tp = [ps(stk, "tp%d%s" % (i, tagp), [128, 512], F32) for i in range(2)]
                pg = [ps(stk, "pg%d%s" % (i, tagp), [128, 512], F32) for i in range(2)]
                pu = [ps(stk, "pu%d%s" % (i, tagp), [128, 512], F32) for i in range(2)]
                pd = [ps(stk, "pd%d%s" % (i, tagp), [128, 512], F32) for i in range(2)]
                identf = make_ident(stk, tagp)
                wgv = wgd.rearrange("(c p) n -> p c n", p=128)
                wuv = wud.rearrange("(c p) n -> p c n", p=128)
                wdv = wdd.rearrange("(k p) n -> p k n", p=128)
                for i, (a, b) in enumerate(ffn_pieces()):
                    P.dma("pool", wg[:, :, a * 128:b * 128], wgv[:, :, a * 128:b * 128], w=[("wg", i)])
                    P.dma("pool", wu[:, :, a * 128:b * 128], wuv[:, :, a * 128:b * 128], w=[("wu", i)])
                for i, (a, b) in enumerate(ffn_pieces()):
                    P.dma("pool", wd[:, a:b, :], wdv[:, a:b, :], w=[("wd", i)])
                load_gb(gb, grow)

                def gate_up(T):
                    nt = (T + 127) // 128
                    xr = [("xT", t) for t in range(nt)]
                    for k in range(NK):
                        bk = k % 2
                        pc = piece_of(k)
                        for c in range(8):
                            P.op("pe", I("matmul",
                                pg[bk][:, :T], wg[:, c, k * 128:(k + 1) * 128], xT[:, c, :T],
                                start=(c == 0), stop=(c == 7)), r=xr + [("wg", pc)], w=[("pg", bk)])
                        for c in range(8):
                            P.op("pe", I("matmul",
                                pu[bk][:, :T], wu[:, c, k * 128:(k + 1) * 128], xT[:, c, :T],
                                start=(c == 0), stop=(c == 7)), r=xr + [("wu", pc)], w=[("pu", bk)])
                        P.op("act", I("activation", out=sg[:, bk, :T], in_=pg[bk][:, :T], func=AF.Silu),
                             r=[("pg", bk)], w=[("sg", bk)])
                        P.op("dve", I("scalar_tensor_tensor",
                            out=hT[:, k, :T], in0=sg[:, bk, :T], scalar=0.5, in1=pu[bk][:, :T],
                            op0=ALU.mult, op1=ALU.mult), r=[("sg", bk), ("pu", bk)], w=[("hT", k)])

                def down_mm(t, hf, pdt, pdtok, tt):
                    for k in range(NK):
                        P.op("pe", I("matmul",
                            pdt[:tt, :], hT[:, k, t * 128:t * 128 + tt], wd[:, k, hf * 512:(hf + 1) * 512],
                            start=(k == 0), stop=(k == NK - 1)),
                            r=[("hT", k), ("wd", piece_of(k))], w=[pdtok])

                load_T(src, groups[0][0], groups[0][1], xt, tp, xT, identf, tagp)
                for gi, (row0, T) in enumerate(groups):
                    nt = (T + 127) // 128
                    gate_up(T)
                    if gi + 1 < len(groups):
                        load_T(src, groups[gi + 1][0], groups[gi + 1][1], xt, tp, xT, identf, tagp)
                    if extra is not None:
                        extra(gi)
                    ln_epilogue(T, nt, src, row0, dsts(gi), None, [(pd[0], ("pd", 0)), (pd[1], ("pd", 1))], yb, gb, stats, mv, rstd, down_mm)
                P.flush("ffn" + tagp)

        groupsA = [(512 * g, 512) for g in range(8)] + [(2 * L, NS)]

        def cast_weights(gi):
            if gi == 0:
                for i in range(4):
                    P.dma("pool", winb[i * 256:(i + 1) * 256, :], win[i * 256:(i + 1) * 256, :])
                P.dma("pool", woutb[:, :], wout[:, :])
            if gi < 8:
                for b in (2 * gi, 2 * gi + 1):
                    if b >= NSQ:
                        continue
                    P.dma("act", nk_o[b, 0:2044, :], ck[b, 4:2048, :])
                    P.dma("act", nv_o[b, 0:2044, :], cv[b, 4:2048, :])

        ffn_phase("A", xin, groupsA, lambda gi: [(x1s, groupsA[gi][0])], w1g, w1u, w1d, 0, extra=cast_weights)

        winv = winb.rearrange("(c p) n -> p c n", p=128)
        if "B1" not in PHASES:
            return nc

        with ExitStack() as stk:
            xt = sb(stk, "xtB", [128, 2, D], F32)
            xT = sb(stk, "xTB", [128, 8, 512], BF16)
            wi = sb(stk, "wiB", [128, 2, 8, 512], BF16)
            kT = sb(stk, "kTB", [128, 4, 2 * L], BF16)
            qT = sb(stk, "qTB", [128, 4, 512], BF16)
            V16 = sb(stk, "V16", [128, 32, 8, 65], BF16)
            V4 = sb(stk, "V4", [128, 2, 4, 8, 65], BF16)
            V1 = sb(stk, "V1", [128, 2, 4, 8, 65], BF16)
            cs = sb(stk, "csB", [128, 2, 512], F32)
            t1 = sb(stk, "t1B", [128, 2, 512], F32)
            t2 = sb(stk, "t2B", [128, 2, 512], F32)
            kf = sb(stk, "kfB", [128, 2, 512], F32)
            vf = sb(stk, "vfB", [128, 2, 512], F32)
            pT = sb(stk, "pTB", [128, 4, 512], BF16)
            pE = sb(stk, "pEB", [128, 4, 512], BF16)
            oacc = sb(stk, "oaccB", [65, 2, 512], F32)
            rden = sb(stk, "rdenB", [65, 2, 512], F32)
            catA = sb(stk, "catAB", [64, 8, 512], BF16)
            cstb = sb(stk, "cstbB", [128, 448], BF16)
            onesf = sb(stk, "onesfB", [128, 64], F32)
            onesh = sb(stk, "oneshB", [128, 64], BF16)
            rhi = sb(stk, "rhiB", [65, 2, 512], BF16)
            rlo = sb(stk, "rloB", [65, 2, 512], BF16)
            identf = make_ident(stk, "B")
            tp = [ps(stk, "tpB%d" % i, [128, 512], F32) for i in range(2)]
            pz = [ps(stk, "pzB%d" % i, [128, 512], F32) for i in range(2)]
            psc = [ps(stk, "pscB%d" % i, [128, 512], F32) for i in range(2)]
            po = [ps(stk, "poB%d" % i, [128, 512], F32) for i in range(2)]

            P.dma("pool", cstb[:, :], cst[:, 0:448], w=[("cstb",)])
            P.op("pool", I("memset", onesf[:], 1.0), w=[("onesf",)])
            P.op("pool", I("memset", onesh[:], 1.0), w=[("onesf",)])
            P.op("pool", I("memset", V16[:, :, :, 64:65], 1.0), w=[("V16i",)])
            P.op("pool", I("memset", V4[:, :, :, :, 64:65], 1.0), w=[("V4i",)])
            P.op("pool", I("memset", V1[:, :, :, :, 64:65], 1.0), w=[("V1i",)])
            mcur = cstb[:, C_MCUR:C_MCUR + 128]
            mprev = cstb[:, C_MPREV:C_MPREV + 128]
            mprevh = cstb[:, C_MPREVH:C_MPREVH + 128]

            wcnt = [0]

            def load_wblock(col0):
                bi = wcnt[0] % 2
                wcnt[0] += 1
                P.dma("sp", wi[:, bi, :, :], winv[:, :, col0:col0 + 512], w=[("wi", bi)])
                return bi

            zc = [0]
            sc_cnt = [0]
            po_cnt = [0]
            pt_cnt = [0]

            for g in range(8):
                main = g >= 4
                gm = g - 4
                tok0 = 512 * g
                load_T(x1s, tok0, 512, xt, tp, xT, identf, "B")
                P.dma("sp", cs[:, 0, :], rope[0, :, tok0:tok0 + 512], w=[("cs",)])
                P.dma("sp", cs[:, 1, :], rope[1, :, tok0:tok0 + 512], w=[("cs",)])
                xr = [("xT", t) for t in range(4)]

                def rope_proj(col, colsw, dst_bf, dst_tok, dst_f32=None):
                    bi = load_wblock(col)
                    bs = load_wblock(colsw)
                    for cq in range(4):
                        za, zb = zc[0] % 2, (zc[0] + 1) % 2
                        zc[0] += 2
                        for (bw, zz) in ((bi, za), (bs, zb)):
                            for c in range(8):
                                P.op("pe", I("matmul",
                                    pz[zz][:, :], wi[:, bw, c, cq * 128:(cq + 1) * 128], xT[:, c, :],
                                    start=(c == 0), stop=(c == 7)), r=xr + [("wi", bw)], w=[("pz", zz)])
                        tb = cq % 2
                        P.op("dve", I("tensor_tensor",
                            out=t1[:, tb, :], in0=pz[za][:, :], in1=cs[:, 0, :], op=ALU.mult),
                            r=[("pz", za), ("cs",)], w=[("t1", tb)])
                        P.op("dve", I("tensor_tensor",
                            out=t2[:, tb, :], in0=pz[zb][:, :], in1=cs[:, 1, :], op=ALU.mult),
                            r=[("pz", zb), ("cs",)], w=[("t2", tb)])
                        if dst_f32 is not None:
                            P.op("pool", I("tensor_tensor",
                                out=kf[:, tb, :], in0=t1[:, tb, :], in1=t2[:, tb, :], op=ALU.add),
                                r=[("t1", tb), ("t2", tb)], w=[("kf", tb)])
                            P.op("act", I("copy", out=dst_bf(cq), in_=kf[:, tb, :]),
                                 r=[("kf", tb)], w=[dst_tok])
                            P.dma("sp", dst_f32(cq), kf[:, tb, :], r=[("kf", tb)])
                        else:
                            P.op("pool", I("tensor_tensor",
                                out=dst_bf(cq), in0=t1[:, tb, :], in1=t2[:, tb, :], op=ALU.add),
                                r=[("t1", tb), ("t2", tb)], w=[dst_tok])

                if main:
                    rope_proj(W_KA, W_KS, lambda cq: kT[:, cq, tok0:tok0 + 512], ("kT",),
                              dst_f32=lambda cq: kT_o[cq * 128:(cq + 1) * 128, gm * 512:(gm + 1) * 512])
                    rope_proj(W_QA, W_QS, lambda cq: qT[:, cq, :], ("qT",))
                else:
                    rope_proj(W_KA, W_KS, lambda cq: kT[:, cq, tok0:tok0 + 512], ("kT",))

                bv = load_wblock(W_VA)
                slot = g % 2

                def vmm(lhs_fn, M):
                    zz = zc[0] % 2
                    zc[0] += 1
                    for c in range(8):
                        P.op("pe", I("matmul",
                            pz[zz][:M, :], lhs_fn(c), wi[:, bv, c, :], start=(c == 0), stop=(c == 7)),
                            r=xr + [("wi", bv)], w=[("pz", zz)])
                    return zz

                for blk in range(4):
                    zz = vmm(lambda c, blk=blk: xT[:, c, blk * 128:(blk + 1) * 128], 128)
                    if main:
                        fb = blk % 2
                        P.op("dve", I("tensor_copy", out=vf[:, fb, :], in_=pz[zz][:, :]),
                             r=[("pz", zz)], w=[("vf", fb)])
                        P.op("pool", I("tensor_copy",
                            out=V1[:, slot, blk, :, 0:64], in_=vf[:, fb, :].rearrange("p (h e) -> p h e", h=8)),
                            r=[("vf", fb), ("V1i",)], w=[("V",)])
                        P.dma("sp", v_o[gm * 512 + blk * 128:gm * 512 + (blk + 1) * 128, :], vf[:, fb, :],
                              r=[("vf", fb)])
                    else:
                        P.op("act", I("copy",
                            out=V1[:, slot, blk, :, 0:64], in_=pz[zz][:, :].rearrange("p (h e) -> p h e", h=8)),
                            r=[("pz", zz), ("V1i",)], w=[("V",)])
                span = g // 4
                gq = g % 4
                P.dma("sp", vscr[tok0:tok0 + 512, :].rearrange("(b p) n -> p b n", p=128),
                      V1[:, slot, :, :, :].rearrange("p b h e -> p b (h e)"), r=[("V",)], w=[("vscr",)])
                if g >= 3:
                    P.dma("sp", V4[:, slot, :, :, :].rearrange("p r h e -> p r (h e)"),
                          vscr[tok0:tok0 + 512, :].rearrange("(i r) n -> i r n", r=4), r=[("vscr",)], w=[("V",)])
                P.dma("sp", V16[32 * gq:32 * gq + 32, span * 16:(span + 1) * 16, :, :].rearrange("p r h e -> p r (h e)"),
                      vscr[tok0:tok0 + 512, :].rearrange("(i r) n -> i r n", r=16), r=[("vscr",)], w=[("V",)])
                import os as _os
                if not main or _os.environ.get("KNOATT"):
                    continue

                jobs = []
                m_cp = cstb[:, 0:256].rearrange("p (t k) -> p t k", t=2)
                m_cph = cstb[:, 0:384].rearrange("p (t k) -> p t k", t=3)[:, 0:3:2, :]

                def make_job(pairs, mask_ops, pv_list, tail):
                    st = {}

                    def S_stage():
                        sb_ = sc_cnt[0] % 2
                        sc_cnt[0] += 1
                        col = 0
                        for (k_ap, M, q_ap, N) in pairs:
                            P.op("pe", I("matmul", psc[sb_][:M, col:col + N], k_ap, q_ap, start=True, stop=True),
                                 r=[("kT",), ("qT",)], w=[("psc", sb_)])
                            col += N
                        pi = pt_cnt[0] % 4
                        pt_cnt[0] += 1
                        st["pi"] = pi
                        P.op("act", I("activation", out=pE[:, pi, :col], in_=psc[sb_][:, :col], func=AF.Exp, scale=0.125),
                             r=[("psc", sb_)], w=[("pE", pi)])
                        for (rows, c0, ncol, view, m_ap) in mask_ops:
                            P.op("pool", I("tensor_tensor", out=view(pT[:rows, pi, c0:c0 + ncol]),
                                           in0=view(pE[:rows, pi, c0:c0 + ncol]), in1=m_ap, op=ALU.mult),
                                 r=[("pE", pi), ("cstb",)], w=[("pT", pi)])

                    def PV_stage():
                        pi = st["pi"]
                        for (o_ap, v_ap, rows, c0, n, st_, sp_, potok) in pv_list:
                            P.op("pe", I("matmul", o_ap, v_ap, pT[:rows, pi, c0:c0 + n], start=st_, stop=sp_),
                                 r=[("pT", pi), ("V",)], w=[potok])
                        if tail is not None:
                            tail()
                    jobs.append((S_stage, PV_stage))

                for h in range(8):
                    c4, pb = h // 2, (h % 2) * 64
                    ob = h % 2
                    qh = qT[pb:pb + 64, c4, :]
                    kh = kT[pb:pb + 64, c4, :]
                    v22 = lambda a: a.rearrange("p (q t k) -> p q t k", q=2, t=2)
                    pob = po_cnt[0] % 2
                    po_cnt[0] += 1
                    for half in range(2):
                        pairs, pv = [], []
                        for qi_, qb in enumerate((2 * half, 2 * half + 1)):
                            tq = tok0 + 128 * qb
                            qa = qh[:, qb * 128:(qb + 1) * 128]
                            vprev = V1[:, 1 - slot, 3, h, :] if qb == 0 else V1[:, slot, qb - 1, h, :]
                            pairs.append((kh[:, tq:tq + 128], 128, qa, 128))
                            pairs.append((kh[:, tq - 128:tq], 128, qa, 128))
                            oq = po[pob][:65, qb * 128:(qb + 1) * 128]
                            pv.append((oq, V1[:, slot, qb, h, :], 128, 256 * qi_, 128, True, False, ("po", pob)))
                            pv.append((oq, vprev, 128, 256 * qi_ + 128, 128, False, True, ("po", pob)))
                        if gm == 0 and half == 0:
                            v12 = lambda a: a.rearrange("p (t k) -> p t k", t=2)
                            mops = [(128, 0, 256, v12, m_cph), (128, 256, 256, v12, m_cp)]
                        else:
                            mops = [(128, 0, 512, v22, m_cp.unsqueeze(1).broadcast_to([128, 2, 2, 128]))]
                        tail = None
                        if half == 1:
                            def tail(pob=pob, ob=ob):
                                P.op("act", I("copy", out=oacc[:, ob, :], in_=po[pob][:65, :]),
                                     r=[("po", pob)], w=[("oacc", ob)])
                        make_job(pairs, mops, pv, tail)
                    pob = po_cnt[0] % 2
                    po_cnt[0] += 1
                    for half in range(2):
                        pairs, pv = [], []
                        for qi_, r4 in enumerate((2 * half, 2 * half + 1)):
                            qa = qh[:, r4:512:4]
                            pairs.append((kh[:, tok0 + r4:tok0 + 512:4], 128, qa, 128))
                            pairs.append((kh[:, tok0 - 512 + r4:tok0:4], 128, qa, 128))
                            oq = po[pob][:65, r4 * 128:(r4 + 1) * 128]
                            pv.append((oq, V4[:, slot, r4, h, :], 128, 256 * qi_, 128, True, False, ("po", pob)))
                            pv.append((oq, V4[:, 1 - slot, r4, h, :], 128, 256 * qi_ + 128, 128, False, True, ("po", pob)))
                        mm = m_cph if gm == 0 else m_cp
                        mops = [(128, 0, 512, v22, mm.unsqueeze(1).broadcast_to([128, 2, 2, 128]))]
                        tail = None
                        if half == 1:
                            def tail(pob=pob, ob=ob):
                                P.op("dve", I("tensor_tensor",
                                    out=oacc[:, ob, :].rearrange("p (i r) -> p r i", r=4),
                                    in0=oacc[:, ob, :].rearrange("p (i r) -> p r i", r=4),
                                    in1=po[pob][:65, :].rearrange("p (r i) -> p r i", r=4), op=ALU.add),
                                    r=[("po", pob)], w=[("oacc", ob)])
                        make_job(pairs, mops, pv, tail)
                    pob = po_cnt[0] % 2
                    po_cnt[0] += 1
                    Mc = 32 * (gm + 1)
                    v8 = lambda a: a.rearrange("p (r i) -> p r i", r=8)
                    for quarter in range(2):
                        pairs, pv = [], []
                        for j in range(8):
                            r16 = 8 * quarter + j
                            pairs.append((kh[:, r16:L:16], 128, qh[:, r16:512:16], 32))
                        for j in range(8):
                            r16 = 8 * quarter + j
                            pairs.append((kh[:, L + r16:L + r16 + 16 * (Mc - 1) + 1:16], Mc, qh[:, r16:512:16], 32))
                        for j in range(8):
                            r16 = 8 * quarter + j
                            oq = po[pob][:65, r16 * 32:(r16 + 1) * 32]
                            pv.append((oq, V16[:, r16, h, :], 128, 32 * j, 32, True, False, ("po", pob)))
                            pv.append((oq, V16[:Mc, 16 + r16, h, :], Mc, 256 + 32 * j, 32, False, True, ("po", pob)))
                        mops = [(128, 0, 256, v8, mprevh[:, 32 * gm:32 * gm + 32].unsqueeze(1).broadcast_to([128, 8, 32])),
                                (Mc, 256, 256, v8, mcur[:Mc, 32 * gm:32 * gm + 32].unsqueeze(1).broadcast_to([Mc, 8, 32]))]
                        tail = None
                        if quarter == 1:
                            def tail(pob=pob, ob=ob, h=h):
                                P.op("dve", I("tensor_tensor",
                                    out=oacc[:, ob, :].rearrange("p (i r) -> p r i", r=16),
                                    in0=oacc[:, ob, :].rearrange("p (i r) -> p r i", r=16),
                                    in1=po[pob][:65, :].rearrange("p (r i) -> p r i", r=16), op=ALU.add),
                                    r=[("po", pob)], w=[("oacc", ob)])
                                P.op("dve", I("reciprocal", out=rden[64:65, ob, :], in_=oacc[64:65, ob, :]),
                                     r=[("oacc", ob)], w=[("rden", ob)])
                                zz = zc[0] % 2
                                zc[0] += 1
                                P.op("dve", I("tensor_copy", out=rhi[64:65, ob, :], in_=rden[64:65, ob, :]),
                                     r=[("rden", ob)], w=[("rhi", ob)])
                                P.op("dve", I("tensor_tensor", out=rlo[64:65, ob, :], in0=rden[64:65, ob, :],
                                              in1=rhi[64:65, ob, :], op=ALU.subtract),
                                     r=[("rden", ob), ("rhi", ob)], w=[("rlo", ob)])
                                P.op("pe", I("matmul", pz[zz][:64, :], onesh[64:65, 0:64], rhi[64:65, ob, :],
                                             start=True, stop=False), r=[("rhi", ob), ("onesf",)], w=[("pz", zz)])
                                P.op("pe", I("matmul", pz[zz][:64, :], onesh[64:65, 0:64], rlo[64:65, ob, :],
                                             start=False, stop=True), r=[("rlo", ob), ("onesf",)], w=[("pz", zz)])
                                P.op("dve", I("tensor_tensor", out=catA[:, h, :], in0=oacc[0:64, ob, :],
                                              in1=pz[zz][:64, :], op=ALU.mult),
                                     r=[("pz", zz), ("oacc", ob)], w=[("catA",)])
                        make_job(pairs, mops, pv, tail)
                prev_pv = None
                for (S_stage, PV_stage) in jobs:
                    S_stage()
                    if prev_pv is not None:
                        prev_pv()
                    prev_pv = PV_stage
                prev_pv()
                P.dma("sp", cats[:, :, gm * 512:(gm + 1) * 512], catA[:, :, :], r=[("catA",)])
            P.flush("B1")

        if "B2" not in PHASES:
            P.flush() if any(P.ops[e] for e in ENGS) else None
            return nc
        woAv = woutb[0:512, :].rearrange("(h p) n -> p h n", p=64)
        woHv = woutb[512:1024, :].rearrange("(k p) n -> p k n", p=128)

        def hgrn_common(stk, tagp, T):
            d = {}
            d["kk"] = sb(stk, "kk" + tagp, [128, 4, T], F32)
            d["ecum"] = sb(stk, "ecum" + tagp, [128, 4, T], F32)
            d["tmpa"] = sb(stk, "tmpa" + tagp, [128, 2, T], F32)
            d["tmpb"] = sb(stk, "tmpb" + tagp, [128, 2, T], F32)
            d["tmpc"] = sb(stk, "tmpc" + tagp, [128, 2, T], F32)
            d["kdec"] = sb(stk, "kdec" + tagp, [128, 4, T], BF16)
            d["kendT"] = sb(stk, "kendT" + tagp, [128, 4, T], BF16)
            d["qdec"] = sb(stk, "qdec" + tagp, [128, 4, T], BF16)
            d["gs"] = sb(stk, "gs" + tagp, [128, 4, T], BF16)
            d["osb"] = sb(stk, "osb" + tagp, [128, 2, T], F32)
            d["osq"] = sb(stk, "osq" + tagp, [128, 2, T], BF16)
            d["catH"] = sb(stk, "catH" + tagp, [128, 4, T], BF16)
            d["oml"] = sb(stk, "oml" + tagp, [128, 2, 4], F32)
            d["onesb"] = sb(stk, "onesb" + tagp, [128, 128], BF16)
            d["cstf"] = sb(stk, "cstf" + tagp, [128, NCST], F32)
            d["cstb"] = sb(stk, "cstbh" + tagp, [128, NCST], BF16)
            d["identb"] = sb(stk, "identb" + tagp, [128, 128], BF16)
            cstf, oml = d["cstf"], d["oml"]
            P.dma("sp", cstf[:, :], cst[:, :], w=[("cstf",)])
            P.dma("pool", d["cstb"][:, :], cst[:, :], w=[("cstbh",)])
            P.op("pool", I("memset", d["onesb"][:], 1.0), w=[("onesb",)])
            P.op("dve", I("tensor_tensor", out=oml[:, 0, :], in0=cstf[:, C_LB + 4:C_LB + 8],
                                                  in1=cstf[:, C_LB:C_LB + 4], op=ALU.subtract),
                 r=[("cstf",)], w=[("oml",)])
            P.op("act", I("activation", out=oml[:, 0, :], in_=oml[:, 0, :], func=AF.Sigmoid), w=[("oml",)])
            P.op("dve", I("tensor_scalar", out=oml[:, 1, :], in0=oml[:, 0, :],
                                                  scalar1=cstf[:, C_FLAG:C_FLAG + 1], scalar2=None, op0=ALU.mult),
                 r=[("cstf",)], w=[("oml",)])
            return d

        def hgrn_gates(d, T, xT, xr, wi, load_wblock, pz, zc, halo, main, rmask_ap, chunk):
            kk, ecum, tmpa, tmpb, tmpc = d["kk"], d["ecum"], d["tmpa"], d["tmpb"], d["tmpc"]
            kdec, kendT, qdec, gs, oml = d["kdec"], d["kendT"], d["qdec"], d["gs"], d["oml"]
            nch = T // chunk

            def proj(bw, hh):
                zz = zc[0] % 2
                zc[0] += 1
                for c in range(8):
                    P.op("pe", I("matmul",
                        pz[zz][:, :T], wi[:, bw, c, hh * 128:(hh + 1) * 128], xT[:, c, :T],
                        start=(c == 0), stop=(c == 7)), r=xr + [("wi", bw)], w=[("pz", zz)])
                return zz

            bw = load_wblock(W_FH)
            for hh in range(4):
                zz = proj(bw, hh)
                tb = hh % 2
                P.op("act", I("activation", out=tmpa[:, tb, :], in_=pz[zz][:, :T],
                                                                  func=AF.Sigmoid, scale=-1.0),
                     r=[("pz", zz)], w=[("tmpa", tb)])
                P.op("dve", I("tensor_scalar",
                    out=kk[:, hh, :], in0=tmpa[:, tb, :], scalar1=oml[:, 1 if halo else 0, hh:hh + 1],
                    scalar2=None, op0=ALU.mult), r=[("tmpa", tb), ("oml",)], w=[("kk", hh)])
                P.op("act", I("activation", out=tmpb[:, tb, :], in_=kk[:, hh, :], func=AF.Ln,
                                                                  scale=-1.0, bias=oneb[:, :]),
                     r=[("kk", hh)], w=[("tmpb", tb)])
                P.op("dve", I("tensor_tensor_scan",
                    out=tmpc[:, tb, :], data0=rmask_ap, data1=tmpb[:, tb, :], initial=0.0,
                    op0=ALU.mult, op1=ALU.add), r=[("tmpb", tb), ("cstf",)], w=[("tmpc", tb)])
                P.op("act", I("activation", out=ecum[:, hh, :], in_=tmpc[:, tb, :], func=AF.Exp),
                     r=[("tmpc", tb)], w=[("ecum", hh)])
                P.op("act", I("activation", out=tmpa[:, tb, :], in_=tmpc[:, tb, :], func=AF.Exp, scale=-1.0),
                     r=[("tmpc", tb)], w=[("tmpa", tb)])
                P.op("dve", I("tensor_tensor",
                    out=kdec[:, hh, :], in0=kk[:, hh, :], in1=tmpa[:, tb, :], op=ALU.mult),
                    r=[("kk", hh), ("tmpa", tb)], w=[("kdec", hh)])
                P.op("dve", I("tensor_tensor",
                    out=kendT[:, hh, :].rearrange("p (n c) -> p n c", c=chunk),
                    in0=kdec[:, hh, :].rearrange("p (n c) -> p n c", c=chunk),
                    in1=ecum[:, hh, chunk - 1:T:chunk].unsqueeze(2).broadcast_to([128, nch, chunk]), op=ALU.mult),
                    r=[("kdec", hh), ("ecum", hh)], w=[("kendT", hh)])
            if main:
                bw = load_wblock(W_QH)
                for hh in range(4):
                    zz = proj(bw, hh)
                    tb = hh % 2
                    P.op("act", I("activation", out=tmpb[:, tb, :], in_=pz[zz][:, :T], func=AF.Silu),
                         r=[("pz", zz)], w=[("tmpb", tb)])
                    P.op("dve", I("tensor_tensor",
                        out=qdec[:, hh, :], in0=tmpb[:, tb, :], in1=ecum[:, hh, :], op=ALU.mult),
                        r=[("tmpb", tb), ("ecum", hh)], w=[("qdec", hh)])
                bw = load_wblock(W_GH)
                for hh in range(4):
                    zz = proj(bw, hh)
                    P.op("act", I("activation", out=gs[:, hh, :], in_=pz[zz][:, :T], func=AF.Silu),
                         r=[("pz", zz)], w=[("gs", hh)])

        def hgrn_out(d, T, hh, o_psum, o_tok, pz, zc):
            osb, osq, catH, gs, cstf, onesb, tmpa = d["osb"], d["osq"], d["catH"], d["gs"], d["cstf"], d["onesb"], d["tmpa"]
            ob = hh % 2
            P.op("dve", I("tensor_copy", out=osb[:, ob, :], in_=o_psum), r=[o_tok], w=[("osb", ob)])
            P.op("pool", I("tensor_tensor", out=osq[:, ob, :], in0=osb[:, ob, :], in1=osb[:, ob, :], op=ALU.mult),
                 r=[("osb", ob)], w=[("osq", ob)])
            zz = zc[0] % 2
            zc[0] += 1
            P.op("pe", I("matmul", pz[zz][:, :T], onesb[:, :], osq[:, ob, :], start=True, stop=True),
                 r=[("osq", ob), ("onesb",)], w=[("pz", zz)])
            P.op("act", I("activation", out=tmpa[:, ob, :], in_=pz[zz][:, :T], func=AF.Sqrt,
                                               scale=1.0 / 128.0, bias=epsb[:, :]),
                 r=[("pz", zz)], w=[("tmpa", ob)])
            P.op("dve", I("reciprocal", out=tmpa[:, ob, :], in_=tmpa[:, ob, :]), w=[("tmpa", ob)])
            P.op("dve", I("tensor_tensor", out=osb[:, ob, :], in0=osb[:, ob, :], in1=tmpa[:, ob, :], op=ALU.mult),
                 r=[("tmpa", ob)], w=[("osb", ob)])
            P.op("dve", I("scalar_tensor_tensor",
                out=catH[:, hh, :], in0=osb[:, ob, :], scalar=cstf[:, C_GN + hh:C_GN + hh + 1], in1=gs[:, hh, :],
                op0=ALU.mult, op1=ALU.mult), r=[("osb", ob), ("gs", hh), ("cstf",)], w=[("catH",)])

        oneb = sb(gstack, "oneb", [128, 1], F32)
        P.op("pool", I("memset", oneb[:], 1.0), w=[("oneb",)])

        with ExitStack() as stk:
            xt = sb(stk, "xtC", [128, 2, D], F32)
            xT = sb(stk, "xTC", [128, 8, 512], BF16)
            wi = sb(stk, "wiC", [128, 2, 8, 512], BF16)
            d = hgrn_common(stk, "C", 512)
            kend_tm = sb(stk, "kendtm", [128, 4, 512], BF16)
            vh_tm = sb(stk, "vhtm", [128, 4, 512], BF16)
            S = sb(stk, "Sst", [128, 4, 128], F32)
            Sbf = sb(stk, "Sbf", [128, 4, 8, 128], BF16)
            ATm = sb(stk, "ATm", [128, 4, 4, 64], BF16)
            catAs = sb(stk, "catAs", [64, 8, 512], BF16)
            woA = sb(stk, "woA", [64, 8, D], BF16)
            woH = sb(stk, "woH", [128, 4, D], BF16)
            yb = sb(stk, "ybC", [128, 4, D], F32)
            gb = sb(stk, "gbC", [128, 2, D], F32)
            stats = sb(stk, "statsC", [128, 4, 2, 6], F32)
            mv = sb(stk, "mvC", [128, 4, 2], F32)
            rstd = sb(stk, "rstdC", [128, 4], F32)
            identf = make_ident(stk, "C")
            identb = d["identb"]
            tp = [ps(stk, "tpC%d" % i, [128, 512], F32) for i in range(2)]
            tpb = ps(stk, "tpbC", [128, 1024], BF16)
            pz = [ps(stk, "pzC%d" % i, [128, 512], F32) for i in range(2)]
            pub = ps(stk, "pubC", [128, 512], F32)
            pa = ps(stk, "paC", [128, 512], F32)
            pob = ps(stk, "pobC", [128, 512], F32)
            P.op("act", I("copy", out=identb[:], in_=identf[:]), r=[("identf",)], w=[("identb",)])
            P.op("pool", I("memset", S[:], 0.0), w=[("S",)])
            P.dma("sp", woA[:, :, :], woAv, w=[("woA",)])
            P.dma("sp", woH[:, :, :], woHv, w=[("woH",)])
            load_gb(gb, 2)
            cstf, cstb = d["cstf"], d["cstb"]
            wcnt = [0]

            def load_wblock(col0):
                bi = wcnt[0] % 2
                wcnt[0] += 1
                P.dma("sp", wi[:, bi, :, :], winv[:, :, col0:col0 + 512], w=[("wi", bi)])
                return bi

            zc = [0]
            for g in range(8):
                main = g >= 4
                gm = g - 4
                tok0 = 512 * g
                load_T(x1s, tok0, 512, xt, tp, xT, identf, "C")
                xr = [("xT", t) for t in range(4)]
                hgrn_gates(d, 512, xT, xr, wi, load_wblock, pz, zc, not main, main,
                           cstf[:, C_RMASK:C_RMASK + 512], 64)
                kendT, kdec, qdec, ecum = d["kendT"], d["kdec"], d["qdec"], d["ecum"]
                for t in range(4):
                    for hh in range(4):
                        P.op("pe", I("transpose",
                            tpb[:, hh * 128:(hh + 1) * 128], kendT[:, hh, t * 128:(t + 1) * 128], identb[:, :]),
                            r=[("kendT", hh), ("identb",)], w=[("tpb",)])
                    P.op("act", I("copy", out=kend_tm[:, t, :], in_=tpb[:, 0:512]), r=[("tpb",)], w=[("kendtm",)])
                bw = load_wblock(W_IH)
                for t in range(4):
                    zz = zc[0] % 2
                    zc[0] += 1
                    for c in range(8):
                        P.op("pe", I("matmul",
                            pz[zz][:, :], xT[:, c, t * 128:(t + 1) * 128], wi[:, bw, c, :],
                            start=(c == 0), stop=(c == 7)), r=xr + [("wi", bw)], w=[("pz", zz)])
                    P.op("act", I("copy", out=vh_tm[:, t, :], in_=pz[zz][:, :]),
                         r=[("pz", zz)], w=[("vhtm",)])
                for ch in range(8):
                    t, hf = ch // 2, ch % 2
                    pr = slice(64 * hf, 64 * hf + 64)
                    for hh in range(4):
                        P.op("pe", I("matmul",
                            pub[:, hh * 128:(hh + 1) * 128], kend_tm[pr, t, hh * 128:(hh + 1) * 128],
                            vh_tm[pr, t, hh * 128:(hh + 1) * 128], start=True, stop=True),
                            r=[("kendtm",), ("vhtm",)], w=[("pub",)])
                    if main:
                        P.op("pool", I("tensor_copy", out=Sbf[:, :, ch, :], in_=S[:, :, :]),
                             r=[("S",)], w=[("Sbf",)])
                    for hh in range(4):
                        P.op("dve", I("scalar_tensor_tensor",
                            out=S[:, hh, :], in0=S[:, hh, :], scalar=ecum[:, hh, ch * 64 + 63:ch * 64 + 64],
                            in1=pub[:, hh * 128:(hh + 1) * 128], op0=ALU.mult, op1=ALU.add),
                            r=[("pub",), ("ecum", hh)], w=[("S",)])
                if not main:
                    continue
                for t in range(4):
                    for hf in range(2):
                        ch = 2 * t + hf
                        pr = slice(64 * hf, 64 * hf + 64)
                        for hh in range(4):
                            P.op("pe", I("matmul",
                                pa[pr, hh * 64:(hh + 1) * 64], kdec[:, hh, ch * 64:(ch + 1) * 64],
                                qdec[:, hh, ch * 64:(ch + 1) * 64], start=True, stop=True),
                                r=[("kdec", hh), ("qdec", hh)], w=[("pa",)])
                    P.op("dve", I("tensor_tensor",
                        out=ATm[:, t, :, :], in0=pa[:, 0:256].rearrange("p (h i) -> p h i", h=4),
                        in1=cstb[:, C_M64:C_M64 + 64].unsqueeze(1).broadcast_to([128, 4, 64]), op=ALU.mult),
                        r=[("pa",), ("cstbh",)], w=[("ATm",)])
                for hh in range(4):
                    for ch in range(8):
                        t, hf = ch // 2, ch % 2
                        pr = slice(64 * hf, 64 * hf + 64)
                        P.op("pe", I("matmul",
                            pob[:, ch * 64:(ch + 1) * 64], Sbf[:, hh, ch, :], qdec[:, hh, ch * 64:(ch + 1) * 64],
                            start=True, stop=False), r=[("Sbf",), ("qdec", hh)], w=[("pob",)])
                        P.op("pe", I("matmul",
                            pob[:, ch * 64:(ch + 1) * 64], vh_tm[pr, t, hh * 128:(hh + 1) * 128], ATm[pr, t, hh, :],
                            start=False, stop=True), r=[("vhtm",), ("ATm",)], w=[("pob",)])
                    hgrn_out(d, 512, hh, pob[:, :], ("pob",), pz, zc)
                P.dma("sp", catAs[:, :, :], cats[:, :, gm * 512:(gm + 1) * 512], w=[("catAs",)])
                catH = d["catH"]

                def wout_mm(t, hf, pdt, pdtok, tt):
                    for h in range(8):
                        P.op("pe", I("matmul",
                            pdt[:tt, :], catAs[:, h, t * 128:t * 128 + tt], woA[:, h, hf * 512:(hf + 1) * 512],
                            start=(h == 0), stop=False), r=[("catAs",), ("woA",)], w=[pdtok])
                    for hh in range(4):
                        P.op("pe", I("matmul",
                            pdt[:tt, :], catH[:, hh, t * 128:t * 128 + tt], woH[:, hh, hf * 512:(hf + 1) * 512],
                            start=False, stop=(hh == 3)), r=[("catH",), ("woH",)], w=[pdtok])

                ln_epilogue(512, 4, x1s, tok0, [(x2s, gm * 512)], None, [(pz[0], ("pz", 0)), (pz[1], ("pz", 1))],
                            yb, gb, stats, mv, rstd, wout_mm)
            P.dma("sp", s_o[:, :, :], S[:, :, :], r=[("S",)])
            P.flush("B2")
        if "B3" not in PHASES:
            return nc
        with ExitStack() as stk:
            T = NS
            xt = sb(stk, "xtS", [128, 2, D], F32)
            xT = sb(stk, "xTS", [128, 8, T], BF16)
            wi = sb(stk, "wiS", [128, 2, 8, 512], BF16)
            d = hgrn_common(stk, "S", T)
            cs = sb(stk, "csS", [128, 2, T], F32)
            t1 = sb(stk, "t1S", [128, 2, T], F32)
            t2 = sb(stk, "t2S", [128, 2, T], F32)
            kfs = sb(stk, "kfsS", [128, 4, T], F32)
            kTs = sb(stk, "kTsS", [128, 4, T], BF16)
            qTs = sb(stk, "qTsS", [128, 4, T], BF16)
            knew = sb(stk, "knewS", [T, 512], F32)
            vnew = sb(stk, "vnewS", [T, 512], F32)
            vnb = sb(stk, "vnbS", [4, 2, 8, 65], BF16)
            kc = sb(stk, "kcS", [128, 2, 7, 512], F32)
            vc = sb(stk, "vcS", [128, 1, 7, 512], F32)
            kcT = sb(stk, "kcTS", [128, 4, 7 * 128], BF16)
            vcb = sb(stk, "vcbS", [128, 7, 8, 65], BF16)
            pexp = sb(stk, "pexpS", [128, 7, 8, 4], BF16)
            pTs = sb(stk, "pTsS", [128, 7, 8, 4], BF16)
            pexn = sb(stk, "pexnS", [4, 8, 4], BF16)
            pTn = sb(stk, "pTnS", [4, 8, 4], BF16)
            oS = sb(stk, "oSS", [65, 16, 32], F32)
            rdn = sb(stk, "rdnS", [65, 512], F32)
            onesf = sb(stk, "onesfS", [128, 64], F32)
            oneshS = sb(stk, "oneshS", [128, 64], BF16)
            rhiS = sb(stk, "rhiS", [65, 512], BF16)
            rloS = sb(stk, "rloS", [65, 512], BF16)
            catAs = sb(stk, "catAsS", [64, 8, T], BF16)
            iT = sb(stk, "iTS", [128, 4, T], BF16)
            S0f = sb(stk, "S0fS", [128, 64, 128], F32)
            S0b = sb(stk, "S0bS", [128, 2, 4, 128], BF16)
            ksvs = sb(stk, "ksvsS", [4, 2, 8, 128], BF16)
            ATs = sb(stk, "ATsS", [4, 64, 4], BF16)
            woA = sb(stk, "woAS", [64, 8, D], BF16)
            woH = sb(stk, "woHS", [128, 4, D], BF16)
            yb = sb(stk, "ybS", [128, 1, D], F32)
            gb = sb(stk, "gbS", [128, 2, D], F32)
            stats = sb(stk, "statsS", [128, 1, 2, 6], F32)
            mv = sb(stk, "mvS", [128, 1, 2], F32)
            rstd = sb(stk, "rstdS", [128, 1], F32)
            identf = make_ident(stk, "S")
            identb = d["identb"]
            cstf, cstb = d["cstf"], d["cstb"]
            tp = [ps(stk, "tpS%d" % i, [128, 512], F32) for i in range(2)]
            tpb = ps(stk, "tpbS", [128, 1024], BF16)
            pz = [ps(stk, "pzS%d" % i, [128, 512], F32) for i in range(2)]
            psc = ps(stk, "pscS", [128, 512], F32)
            pos = ps(stk, "posS", [128, 512], F32)
            pub = ps(stk, "pubS", [128, 512], F32)
            P.op("act", I("copy", out=identb[:], in_=identf[:]), r=[("identf",)], w=[("identb",)])
            P.op("pool", I("memset", onesf[:], 1.0), w=[("onesf",)])
            P.op("pool", I("memset", oneshS[:], 1.0), w=[("onesf",)])
            P.op("pool", I("memset", vcb[:, :, :, 64:65], 1.0), w=[("vcbi",)])
            P.op("pool", I("memset", vnb[:, :, :, 64:65], 1.0), w=[("vnbi",)])
            P.dma("sp", woA[:, :, :], woAv, w=[("woA",)])
            P.dma("sp", woH[:, :, :], woHv, w=[("woH",)])
            load_gb(gb, 2)
            stv = st.rearrange("(bh k) v -> k bh v", k=128)
            for i in range(4):
                P.dma("sp", S0f[:, 16 * i:16 * i + 16, :], stv[:, 16 * i:16 * i + 16, :], w=[("S0f", i)])
            P.dma("sp", cs[:, 0, :], rope[0, :, 2 * L:2 * L + T], w=[("cs",)])
            P.dma("sp", cs[:, 1, :], rope[1, :, 2 * L:2 * L + T], w=[("cs",)])
            wcnt = [0]

            def load_wblock(col0):
                bi = wcnt[0] % 2
                wcnt[0] += 1
                P.dma("sp", wi[:, bi, :, :], winv[:, :, col0:col0 + 512], w=[("wi", bi)])
                return bi

            zc = [0]
            load_T(x1s, 2 * L, T, xt, tp, xT, identf, "S")
            xr = [("xT", 0)]

            def rope_proj_s(col, colsw, dst_bf, dst_tok, f32dst):
                bi = load_wblock(col)
                bs = load_wblock(colsw)
                for cq in range(4):
                    za, zb = zc[0] % 2, (zc[0] + 1) % 2
                    zc[0] += 2
                    for (bw, zz) in ((bi, za), (bs, zb)):
                        for c in range(8):
                            P.op("pe", I("matmul",
                                pz[zz][:, :T], wi[:, bw, c, cq * 128:(cq + 1) * 128], xT[:, c, :],
                                start=(c == 0), stop=(c == 7)), r=xr + [("wi", bw)], w=[("pz", zz)])
                    tb = cq % 2
                    P.op("dve", I("tensor_tensor",
                        out=t1[:, tb, :], in0=pz[za][:, :T], in1=cs[:, 0, :], op=ALU.mult),
                        r=[("pz", za), ("cs",)], w=[("t1", tb)])
                    P.op("dve", I("tensor_tensor",
                        out=t2[:, tb, :], in0=pz[zb][:, :T], in1=cs[:, 1, :], op=ALU.mult),
                        r=[("pz", zb), ("cs",)], w=[("t2", tb)])
                    if f32dst is not None:
                        P.op("pool", I("tensor_tensor",
                            out=f32dst[:, cq, :], in0=t1[:, tb, :], in1=t2[:, tb, :], op=ALU.add),
                            r=[("t1", tb), ("t2", tb)], w=[("kfs",)])
                        P.op("act", I("copy", out=dst_bf[:, cq, :], in_=f32dst[:, cq, :]),
                             r=[("kfs",)], w=[dst_tok])
                    else:
                        P.op("pool", I("tensor_tensor",
                            out=dst_bf[:, cq, :], in0=t1[:, tb, :], in1=t2[:, tb, :], op=ALU.add),
                            r=[("t1", tb), ("t2", tb)], w=[dst_tok])

            rope_proj_s(W_KA, W_KS, kTs, ("kTs",), kfs)
            rope_proj_s(W_QA, W_QS, qTs, ("qTs",), None)
            for cq in range(4):
                P.op("pe", I("transpose", tp[0][:T, cq * 128:(cq + 1) * 128], kfs[:, cq, :], identf[:, :]),
                     r=[("kfs",), ("identf",)], w=[("tp", 0)])
            P.op("act", I("copy", out=knew[:, :], in_=tp[0][:T, :]), r=[("tp", 0)], w=[("knew",)])
            bv = load_wblock(W_VA)
            zz = zc[0] % 2
            zc[0] += 1
            for c in range(8):
                P.op("pe", I("matmul", pz[zz][:T, :], xT[:, c, :], wi[:, bv, c, :],
                                                          start=(c == 0), stop=(c == 7)),
                     r=xr + [("wi", bv)], w=[("pz", zz)])
            P.op("act", I("copy", out=vnew[:, :], in_=pz[zz][:T, :]), r=[("pz", zz)], w=[("vnew",)])
            for b in range(NSQ):
                P.dma("sp", nk_o[b, 2044:2048, :], knew[4 * b:4 * b + 4, :], r=[("knew",)])
                P.dma("sp", nv_o[b, 2044:2048, :], vnew[4 * b:4 * b + 4, :], r=[("vnew",)], w=[("nvrow", b)])

            mult = cstb[:, C_MULT:C_MULT + 28].rearrange("p (t s) -> p t s", t=7)
            for b in range(NSQ):
                bf = b % 2
                P.dma("sp", kc[:, bf, 0:4, :], ck[b, 1536:2048, :].rearrange("(t p) n -> p t n", p=128), w=[("kc", bf)])
                P.dma("pool", vnb[:, bf, :, 0:64], nv_o[b, 2044:2048, :].rearrange("s (h e) -> s h e", h=8),
                      r=[("nvrow", b), ("vnbi",)], w=[("vnb", bf)])
                P.dma("sp", vc[:, 0, 0:4, :], cv[b, 1536:2048, :].rearrange("(t p) n -> p t n", p=128), w=[("vc", 0)])
                for t in range(3):
                    for r in range(4):
                        P.dma("sp", kc[32 * r:32 * r + 32, bf, 4 + t, :], ck[b, t * 512 + r:(t + 1) * 512:16, :],
                              w=[("kc", bf)])
                        P.dma("sp", vc[32 * r:32 * r + 32, 0, 4 + t, :], cv[b, t * 512 + r:(t + 1) * 512:16, :],
                              w=[("vc", 0)])
                for c4 in range(4):
                    for tg in range(2):
                        tiles = list(range(4 * tg, min(7, 4 * tg + 4)))
                        bk = (2 * c4 + tg) % 2
                        for j, tl in enumerate(tiles):
                            P.op("pe", I("transpose",
                                tp[bk][:, j * 128:(j + 1) * 128], kc[:, bf, tl, c4 * 128:(c4 + 1) * 128], identf[:, :]),
                                r=[("kc", bf), ("identf",)], w=[("tp", bk)])
                        n = len(tiles) * 128
                        P.op("act", I("copy",
                            out=kcT[:, c4, tg * 512:tg * 512 + n], in_=tp[bk][:, :n]),
                            r=[("tp", bk)], w=[("kcT",)])
                for tl in range(7):
                    P.op("pool", I("tensor_copy",
                        out=vcb[:, tl, :, 0:64], in_=vc[:, 0, tl, :].rearrange("p (h e) -> p h e", h=8)),
                        r=[("vc", 0), ("vcbi",)], w=[("vcb",)])
                for par, bank, btok in ((0, psc, ("psc",)), (1, pub, ("pub",))):
                    pb = par * 64
                    for tl in range(7):
                        for h2 in range(4):
                            c4 = h2
                            P.op("pe", I("matmul",
                                bank[:, (tl * 4 + h2) * 4:(tl * 4 + h2) * 4 + 4], kcT[pb:pb + 64, c4, tl * 128:(tl + 1) * 128],
                                qTs[pb:pb + 64, c4, 4 * b:4 * b + 4], start=True, stop=True),
                                r=[("kcT",), ("qTs",)], w=[btok])
                    for h2 in range(4):
                        c4 = h2
                        P.op("pe", I("matmul",
                            bank[0:4, 112 + h2 * 4:112 + h2 * 4 + 4], kTs[pb:pb + 64, c4, 4 * b:4 * b + 4],
                            qTs[pb:pb + 64, c4, 4 * b:4 * b + 4], start=True, stop=True),
                            r=[("kTs",), ("qTs",)], w=[btok])
                    P.op("act", I("activation", out=pexp[:, :, par:8:2, :],
                                  in_=bank[:, 0:112].rearrange("p (t h s) -> p t h s", t=7, h=4),
                                  func=AF.Exp, scale=0.125), r=[btok], w=[("pexp",)])
                    P.op("act", I("activation", out=pexn[:, par:8:2, :],
                                  in_=bank[0:4, 112:128].rearrange("p (h s) -> p h s", h=4),
                                  func=AF.Exp, scale=0.125), r=[btok], w=[("pexn",)])
                P.op("dve", I("tensor_tensor",
                    out=pTs[:, :, :, :], in0=pexp[:, :, :, :],
                    in1=mult.unsqueeze(2).broadcast_to([128, 7, 8, 4]), op=ALU.mult),
                    r=[("pexp",), ("cstbh",)], w=[("pTs",)])
                P.op("dve", I("tensor_tensor",
                    out=pTn[:, :, :], in0=pexn[:, :, :],
                    in1=cstb[0:4, C_MNEW:C_MNEW + 4].unsqueeze(1).broadcast_to([4, 8, 4]), op=ALU.mult),
                    r=[("pexn",), ("cstbh",)], w=[("pTn",)])
                for h in range(8):
                    for tl in range(7):
                        P.op("pe", I("matmul",
                            pos[0:65, h * 4:h * 4 + 4], vcb[:, tl, h, :], pTs[:, tl, h, :],
                            start=(tl == 0), stop=False), r=[("vcb",), ("pTs",)], w=[("pos",)])
                    P.op("pe", I("matmul",
                        pos[0:65, h * 4:h * 4 + 4], vnb[0:4, bf, h, :], pTn[0:4, h, :], start=False, stop=True),
                        r=[("vnb", bf), ("pTn",)], w=[("pos",)])
                P.op("act", I("copy", out=oS[:, b, :], in_=pos[0:65, 0:32]), r=[("pos",)], w=[("oS",)])
            oSf = oS[:, :, :].rearrange("p b x -> p (b x)")
            P.op("dve", I("reciprocal", out=rdn[64:65, :], in_=oSf[64:65, :]), r=[("oS",)], w=[("rdn",)])
            P.op("dve", I("tensor_copy", out=rhiS[64:65, :], in_=rdn[64:65, :]), r=[("rdn",)], w=[("rhiS",)])
            P.op("dve", I("tensor_tensor", out=rloS[64:65, :], in0=rdn[64:65, :], in1=rhiS[64:65, :], op=ALU.subtract),
                 r=[("rdn",), ("rhiS",)], w=[("rloS",)])
            P.op("pe", I("matmul", pz[0][:64, :], oneshS[64:65, 0:64], rhiS[64:65, :], start=True, stop=False),
                 r=[("rhiS",), ("onesf",)], w=[("pz", 0)])
            P.op("pe", I("matmul", pz[0][:64, :], oneshS[64:65, 0:64], rloS[64:65, :], start=False, stop=True),
                 r=[("rloS",), ("onesf",)], w=[("pz", 0)])
            P.op("dve", I("tensor_tensor",
                out=catAs[:, :, :].rearrange("p h (b s) -> p b h s", s=4),
                in0=oS[0:64, :, :].rearrange("p b (h s) -> p b h s", s=4),
                in1=pz[0][:64, :].rearrange("p (b h s) -> p b h s", h=8, s=4), op=ALU.mult),
                r=[("pz", 0), ("oS",)], w=[("catAs",)])

            hgrn_gates(d, T, xT, xr, wi, load_wblock, pz, zc, False, True, cstf[:, C_RMASKS:C_RMASKS + T], 4)
            kendT, kdec, qdec, ecum = d["kendT"], d["kdec"], d["qdec"], d["ecum"]
            bw = load_wblock(W_IH)
            for hh in range(4):
                zz = zc[0] % 2
                zc[0] += 1
                for c in range(8):
                    P.op("pe", I("matmul",
                        pz[zz][:, :T], wi[:, bw, c, hh * 128:(hh + 1) * 128], xT[:, c, :],
                        start=(c == 0), stop=(c == 7)), r=xr + [("wi", bw)], w=[("pz", zz)])
                P.op("act", I("copy", out=iT[:, hh, :], in_=pz[zz][:, :T]),
                     r=[("pz", zz)], w=[("iT",)])
            for b in range(NSQ):
                for hh in range(4):
                    P.op("pe", I("matmul",
                        psc[0:4, (b * 4 + hh) * 4:(b * 4 + hh) * 4 + 4], kdec[:, hh, 4 * b:4 * b + 4],
                        qdec[:, hh, 4 * b:4 * b + 4], start=True, stop=True),
                        r=[("kdec", hh), ("qdec", hh)], w=[("psc",)])
            P.op("dve", I("tensor_tensor",
                out=ATs[:, :, :], in0=psc[0:4, 0:256].rearrange("p (x i) -> p x i", i=4),
                in1=cstb[0:4, C_MS4:C_MS4 + 4].unsqueeze(1).broadcast_to([4, 64, 4]), op=ALU.mult),
                r=[("psc",), ("cstbh",)], w=[("ATs",)])
            for b in range(NSQ):
                bf = b % 2
                for hh in range(4):
                    P.op("pe", I("transpose",
                        tpb[0:4, hh * 128:(hh + 1) * 128], kendT[:, hh, 4 * b:4 * b + 4], identb[:, :]),
                        r=[("kendT", hh), ("identb",)], w=[("tpb",)])
                for hh in range(4):
                    P.op("pe", I("transpose",
                        tpb[0:4, (4 + hh) * 128:(5 + hh) * 128], iT[:, hh, 4 * b:4 * b + 4], identb[:, :]),
                        r=[("iT",), ("identb",)], w=[("tpb",)])
                P.op("act", I("copy", out=ksvs[:, bf, :, :].rearrange("p x n -> p (x n)"), in_=tpb[0:4, :]),
                     r=[("tpb",)], w=[("ksvs", bf)])
                P.op("pool", I("tensor_copy", out=S0b[:, bf, :, :], in_=S0f[:, 4 * b:4 * b + 4, :]),
                     r=[("S0f", b // 4)], w=[("S0b", bf)])
                for hh in range(4):
                    bh = 4 * b + hh
                    P.op("pe", I("matmul",
                        pos[:, bh * 4:bh * 4 + 4], S0b[:, bf, hh, :], qdec[:, hh, 4 * b:4 * b + 4],
                        start=True, stop=False), r=[("S0b", bf), ("qdec", hh)], w=[("pos",)])
                    P.op("pe", I("matmul",
                        pos[:, bh * 4:bh * 4 + 4], ksvs[0:4, bf, 4 + hh, :], ATs[0:4, bh, :],
                        start=False, stop=True), r=[("ksvs", bf), ("ATs",)], w=[("pos",)])
                for hh in range(4):
                    P.op("pe", I("matmul",
                        pub[:, hh * 128:(hh + 1) * 128], ksvs[0:4, bf, hh, :], ksvs[0:4, bf, 4 + hh, :],
                        start=True, stop=True), r=[("ksvs", bf)], w=[("pub",)])
                for hh in range(4):
                    bh = 4 * b + hh
                    P.op("dve", I("scalar_tensor_tensor",
                        out=S0f[:, bh, :], in0=S0f[:, bh, :], scalar=ecum[:, hh, 4 * b + 3:4 * b + 4],
                        in1=pub[:, hh * 128:(hh + 1) * 128], op0=ALU.mult, op1=ALU.add),
                        r=[("pub",), ("ecum", hh), ("S0b", bf)], w=[("S0f", b // 4)])
            for i in range(4):
                P.dma("sp", ns_o[:, 16 * i:16 * i + 16, :], S0f[:, 16 * i:16 * i + 16, :], r=[("S0f", i)])
            for hh in range(4):
                hgrn_out(d, T, hh, pos[:, 0:256].rearrange("p (b h s) -> p b h s", h=4, s=4)[:, :, hh, :],
                         ("pos",), pz, zc)
            catH = d["catH"]

            def wout_mm(t, hf, pdt, pdtok, tt):
                for h in range(8):
                    P.op("pe", I("matmul",
                        pdt[:tt, :], catAs[:, h, :], woA[:, h, hf * 512:(hf + 1) * 512],
                        start=(h == 0), stop=False), r=[("catAs",), ("woA",)], w=[pdtok])
                for hh in range(4):
                    P.op("pe", I("matmul",
                        pdt[:tt, :], catH[:, hh, :], woH[:, hh, hf * 512:(hf + 1) * 512],
                        start=False, stop=(hh == 3)), r=[("catH",), ("woH",)], w=[pdtok])

            ln_epilogue(T, 1, x1s, 2 * L, [(x2s, L)], None, [(pz[0], ("pz", 0)), (pz[1], ("pz", 1))],
                        yb, gb, stats, mv, rstd, wout_mm)
            P.flush("B3")

        if "C" not in PHASES:
            return nc
        groupsC = [(512 * g, 512) for g in range(4)] + [(L, NS)]
        ffn_phase("C2", x2s, groupsC, lambda gi: [(y_o, groupsC[gi][0])], w2g, w2u, w2d, 4)
    return nc


def _rope_tables(positions):
    half = 32
    inv = (np.float32(10000.0) ** (-np.arange(half, dtype=np.float32) / np.float32(half))).astype(np.float32)
    ang = positions.astype(np.float32)[None, :] * inv[:, None]
    cos = np.cos(ang).astype(np.float32)
    sin = np.sin(ang).astype(np.float32)
    cosT = np.tile(cos, (4, 1))
    sinS = np.concatenate([-sin, sin, -sin, sin], axis=0)
    return cosT, sinS


def _consts(half_flag, hg_lower_bound, hg_norm_g):
    c = np.zeros((128, NCST), np.float32)
    k = np.arange(128)[:, None]
    q = np.arange(128)[None, :]
    c[:, C_MCUR:C_MCUR + 128] = (k <= q)
    c[:, C_MPREV:C_MPREV + 128] = (k >= q)
    c[:, C_MPREVH:C_MPREVH + 128] = (k >= q) * float(half_flag)
    c[:, C_M64:C_M64 + 64] = ((np.arange(128)[:, None] % 64) <= np.arange(64)[None, :])
    c[:, C_RMASK:C_RMASK + 512] = (np.arange(512) % 64 != 0)[None, :]
    c[:, C_RMASKS:C_RMASKS + 64] = (np.arange(64) % 4 != 0)[None, :]
    c[:, C_FLAG] = float(half_flag)
    lb = hg_lower_bound.reshape(2, 4, 128)
    c[:, C_LB:C_LB + 8] = lb.transpose(2, 0, 1).reshape(128, 8)
    c[:, C_GN:C_GN + 4] = hg_norm_g.reshape(4, 128).T
    p = np.arange(128)
    mult = np.zeros((128, 7, 4), np.float32)
    for t in range(7):
        if t < 4:
            row = 1536 + 128 * t + p
        else:
            row = 16 * ((t - 4) * 32 + p % 32) + p // 32
        for s_ in range(4):
            mult[:, t, s_] = ((row >= 1920 + s_).astype(np.float32) + ((row >= 1536) & (row % 4 == s_))
                              + (row % 16 == s_))
    c[:, C_MULT:C_MULT + 28] = mult.reshape(128, 28)
    sp = np.arange(4)[:, None]
    sq = np.arange(4)[None, :]
    c[0:4, C_MNEW:C_MNEW + 4] = (sp < sq) * 1.0 + (sp == sq) * 3.0
    c[0:4, C_MS4:C_MS4 + 4] = (sp <= sq)
    return c


_PERM = None


def _swap_perm():
    idx = np.arange(512).reshape(8, 2, 32)
    return idx[:, ::-1, :].reshape(512)


def kernel(x_prompt, x_sample, cache_k, cache_v, state_hgrn,
           ffn1_w_gate, ffn1_w_up, ffn1_w_down, ln1_g, ln1_b,
           w_in, hg_lower_bound, hg_norm_g, w_out, ln2_g, ln2_b,
           ffn2_w_gate, ffn2_w_up, ffn2_w_down, ln3_g, ln3_b):
    f = lambda a: np.ascontiguousarray(np.asarray(a, dtype=np.float32))
    x_prompt, x_sample = f(x_prompt), f(x_sample)
    cache_k, cache_v, state_hgrn = np.asarray(cache_k), np.asarray(cache_v), np.asarray(state_hgrn)
    win0 = f(w_in)[0]
    perm = _swap_perm()
    win_ext = np.ascontiguousarray(np.concatenate(
        [win0, win0[:, 0:512][:, perm], win0[:, 512:1024][:, perm]], axis=1))
    lnv = np.ascontiguousarray(np.stack([f(ln1_g)[0], f(ln1_b)[0], f(ln2_g)[0], f(ln2_b)[0], f(ln3_g)[0], f(ln3_b)[0]]))
    shared = {
        "w1g": f(ffn1_w_gate)[0], "w1u": f(ffn1_w_up)[0], "w1d": f(ffn1_w_down)[0],
        "w2g": f(ffn2_w_gate)[0], "w2u": f(ffn2_w_up)[0], "w2d": f(ffn2_w_down)[0],
        "win": win_ext, "wout": f(w_out)[0], "lnv": lnv,
    }
    hlb, hng = f(hg_lower_bound), f(hg_norm_g)[0]
    in_maps = []
    for c in range(NCORES):
        b, half = c // 2, c % 2
        xs = x_sample[16 * c:16 * c + 16].reshape(NS, D)
        if half == 0:
            halo = np.zeros((L, D), np.float32)
            pos_h = np.zeros(L, np.int64)
        else:
            halo = x_prompt[b, 0:L]
            pos_h = np.arange(0, L)
        xin = np.concatenate([halo, x_prompt[b, half * L:(half + 1) * L], xs], axis=0)
        pos = np.concatenate([pos_h, np.arange(half * L, (half + 1) * L), 8192 + (np.arange(NS) % 4)])
        cosT, sinS = _rope_tables(pos)
        m = {}
        for kk, vv in shared.items():
            if kk == "lnv":
                m[kk] = vv
            else:
                m[kk] = np.concatenate([vv, np.full((1, vv.shape[1]), float(c), np.float32)], axis=0)
        m["xin"] = np.ascontiguousarray(xin)
        m["ck"] = np.ascontiguousarray(cache_k[0, 16 * c:16 * c + 16].reshape(16, 2048, 512), dtype=np.float32)
        m["cv"] = np.ascontiguousarray(cache_v[0, 16 * c:16 * c + 16].reshape(16, 2048, 512), dtype=np.float32)
        m["st"] = np.ascontiguousarray(state_hgrn[0, 16 * c:16 * c + 16].reshape(64 * 128, 128), dtype=np.float32)
        m["cst"] = _consts(half, hlb, hng)
        m["rope"] = np.ascontiguousarray(np.stack([cosT, sinS]))
        in_maps.append(m)
    import os
    if os.environ.get("KDBG"):
        ncr = int(os.environ.get("KCORES", "8"))
        nc = build_program(tuple(os.environ["KDBG"].split(",")))
        ims = in_maps[:ncr]
        for m in ims:
            m["ck"], m["cv"] = m["ck"][:NSQ], m["cv"][:NSQ]
        res = run_bass_kernel_spmd(nc, ims, core_ids=list(range(ncr)), trace=bool(os.environ.get("KTRACE")))
        return res, in_maps
    nc = build_program()
    res = run_bass_kernel_spmd(nc, in_maps, core_ids=list(range(NCORES)))
    R = res.results
    y_prompt = np.empty((4, 4096, D), np.float32)
    y_sample = np.empty((128, 4, D), np.float32)
    nkp = np.empty((1, 4, 2048, 8, 64), np.float32)
    nvp = np.empty((1, 4, 2048, 8, 64), np.float32)
    nsp = np.empty((1, 4, 4, 128, 128), np.float32)
    nks = np.empty((1, 128, 2048, 8, 64), np.float32)
    nvs = np.empty((1, 128, 2048, 8, 64), np.float32)
    nss = np.empty((1, 128, 4, 128, 128), np.float32)
    for c in range(NCORES):
        b, half = c // 2, c % 2
        r = R[c]
        y_prompt[b, half * L:(half + 1) * L] = r["y"][0:L]
        y_sample[16 * c:16 * c + 16] = r["y"][L:L + NS].reshape(16, 4, D)
        if half == 1:
            nkp[0, b] = r["kTo"].T.reshape(2048, 8, 64)
            nvp[0, b] = r["vo"].reshape(2048, 8, 64)
            nsp[0, b] = r["so"].transpose(1, 0, 2)
        nks[0, 16 * c:16 * c + 16] = r["nk"].reshape(16, 2048, 8, 64)
        nvs[0, 16 * c:16 * c + 16] = r["nv"].reshape(16, 2048, 8, 64)
        nss[0, 16 * c:16 * c + 16] = r["ns"].reshape(128, 16, 4, 128).transpose(1, 2, 0, 3)
    return (y_prompt, y_sample, nkp, nvp, nsp, nks, nvs, nss)
```

```python
import numpy as np
from contextlib import ExitStack
import concourse.bass as bass
import concourse.mybir as mybir
from concourse.bass_utils import run_bass_kernel_spmd

F32 = mybir.dt.float32
BF16 = mybir.dt.bfloat16
AF = mybir.ActivationFunctionType
ALU = mybir.AluOpType

NCORES = 8
D = 1024
FF = 2816
NK = FF // 128
L = 2048
NS = 64
NTOK = 2 * L + NS
NMAIN = L + NS
WINC = 4608
ALPHA = 2.0 ** 0.25
EPS = 1e-5
NCST = 1080
import os
NSQ = int(os.environ.get("KNSQ", "16"))
KLVL = int(os.environ.get("KLVL", "99"))
C_MCUR, C_MPREV, C_MPREVH, C_M64, C_RMASK, C_RMASKS, C_FLAG = 0, 128, 256, 384, 448, 960, 1024
C_LB, C_GN, C_MULT, C_MNEW, C_MS4 = 1025, 1033, 1037, 1065, 1069
W_QA, W_KA, W_VA, W_QH, W_FH, W_IH, W_GH, W_QS, W_KS = 0, 512, 1024, 1536, 2048, 2560, 3072, 3584, 4096

ENGS = ("pe", "act", "dve", "pool", "sp")
PSUM_TOK = {"tp", "pg", "pu", "pd", "pz", "psc", "po", "pub", "pa", "pob", "tpb", "pos"}


def I(name, *a, **k):
    return lambda e: getattr(e, name)(*a, **k)


class Op:
    __slots__ = ("eng", "fn", "deps", "dma", "sig", "sem", "val")

    def __init__(self, eng, fn, dma):
        self.eng, self.fn, self.dma = eng, fn, dma
        self.deps, self.sig, self.sem, self.val = [], dma, None, 0


class Prog:
    def __init__(self, nc, stack):
        self.nc = nc
        self.esem = {e: stack.enter_context(nc.semaphore("es_" + e)) for e in ENGS}
        self.ecount = {e: 0 for e in ENGS}
        self.dsems = {q: [stack.enter_context(nc.semaphore("ds_%s%d" % (q, i))) for i in range(n)]
                      for q, n in (("sp", 24), ("pool", 12), ("act", 6))}
        self.dval = {}
        self.drr = {q: 0 for q in self.dsems}
        self.reset()

    def reset(self):
        self.ops = {e: [] for e in ENGS}
        self.last_w, self.readers = {}, {}
        self.gen_deps = {}
        self.dlast = {}

    def op(self, eng, fn, r=(), w=(), dma=False):
        o = Op(eng, fn, dma)
        deps = []
        for t in r:
            if t in self.last_w:
                deps.extend(self.last_w[t])
            if t[0] in PSUM_TOK:
                rd = self.readers.get(t)
                if rd:
                    deps.extend(x for en, x in rd[0].items() if en != eng)
        append_w = set()
        for t in w:
            rd = self.readers.get(t)
            has_rd = bool(rd and (rd[0] or rd[1]))
            lw = self.last_w.get(t)
            if dma and lw and not has_rd and all(x.dma for x in lw):
                append_w.add(t)
                deps.extend(self.gen_deps.get(t, ()))
                continue
            gd = []
            if lw:
                gd.extend(lw)
            if rd:
                gd.extend(rd[0].values())
                gd.extend(rd[1])
            self.gen_deps[t] = gd
            deps.extend(gd)
        for t in r:
            rd = self.readers.setdefault(t, ({}, []))
            if dma:
                rd[1].append(o)
            else:
                rd[0][eng] = o
        for t in w:
            if t in append_w:
                self.last_w[t].append(o)
            else:
                self.last_w[t] = [o]
                self.readers[t] = ({}, [])
        if dma:
            ring = self.dsems[eng]
            s = ring[self.drr[eng] % len(ring)]
            self.drr[eng] += 1
            if s in self.dlast:
                deps.append(self.dlast[s])
            self.dlast[s] = o
            o.sem = s
            self.dval[s] = self.dval.get(s, 0) + 16
            o.val = self.dval[s]
        seen = set()
        for d in deps:
            if d is o or id(d) in seen:
                continue
            seen.add(id(d))
            if d.eng == "pe" and eng == "pe" and not d.dma:
                continue
            d.sig = True
            o.deps.append(d)
        self.ops[eng].append(o)
        return o

    def dma(self, q, out, in_, r=(), w=()):
        return self.op(q, I("dma_start", out=out, in_=in_), r=r, w=w, dma=True)

    def flush(self, name="ph"):
        nc = self.nc
        finals = []
        for e in ENGS:
            lst = self.ops[e]
            comp = [o for o in lst if not o.dma]
            if comp:
                comp[-1].sig = True
            finals.extend(o for o in lst if o.dma)
        for e in ENGS:
            for o in self.ops[e]:
                if not o.dma and o.sig:
                    self.ecount[e] += 1
                    o.sem, o.val = self.esem[e], self.ecount[e]
        last_comp = {}
        for e in ENGS:
            comp = [o for o in self.ops[e] if not o.dma]
            if comp:
                last_comp[e] = comp[-1]
        dma_final = {}
        for o in finals:
            dma_final[o.sem] = max(dma_final.get(o.sem, 0), o.val)

        def emit(eng_name):
            def body(eng):
                seen = {}
                for o in self.ops[eng_name]:
                    for d in o.deps:
                        if seen.get(d.sem, 0) < d.val:
                            eng.wait_ge(d.sem, d.val)
                            seen[d.sem] = d.val
                    ins = o.fn(eng)
                    if o.sig:
                        ins.then_inc(o.sem, 16 if o.dma else 1)
                for e2, lo in last_comp.items():
                    if seen.get(lo.sem, 0) < lo.val:
                        eng.wait_ge(lo.sem, lo.val)
                for s, v in dma_final.items():
                    if seen.get(s, 0) < v:
                        eng.wait_ge(s, v)
            return body

        with nc.named_scope(name), nc.Block() as block:
            block.tensor(emit("pe"))
            block.scalar(emit("act"))
            block.vector(emit("dve"))
            block.gpsimd(emit("pool"))
            block.sync(emit("sp"))
        self.reset()


def ffn_pieces():
    return [(0, 6), (6, 12), (12, 17), (17, 22)]


def piece_of(k):
    for i, (a, b) in enumerate(ffn_pieces()):
        if a <= k < b:
            return i
    raise ValueError


def build_program(PHASES=("A", "B1", "B2", "B3", "C")):
    nc = bass.Bass("TRN2", target_bir_lowering=False)

    def din(name, shape):
        return nc.dram_tensor(name, shape, F32, kind="ExternalInput").ap()

    def dout(name, shape):
        return nc.dram_tensor(name, shape, F32, kind="ExternalOutput").ap()

    xin = din("xin", [NTOK, D])
    ck = din("ck", [NSQ, 2048, 512])
    cv = din("cv", [NSQ, 2048, 512])
    st = din("st", [64 * 128, 128])
    w1g, w1u, w1d = din("w1g", [D + 1, FF])[0:D, :], din("w1u", [D + 1, FF])[0:D, :], din("w1d", [FF + 1, D])[0:FF, :]
    w2g, w2u, w2d = din("w2g", [D + 1, FF])[0:D, :], din("w2u", [D + 1, FF])[0:D, :], din("w2d", [FF + 1, D])[0:FF, :]
    win = din("win", [D + 1, WINC])[0:D, :]
    wout = din("wout", [D + 1, D])[0:D, :]
    lnv = din("lnv", [6, D])
    cst = din("cst", [128, NCST])
    rope = din("rope", [2, 128, NTOK])

    y_o = dout("y", [NMAIN, D])
    kT_o = dout("kTo", [512, L])
    v_o = dout("vo", [L, 512])
    s_o = dout("so", [128, 4, 128])
    nk_o = dout("nk", [NSQ, 2048, 512])
    nv_o = dout("nv", [NSQ, 2048, 512])
    ns_o = dout("ns", [128, 64, 128])

    import os
    if os.environ.get("KDBG"):
        x1s = dout("x1s", [NTOK, D])
        x2s = dout("x2s", [NMAIN, D])
    else:
        x1s = nc.dram_tensor("x1s", [NTOK, D], F32).ap()
        x2s = nc.dram_tensor("x2s", [NMAIN, D], F32).ap()
    winb = nc.dram_tensor("winb", [D, WINC], BF16).ap()
    woutb = nc.dram_tensor("woutb", [D, D], BF16).ap()
    cats = nc.dram_tensor("cats", [64, 8, NMAIN], BF16).ap()
    vscr = nc.dram_tensor("vscr", [2 * L, 520], BF16).ap()

    gstack = ExitStack()
    with gstack:
        P = Prog(nc, gstack)

        def sb(stack, name, shape, dt=F32):
            return stack.enter_context(nc.sbuf_tensor(name, shape, dt))

        def ps(stack, name, shape, dt=F32):
            return stack.enter_context(nc.psum_tensor(name, shape, dt))

        def make_ident(stk, tagp):
            identf = sb(stk, "identf" + tagp, [128, 128], F32)
            P.op("pool", I("memset", identf[:], 0.0), w=[("identf",)])
            P.op("pool", I("affine_select", out=identf[:], in_=identf[:], pattern=[[-1, 128]],
                                                   compare_op=ALU.not_equal, fill=1.0, base=0,
                                                   channel_multiplier=1), w=[("identf",)])
            return identf

        def load_T(src, row0, T, xt, tp, xT, identf, tagp):
            nt = (T + 127) // 128
            for t in range(nt):
                tt = min(128, T - 128 * t)
                r0 = row0 + 128 * t
                bi = load_T.cnt % 2
                load_T.cnt += 1
                P.dma("sp", xt[:tt, bi, :], src[r0:r0 + tt, :], w=[("xt", bi)])
                for cg in range(2):
                    bk = load_T.tcnt % 2
                    load_T.tcnt += 1
                    for j in range(4):
                        c = 4 * cg + j
                        P.op("pe", I("transpose",
                            tp[bk][:, j * 128:j * 128 + tt], xt[:tt, bi, c * 128:(c + 1) * 128], identf[:tt, :tt]),
                            r=[("xt", bi), ("identf",)], w=[("tp", bk)])
                    P.op("act", I("copy",
                        out=xT[:, 4 * cg:4 * cg + 4, t * 128:t * 128 + tt],
                        in_=tp[bk][:, :].rearrange("p (j n) -> p j n", j=4)[:, :, :tt]),
                        r=[("tp", bk)], w=[("xT", t)])
        load_T.cnt = 0
        load_T.tcnt = 0

        def ln_epilogue(T, nt, src, row0, dst, drow0, pd_groups, yb, gb, stats, mv, rstd, emit_mm):
            for t in range(nt):
                tt = min(128, T - 128 * t)
                P.dma("sp", yb[:tt, t, :], src[row0 + 128 * t:row0 + 128 * t + tt, :], w=[("yb", t)])
                for hf in range(2):
                    bk = ln_epilogue.cnt % 2
                    ln_epilogue.cnt += 1
                    pdt, pdtok = pd_groups[bk]
                    emit_mm(t, hf, pdt, pdtok, tt)
                    P.op("dve", I("scalar_tensor_tensor",
                        out=yb[:tt, t, hf * 512:(hf + 1) * 512], in0=yb[:tt, t, hf * 512:(hf + 1) * 512],
                        scalar=ALPHA, in1=pdt[:tt, :], op0=ALU.mult, op1=ALU.add),
                        r=[pdtok], w=[("yb", t)])
                    P.op("dve", I("bn_stats",
                        out=stats[:tt, t, hf, :], in_=yb[:tt, t, hf * 512:(hf + 1) * 512]),
                        r=[("yb", t)], w=[("stats", t)])
                P.op("dve", I("bn_aggr",
                    out=mv[:tt, t, :], in_=stats[:tt, t, :, :].rearrange("p a b -> p (a b)")),
                    r=[("stats", t)], w=[("mv",)])
            pp = min(128, T)
            P.op("act", I("activation", out=rstd[:pp, 0:nt], in_=mv[:pp, 0:nt, 1], func=AF.Sqrt,
                                               bias=epsb[:pp, :], scale=1.0),
                 r=[("mv",)], w=[("rstd",)])
            P.op("dve", I("reciprocal", out=rstd[:pp, 0:nt], in_=rstd[:pp, 0:nt]), w=[("rstd",)])
            for t in range(nt):
                tt = min(128, T - 128 * t)
                P.op("dve", I("tensor_scalar",
                    out=yb[:tt, t, :], in0=yb[:tt, t, :], scalar1=mv[:tt, t, 0:1], scalar2=rstd[:tt, t:t + 1],
                    op0=ALU.subtract, op1=ALU.mult), r=[("mv",), ("rstd",)], w=[("yb", t)])
                P.op("pool", I("tensor_tensor",
                    out=yb[:tt, t, :], in0=yb[:tt, t, :], in1=gb[:tt, 0, :], op=ALU.mult),
                    r=[("gb",)], w=[("yb", t)])
                P.op("pool", I("tensor_tensor",
                    out=yb[:tt, t, :], in0=yb[:tt, t, :], in1=gb[:tt, 1, :], op=ALU.add),
                    r=[("gb",)], w=[("yb", t)])
                for (dd, dr) in (dst if isinstance(dst, list) else [(dst, drow0)]):
                    P.dma("sp", dd[dr + 128 * t:dr + 128 * t + tt, :], yb[:tt, t, :], r=[("yb", t)])
        ln_epilogue.cnt = 0

        def load_gb(gb, grow):
            P.dma("sp", gb[:, 0, :], lnv[grow:grow + 1, :].partition_broadcast(128), w=[("gb",)])
            P.dma("sp", gb[:, 1, :], lnv[grow + 1:grow + 2, :].partition_broadcast(128), w=[("gb",)])

        epsb = sb(gstack, "epsb", [128, 1], F32)
        P.op("pool", I("memset", epsb[:], EPS), w=[("epsb",)])

        def ffn_phase(tagp, src, groups, dsts, wgd, wud, wdd, grow, extra=None):
            with ExitStack() as stk:
                wg = sb(stk, "wg" + tagp, [128, 8, FF], BF16)
                wu = sb(stk, "wu" + tagp, [128, 8, FF], BF16)
                wd = sb(stk, "wd" + tagp, [128, NK, D], BF16)
                xt = sb(stk, "xt" + tagp, [128, 2, D], F32)
                xT = sb(stk, "xT" + tagp, [128, 8, 512], BF16)
                hT = sb(stk, "hT" + tagp, [128, NK, 512], BF16)
                sg = sb(stk, "sg" + tagp, [128, 2, 512], F32)
                yb = sb(stk, "yb" + tagp, [128, 4, D], F32)
                gb = sb(stk, "gb" + tagp, [128, 2, D], F32)
                stats = sb(stk, "stats" + tagp, [128, 4, 2, 6], F32)
                mv = sb(stk, "mv" + tagp, [128, 4, 2], F32)
                rstd = sb(stk, "rstd" + tagp, [128, 4], F32)
                tp = [ps(stk, "tp%d%s" % (i, tagp), [128, 512], F32) for i in range(2)]
                pg = [ps(stk, "pg%d%s" % (i, tagp), [128, 512], F32) for i in range(2)]
                pu = [ps(stk, "pu%d%s" % (i, tagp), [128, 512], F32) for i in range(2)]
                pd = [ps(stk, "pd%d%s" % (i, tagp), [128, 512], F32) for i in range(2)]
                identf = make_ident(stk, tagp)
                wgv = wgd.rearrange("(c p) n -> p c n", p=128)
                wuv = wud.rearrange("(c p) n -> p c n", p=128)
                wdv = wdd.rearrange("(k p) n -> p k n", p=128)
                for i, (a, b) in enumerate(ffn_pieces()):
                    P.dma("pool", wg[:, :, a * 128:b * 128], wgv[:, :, a * 128:b * 128], w=[("wg", i)])
                    P.dma("pool", wu[:, :, a * 128:b * 128], wuv[:, :, a * 128:b * 128], w=[("wu", i)])
                for i, (a, b) in enumerate(ffn_pieces()):
                    P.dma("pool", wd[:, a:b, :], wdv[:, a:b, :], w=[("wd", i)])
                load_gb(gb, grow)

                def gate_up(T):
                    nt = (T + 127) // 128
                    xr = [("xT", t) for t in range(nt)]
                    for k in range(NK):
                        bk = k % 2
                        pc = piece_of(k)
                        for c in range(8):
                            P.op("pe", I("matmul",
                                pg[bk][:, :T], wg[:, c, k * 128:(k + 1) * 128], xT[:, c, :T],
                                start=(c == 0), stop=(c == 7)), r=xr + [("wg", pc)], w=[("pg", bk)])
                        for c in range(8):
                            P.op("pe", I("matmul",
                                pu[bk][:, :T], wu[:, c, k * 128:(k + 1) * 128], xT[:, c, :T],
                                start=(c == 0), stop=(c == 7)), r=xr + [("wu", pc)], w=[("pu", bk)])
                        P.op("act", I("activation", out=sg[:, bk, :T], in_=pg[bk][:, :T], func=AF.Silu),
                             r=[("pg", bk)], w=[("sg", bk)])
                        P.op("dve", I("scalar_tensor_tensor",
                            out=hT[:, k, :T], in0=sg[:, bk, :T], scalar=0.5, in1=pu[bk][:, :T],
                            op0=ALU.mult, op1=ALU.mult), r=[("sg", bk), ("pu", bk)], w=[("hT", k)])

                def down_mm(t, hf, pdt, pdtok, tt):
                    for k in range(NK):
                        P.op("pe", I("matmul",
                            pdt[:tt, :], hT[:, k, t * 128:t * 128 + tt], wd[:, k, hf * 512:(hf + 1) * 512],
                            start=(k == 0), stop=(k == NK - 1)),
                            r=[("hT", k), ("wd", piece_of(k))], w=[pdtok])

                load_T(src, groups[0][0], groups[0][1], xt, tp, xT, identf, tagp)
                for gi, (row0, T) in enumerate(groups):
                    nt = (T + 127) // 128
                    gate_up(T)
                    if gi + 1 < len(groups):
                        load_T(src, groups[gi + 1][0], groups[gi + 1][1], xt, tp, xT, identf, tagp)
                    if extra is not None:
                        extra(gi)
                    ln_epilogue(T, nt, src, row0, dsts(gi), None, [(pd[0], ("pd", 0)), (pd[1], ("pd", 1))], yb, gb, stats, mv, rstd, down_mm)
                P.flush("ffn" + tagp)

        groupsA = [(512 * g, 512) for g in range(8)] + [(2 * L, NS)]

        def cast_weights(gi):
            if gi == 0:
                for i in range(4):
                    P.dma("pool", winb[i * 256:(i + 1) * 256, :], win[i * 256:(i + 1) * 256, :])
                P.dma("pool", woutb[:, :], wout[:, :])
            if gi < 8:
                for b in (2 * gi, 2 * gi + 1):
                    if b >= NSQ:
                        continue
                    P.dma("act", nk_o[b, 0:2044, :], ck[b, 4:2048, :])
                    P.dma("act", nv_o[b, 0:2044, :], cv[b, 4:2048, :])

        ffn_phase("A", xin, groupsA, lambda gi: [(x1s, groupsA[gi][0])], w1g, w1u, w1d, 0, extra=cast_weights)

        winv = winb.rearrange("(c p) n -> p c n", p=128)
        if "B1" not in PHASES:
            return nc

        with ExitStack() as stk:
            xt = sb(stk, "xtB", [128, 2, D], F32)
            xT = sb(stk, "xTB", [128, 8, 512], BF16)
            wi = sb(stk, "wiB", [128, 2, 8, 512], BF16)
            kT = sb(stk, "kTB", [128, 4, 2 * L], BF16)
            qT = sb(stk, "qTB", [128, 4, 512], BF16)
            V16 = sb(stk, "V16", [128, 32, 8, 65], BF16)
            V4 = sb(stk, "V4", [128, 2, 4, 8, 65], BF16)
            V1 = sb(stk, "V1", [128, 2, 4, 8, 65], BF16)
            cs = sb(stk, "csB", [128, 2, 512], F32)
            t1 = sb(stk, "t1B", [128, 2, 512], F32)
            t2 = sb(stk, "t2B", [128, 2, 512], F32)
            kf = sb(stk, "kfB", [128, 2, 512], F32)
            vf = sb(stk, "vfB", [128, 2, 512], F32)
            pT = sb(stk, "pTB", [128, 4, 512], BF16)
            pE = sb(stk, "pEB", [128, 4, 512], BF16)
            oacc = sb(stk, "oaccB", [65, 2, 512], F32)
            rden = sb(stk, "rdenB", [65, 2, 512], F32)
            catA = sb(stk, "catAB", [64, 8, 512], BF16)
            cstb = sb(stk, "cstbB", [128, 448], BF16)
            onesf = sb(stk, "onesfB", [128, 64], F32)
            onesh = sb(stk, "oneshB", [128, 64], BF16)
            rhi = sb(stk, "rhiB", [65, 2, 512], BF16)
            rlo = sb(stk, "rloB", [65, 2, 512], BF16)
            identf = make_ident(stk, "B")
            tp = [ps(stk, "tpB%d" % i, [128, 512], F32) for i in range(2)]
            pz = [ps(stk, "pzB%d" % i, [128, 512], F32) for i in range(2)]
            psc = [ps(stk, "pscB%d" % i, [128, 512], F32) for i in range(2)]
            po = [ps(stk, "poB%d" % i, [128, 512], F32) for i in range(2)]

            P.dma("pool", cstb[:, :], cst[:, 0:448], w=[("cstb",)])
            P.op("pool", I("memset", onesf[:], 1.0), w=[("onesf",)])
            P.op("pool", I("memset", onesh[:], 1.0), w=[("onesf",)])
            P.op("pool", I("memset", V16[:, :, :, 64:65], 1.0), w=[("V16i",)])
            P.op("pool", I("memset", V4[:, :, :, :, 64:65], 1.0), w=[("V4i",)])
            P.op("pool", I("memset", V1[:, :, :, :, 64:65], 1.0), w=[("V1i",)])
            mcur = cstb[:, C_MCUR:C_MCUR + 128]
            mprev = cstb[:, C_MPREV:C_MPREV + 128]
            mprevh = cstb[:, C_MPREVH:C_MPREVH + 128]

            wcnt = [0]

            def load_wblock(col0):
                bi = wcnt[0] % 2
                wcnt[0] += 1
                P.dma("sp", wi[:, bi, :, :], winv[:, :, col0:col0 + 512], w=[("wi", bi)])
                return bi

            zc = [0]
            sc_cnt = [0]
            po_cnt = [0]
            pt_cnt = [0]

            for g in range(8):
                main = g >= 4
                gm = g - 4
                tok0 = 512 * g
                load_T(x1s, tok0, 512, xt, tp, xT, identf, "B")
                P.dma("sp", cs[:, 0, :], rope[0, :, tok0:tok0 + 512], w=[("cs",)])
                P.dma("sp", cs[:, 1, :], rope[1, :, tok0:tok0 + 512], w=[("cs",)])
                xr = [("xT", t) for t in range(4)]

                def rope_proj(col, colsw, dst_bf, dst_tok, dst_f32=None):
                    bi = load_wblock(col)
                    bs = load_wblock(colsw)
                    for cq in range(4):
                        za, zb = zc[0] % 2, (zc[0] + 1) % 2
                        zc[0] += 2
                        for (bw, zz) in ((bi, za), (bs, zb)):
                            for c in range(8):
                                P.op("pe", I("matmul",
                                    pz[zz][:, :], wi[:, bw, c, cq * 128:(cq + 1) * 128], xT[:, c, :],
                                    start=(c == 0), stop=(c == 7)), r=xr + [("wi", bw)], w=[("pz", zz)])
                        tb = cq % 2
                        P.op("dve", I("tensor_tensor",
                            out=t1[:, tb, :], in0=pz[za][:, :], in1=cs[:, 0, :], op=ALU.mult),
                            r=[("pz", za), ("cs",)], w=[("t1", tb)])
                        P.op("dve", I("tensor_tensor",
                            out=t2[:, tb, :], in0=pz[zb][:, :], in1=cs[:, 1, :], op=ALU.mult),
                            r=[("pz", zb), ("cs",)], w=[("t2", tb)])
                        if dst_f32 is not None:
                            P.op("pool", I("tensor_tensor",
                                out=kf[:, tb, :], in0=t1[:, tb, :], in1=t2[:, tb, :], op=ALU.add),
                                r=[("t1", tb), ("t2", tb)], w=[("kf", tb)])
                            P.op("act", I("copy", out=dst_bf(cq), in_=kf[:, tb, :]),
                                 r=[("kf", tb)], w=[dst_tok])
                            P.dma("sp", dst_f32(cq), kf[:, tb, :], r=[("kf", tb)])
                        else:
                            P.op("pool", I("tensor_tensor",
                                out=dst_bf(cq), in0=t1[:, tb, :], in1=t2[:, tb, :], op=ALU.add),
                                r=[("t1", tb), ("t2", tb)], w=[dst_tok])

                if main:
                    rope_proj(W_KA, W_KS, lambda cq: kT[:, cq, tok0:tok0 + 512], ("kT",),
                              dst_f32=lambda cq: kT_o[cq * 128:(cq + 1) * 128, gm * 512:(gm + 1) * 512])
                    rope_proj(W_QA, W_QS, lambda cq: qT[:, cq, :], ("qT",))
                else:
                    rope_proj(W_KA, W_KS, lambda cq: kT[:, cq, tok0:tok0 + 512], ("kT",))

                bv = load_wblock(W_VA)
                slot = g % 2

                def vmm(lhs_fn, M):
                    zz = zc[0] % 2
                    zc[0] += 1
                    for c in range(8):
                        P.op("pe", I("matmul",
                            pz[zz][:M, :], lhs_fn(c), wi[:, bv, c, :], start=(c == 0), stop=(c == 7)),
                            r=xr + [("wi", bv)], w=[("pz", zz)])
                    return zz

                for blk in range(4):
                    zz = vmm(lambda c, blk=blk: xT[:, c, blk * 128:(blk + 1) * 128], 128)
                    if main:
                        fb = blk % 2
                        P.op("dve", I("tensor_copy", out=vf[:, fb, :], in_=pz[zz][:, :]),
                             r=[("pz", zz)], w=[("vf", fb)])
                        P.op("pool", I("tensor_copy",
                            out=V1[:, slot, blk, :, 0:64], in_=vf[:, fb, :].rearrange("p (h e) -> p h e", h=8)),
                            r=[("vf", fb), ("V1i",)], w=[("V",)])
                        P.dma("sp", v_o[gm * 512 + blk * 128:gm * 512 + (blk + 1) * 128, :], vf[:, fb, :],
                              r=[("vf", fb)])
                    else:
                        P.op("act", I("copy",
                            out=V1[:, slot, blk, :, 0:64], in_=pz[zz][:, :].rearrange("p (h e) -> p h e", h=8)),
                            r=[("pz", zz), ("V1i",)], w=[("V",)])
                span = g // 4
                gq = g % 4
                P.dma("sp", vscr[tok0:tok0 + 512, :].rearrange("(b p) n -> p b n", p=128),
                      V1[:, slot, :, :, :].rearrange("p b h e -> p b (h e)"), r=[("V",)], w=[("vscr",)])
                if g >= 3:
                    P.dma("sp", V4[:, slot, :, :, :].rearrange("p r h e -> p r (h e)"),
                          vscr[tok0:tok0 + 512, :].rearrange("(i r) n -> i r n", r=4), r=[("vscr",)], w=[("V",)])
                P.dma("sp", V16[32 * gq:32 * gq + 32, span * 16:(span + 1) * 16, :, :].rearrange("p r h e -> p r (h e)"),
                      vscr[tok0:tok0 + 512, :].rearrange("(i r) n -> i r n", r=16), r=[("vscr",)], w=[("V",)])
                import os as _os
                if not main or _os.environ.get("KNOATT"):
                    continue

                jobs = []
                m_cp = cstb[:, 0:256].rearrange("p (t k) -> p t k", t=2)
                m_cph = cstb[:, 0:384].rearrange("p (t k) -> p t k", t=3)[:, 0:3:2, :]

                def make_job(pairs, mask_ops, pv_list, tail):
                    st = {}

                    def S_stage():
                        sb_ = sc_cnt[0] % 2
                        sc_cnt[0] += 1
                        col = 0
                        for (k_ap, M, q_ap, N) in pairs:
                            P.op("pe", I("matmul", psc[sb_][:M, col:col + N], k_ap, q_ap, start=True, stop=True),
                                 r=[("kT",), ("qT",)], w=[("psc", sb_)])
                            col += N
                        pi = pt_cnt[0] % 4
                        pt_cnt[0] += 1
                        st["pi"] = pi
                        P.op("act", I("activation", out=pE[:, pi, :col], in_=psc[sb_][:, :col], func=AF.Exp, scale=0.125),
                             r=[("psc", sb_)], w=[("pE", pi)])
                        for (rows, c0, ncol, view, m_ap) in mask_ops:
                            P.op("pool", I("tensor_tensor", out=view(pT[:rows, pi, c0:c0 + ncol]),
                                           in0=view(pE[:rows, pi, c0:c0 + ncol]), in1=m_ap, op=ALU.mult),
                                 r=[("pE", pi), ("cstb",)], w=[("pT", pi)])

                    def PV_stage():
                        pi = st["pi"]
                        for (o_ap, v_ap, rows, c0, n, st_, sp_, potok) in pv_list:
                            P.op("pe", I("matmul", o_ap, v_ap, pT[:rows, pi, c0:c0 + n], start=st_, stop=sp_),
                                 r=[("pT", pi), ("V",)], w=[potok])
                        if tail is not None:
                            tail()
                    jobs.append((S_stage, PV_stage))

                for h in range(8):
                    c4, pb = h // 2, (h % 2) * 64
                    ob = h % 2
                    qh = qT[pb:pb + 64, c4, :]
                    kh = kT[pb:pb + 64, c4, :]
                    v22 = lambda a: a.rearrange("p (q t k) -> p q t k", q=2, t=2)
                    pob = po_cnt[0] % 2
                    po_cnt[0] += 1
                    for half in range(2):
                        pairs, pv = [], []
                        for qi_, qb in enumerate((2 * half, 2 * half + 1)):
                            tq = tok0 + 128 * qb
                            qa = qh[:, qb * 128:(qb + 1) * 128]
                            vprev = V1[:, 1 - slot, 3, h, :] if qb == 0 else V1[:, slot, qb - 1, h, :]
                            pairs.append((kh[:, tq:tq + 128], 128, qa, 128))
                            pairs.append((kh[:, tq - 128:tq], 128, qa, 128))
                            oq = po[pob][:65, qb * 128:(qb + 1) * 128]
                            pv.append((oq, V1[:, slot, qb, h, :], 128, 256 * qi_, 128, True, False, ("po", pob)))
                            pv.append((oq, vprev, 128, 256 * qi_ + 128, 128, False, True, ("po", pob)))
                        if gm == 0 and half == 0:
                            v12 = lambda a: a.rearrange("p (t k) -> p t k", t=2)
                            mops = [(128, 0, 256, v12, m_cph), (128, 256, 256, v12, m_cp)]
                        else:
                            mops = [(128, 0, 512, v22, m_cp.unsqueeze(1).broadcast_to([128, 2, 2, 128]))]
                        tail = None
                        if half == 1:
                            def tail(pob=pob, ob=ob):
                                P.op("act", I("copy", out=oacc[:, ob, :], in_=po[pob][:65, :]),
                                     r=[("po", pob)], w=[("oacc", ob)])
                        make_job(pairs, mops, pv, tail)
                    pob = po_cnt[0] % 2
                    po_cnt[0] += 1
                    for half in range(2):
                        pairs, pv = [], []
                        for qi_, r4 in enumerate((2 * half, 2 * half + 1)):
                            qa = qh[:, r4:512:4]
                            pairs.append((kh[:, tok0 + r4:tok0 + 512:4], 128, qa, 128))
                            pairs.append((kh[:, tok0 - 512 + r4:tok0:4], 128, qa, 128))
                            oq = po[pob][:65, r4 * 128:(r4 + 1) * 128]
                            pv.append((oq, V4[:, slot, r4, h, :], 128, 256 * qi_, 128, True, False, ("po", pob)))
                            pv.append((oq, V4[:, 1 - slot, r4, h, :], 128, 256 * qi_ + 128, 128, False, True, ("po", pob)))
                        mm = m_cph if gm == 0 else m_cp
                        mops = [(128, 0, 512, v22, mm.unsqueeze(1).broadcast_to([128, 2, 2, 128]))]
                        tail = None
                        if half == 1:
                            def tail(pob=pob, ob=ob):
                                P.op("dve", I("tensor_tensor",
                                    out=oacc[:, ob, :].rearrange("p (i r) -> p r i", r=4),
                                    in0=oacc[:, ob, :].rearrange("p (i r) -> p r i", r=4),
                                    in1=po[pob][:65, :].rearrange("p (r i) -> p r i", r=4), op=ALU.add),
                                    r=[("po", pob)], w=[("oacc", ob)])
                        make_job(pairs, mops, pv, tail)
                    pob = po_cnt[0] % 2
                    po_cnt[0] += 1
                    Mc = 32 * (gm + 1)
                    v8 = lambda a: a.rearrange("p (r i) -> p r i", r=8)
                    for quarter in range(2):
                        pairs, pv = [], []
                        for j in range(8):
                            r16 = 8 * quarter + j
                            pairs.append((kh[:, r16:L:16], 128, qh[:, r16:512:16], 32))
                        for j in range(8):
                            r16 = 8 * quarter + j
                            pairs.append((kh[:, L + r16:L + r16 + 16 * (Mc - 1) + 1:16], Mc, qh[:, r16:512:16], 32))
                        for j in range(8):
                            r16 = 8 * quarter + j
                            oq = po[pob][:65, r16 * 32:(r16 + 1) * 32]
                            pv.append((oq, V16[:, r16, h, :], 128, 32 * j, 32, True, False, ("po", pob)))
                            pv.append((oq, V16[:Mc, 16 + r16, h, :], Mc, 256 + 32 * j, 32, False, True, ("po", pob)))
                        mops = [(128, 0, 256, v8, mprevh[:, 32 * gm:32 * gm + 32].unsqueeze(1).broadcast_to([128, 8, 32])),
                                (Mc, 256, 256, v8, mcur[:Mc, 32 * gm:32 * gm + 32].unsqueeze(1).broadcast_to([Mc, 8, 32]))]
                        tail = None
                        if quarter == 1:
                            def tail(pob=pob, ob=ob, h=h):
                                P.op("dve", I("tensor_tensor",
                                    out=oacc[:, ob, :].rearrange("p (i r) -> p r i", r=16),
                                    in0=oacc[:, ob, :].rearrange("p (i r) -> p r i", r=16),
                                    in1=po[pob][:65, :].rearrange("p (r i) -> p r i", r=16), op=ALU.add),
                                    r=[("po", pob)], w=[("oacc", ob)])
                                P.op("dve", I("reciprocal", out=rden[64:65, ob, :], in_=oacc[64:65, ob, :]),
                                     r=[("oacc", ob)], w=[("rden", ob)])
                                zz = zc[0] % 2
                                zc[0] += 1
                                P.op("dve", I("tensor_copy", out=rhi[64:65, ob, :], in_=rden[64:65, ob, :]),
                                     r=[("rden", ob)], w=[("rhi", ob)])
                                P.op("dve", I("tensor_tensor", out=rlo[64:65, ob, :], in0=rden[64:65, ob, :],
                                              in1=rhi[64:65, ob, :], op=ALU.subtract),
                                     r=[("rden", ob), ("rhi", ob)], w=[("rlo", ob)])
                                P.op("pe", I("matmul", pz[zz][:64, :], onesh[64:65, 0:64], rhi[64:65, ob, :],
                                             start=True, stop=False), r=[("rhi", ob), ("onesf",)], w=[("pz", zz)])
                                P.op("pe", I("matmul", pz[zz][:64, :], onesh[64:65, 0:64], rlo[64:65, ob, :],
                                             start=False, stop=True), r=[("rlo", ob), ("onesf",)], w=[("pz", zz)])
                                P.op("dve", I("tensor_tensor", out=catA[:, h, :], in0=oacc[0:64, ob, :],
                                              in1=pz[zz][:64, :], op=ALU.mult),
                                     r=[("pz", zz), ("oacc", ob)], w=[("catA",)])
                        make_job(pairs, mops, pv, tail)
                prev_pv = None
                for (S_stage, PV_stage) in jobs:
                    S_stage()
                    if prev_pv is not None:
                        prev_pv()
                    prev_pv = PV_stage
                prev_pv()
                P.dma("sp", cats[:, :, gm * 512:(gm + 1) * 512], catA[:, :, :], r=[("catA",)])
            P.flush("B1")

        if "B2" not in PHASES:
            P.flush() if any(P.ops[e] for e in ENGS) else None
            return nc
        woAv = woutb[0:512, :].rearrange("(h p) n -> p h n", p=64)
        woHv = woutb[512:1024, :].rearrange("(k p) n -> p k n", p=128)

        def hgrn_common(stk, tagp, T):
            d = {}
            d["kk"] = sb(stk, "kk" + tagp, [128, 4, T], F32)
            d["ecum"] = sb(stk, "ecum" + tagp, [128, 4, T], F32)
            d["tmpa"] = sb(stk, "tmpa" + tagp, [128, 2, T], F32)
            d["tmpb"] = sb(stk, "tmpb" + tagp, [128, 2, T], F32)
            d["tmpc"] = sb(stk, "tmpc" + tagp, [128, 2, T], F32)
            d["kdec"] = sb(stk, "kdec" + tagp, [128, 4, T], BF16)
            d["kendT"] = sb(stk, "kendT" + tagp, [128, 4, T], BF16)
            d["qdec"] = sb(stk, "qdec" + tagp, [128, 4, T], BF16)
            d["gs"] = sb(stk, "gs" + tagp, [128, 4, T], BF16)
            d["osb"] = sb(stk, "osb" + tagp, [128, 2, T], F32)
            d["osq"] = sb(stk, "osq" + tagp, [128, 2, T], BF16)
            d["catH"] = sb(stk, "catH" + tagp, [128, 4, T], BF16)
            d["oml"] = sb(stk, "oml" + tagp, [128, 2, 4], F32)
            d["onesb"] = sb(stk, "onesb" + tagp, [128, 128], BF16)
            d["cstf"] = sb(stk, "cstf" + tagp, [128, NCST], F32)
            d["cstb"] = sb(stk, "cstbh" + tagp, [128, NCST], BF16)
            d["identb"] = sb(stk, "identb" + tagp, [128, 128], BF16)
            cstf, oml = d["cstf"], d["oml"]
            P.dma("sp", cstf[:, :], cst[:, :], w=[("cstf",)])
            P.dma("pool", d["cstb"][:, :], cst[:, :], w=[("cstbh",)])
            P.op("pool", I("memset", d["onesb"][:], 1.0), w=[("onesb",)])
            P.op("dve", I("tensor_tensor", out=oml[:, 0, :], in0=cstf[:, C_LB + 4:C_LB + 8],
                                                  in1=cstf[:, C_LB:C_LB + 4], op=ALU.subtract),
                 r=[("cstf",)], w=[("oml",)])
            P.op("act", I("activation", out=oml[:, 0, :], in_=oml[:, 0, :], func=AF.Sigmoid), w=[("oml",)])
            P.op("dve", I("tensor_scalar", out=oml[:, 1, :], in0=oml[:, 0, :],
                                                  scalar1=cstf[:, C_FLAG:C_FLAG + 1], scalar2=None, op0=ALU.mult),
                 r=[("cstf",)], w=[("oml",)])
            return d

        def hgrn_gates(d, T, xT, xr, wi, load_wblock, pz, zc, halo, main, rmask_ap, chunk):
            kk, ecum, tmpa, tmpb, tmpc = d["kk"], d["ecum"], d["tmpa"], d["tmpb"], d["tmpc"]
            kdec, kendT, qdec, gs, oml = d["kdec"], d["kendT"], d["qdec"], d["gs"], d["oml"]
            nch = T // chunk

            def proj(bw, hh):
                zz = zc[0] % 2
                zc[0] += 1
                for c in range(8):
                    P.op("pe", I("matmul",
                        pz[zz][:, :T], wi[:, bw, c, hh * 128:(hh + 1) * 128], xT[:, c, :T],
                        start=(c == 0), stop=(c == 7)), r=xr + [("wi", bw)], w=[("pz", zz)])
                return zz

            bw = load_wblock(W_FH)
            for hh in range(4):
                zz = proj(bw, hh)
                tb = hh % 2
                P.op("act", I("activation", out=tmpa[:, tb, :], in_=pz[zz][:, :T],
                                                                  func=AF.Sigmoid, scale=-1.0),
                     r=[("pz", zz)], w=[("tmpa", tb)])
                P.op("dve", I("tensor_scalar",
                    out=kk[:, hh, :], in0=tmpa[:, tb, :], scalar1=oml[:, 1 if halo else 0, hh:hh + 1],
                    scalar2=None, op0=ALU.mult), r=[("tmpa", tb), ("oml",)], w=[("kk", hh)])
                P.op("act", I("activation", out=tmpb[:, tb, :], in_=kk[:, hh, :], func=AF.Ln,
                                                                  scale=-1.0, bias=oneb[:, :]),
                     r=[("kk", hh)], w=[("tmpb", tb)])
                P.op("dve", I("tensor_tensor_scan",
                    out=tmpc[:, tb, :], data0=rmask_ap, data1=tmpb[:, tb, :], initial=0.0,
                    op0=ALU.mult, op1=ALU.add), r=[("tmpb", tb), ("cstf",)], w=[("tmpc", tb)])
                P.op("act", I("activation", out=ecum[:, hh, :], in_=tmpc[:, tb, :], func=AF.Exp),
                     r=[("tmpc", tb)], w=[("ecum", hh)])
                P.op("act", I("activation", out=tmpa[:, tb, :], in_=tmpc[:, tb, :], func=AF.Exp, scale=-1.0),
                     r=[("tmpc", tb)], w=[("tmpa", tb)])
                P.op("dve", I("tensor_tensor",
                    out=kdec[:, hh, :], in0=kk[:, hh, :], in1=tmpa[:, tb, :], op=ALU.mult),
                    r=[("kk", hh), ("tmpa", tb)], w=[("kdec", hh)])
                P.op("dve", I("tensor_tensor",
                    out=kendT[:, hh, :].rearrange("p (n c) -> p n c", c=chunk),
                    in0=kdec[:, hh, :].rearrange("p (n c) -> p n c", c=chunk),
                    in1=ecum[:, hh, chunk - 1:T:chunk].unsqueeze(2).broadcast_to([128, nch, chunk]), op=ALU.mult),
                    r=[("kdec", hh), ("ecum", hh)], w=[("kendT", hh)])
            if main:
                bw = load_wblock(W_QH)
                for hh in range(4):
                    zz = proj(bw, hh)
                    tb = hh % 2
                    P.op("act", I("activation", out=tmpb[:, tb, :], in_=pz[zz][:, :T], func=AF.Silu),
                         r=[("pz", zz)], w=[("tmpb", tb)])
                    P.op("dve", I("tensor_tensor",
                        out=qdec[:, hh, :], in0=tmpb[:, tb, :], in1=ecum[:, hh, :], op=ALU.mult),
                        r=[("tmpb", tb), ("ecum", hh)], w=[("qdec", hh)])
                bw = load_wblock(W_GH)
                for hh in range(4):
                    zz = proj(bw, hh)
                    P.op("act", I("activation", out=gs[:, hh, :], in_=pz[zz][:, :T], func=AF.Silu),
                         r=[("pz", zz)], w=[("gs", hh)])

        def hgrn_out(d, T, hh, o_psum, o_tok, pz, zc):
            osb, osq, catH, gs, cstf, onesb, tmpa = d["osb"], d["osq"], d["catH"], d["gs"], d["cstf"], d["onesb"], d["tmpa"]
            ob = hh % 2
            P.op("dve", I("tensor_copy", out=osb[:, ob, :], in_=o_psum), r=[o_tok], w=[("osb", ob)])
            P.op("pool", I("tensor_tensor", out=osq[:, ob, :], in0=osb[:, ob, :], in1=osb[:, ob, :], op=ALU.mult),
                 r=[("osb", ob)], w=[("osq", ob)])
            zz = zc[0] % 2
            zc[0] += 1
            P.op("pe", I("matmul", pz[zz][:, :T], onesb[:, :], osq[:, ob, :], start=True, stop=True),
                 r=[("osq", ob), ("onesb",)], w=[("pz", zz)])
            P.op("act", I("activation", out=tmpa[:, ob, :], in_=pz[zz][:, :T], func=AF.Sqrt,
                                               scale=1.0 / 128.0, bias=epsb[:, :]),
                 r=[("pz", zz)], w=[("tmpa", ob)])
            P.op("dve", I("reciprocal", out=tmpa[:, ob, :], in_=tmpa[:, ob, :]), w=[("tmpa", ob)])
            P.op("dve", I("tensor_tensor", out=osb[:, ob, :], in0=osb[:, ob, :], in1=tmpa[:, ob, :], op=ALU.mult),
                 r=[("tmpa", ob)], w=[("osb", ob)])
            P.op("dve", I("scalar_tensor_tensor",
                out=catH[:, hh, :], in0=osb[:, ob, :], scalar=cstf[:, C_GN + hh:C_GN + hh + 1], in1=gs[:, hh, :],
                op0=ALU.mult, op1=ALU.mult), r=[("osb", ob), ("gs", hh), ("cstf",)], w=[("catH",)])

        oneb = sb(gstack, "oneb", [128, 1], F32)
        P.op("pool", I("memset", oneb[:], 1.0), w=[("oneb",)])

        with ExitStack() as stk:
            xt = sb(stk, "xtC", [128, 2, D], F32)
            xT = sb(stk, "xTC", [128, 8, 512], BF16)
            wi = sb(stk, "wiC", [128, 2, 8, 512], BF16)
            d = hgrn_common(stk, "C", 512)
            kend_tm = sb(stk, "kendtm", [128, 4, 512], BF16)
            vh_tm = sb(stk, "vhtm", [128, 4, 512], BF16)
            S = sb(stk, "Sst", [128, 4, 128], F32)
            Sbf = sb(stk, "Sbf", [128, 4, 8, 128], BF16)
            ATm = sb(stk, "ATm", [128, 4, 4, 64], BF16)
            catAs = sb(stk, "catAs", [64, 8, 512], BF16)
            woA = sb(stk, "woA", [64, 8, D], BF16)
            woH = sb(stk, "woH", [128, 4, D], BF16)
            yb = sb(stk, "ybC", [128, 4, D], F32)
            gb = sb(stk, "gbC", [128, 2, D], F32)
            stats = sb(stk, "statsC", [128, 4, 2, 6], F32)
            mv = sb(stk, "mvC", [128, 4, 2], F32)
            rstd = sb(stk, "rstdC", [128, 4], F32)
            identf = make_ident(stk, "C")
            identb = d["identb"]
            tp = [ps(stk, "tpC%d" % i, [128, 512], F32) for i in range(2)]
            tpb = ps(stk, "tpbC", [128, 1024], BF16)
            pz = [ps(stk, "pzC%d" % i, [128, 512], F32) for i in range(2)]
            pub = ps(stk, "pubC", [128, 512], F32)
            pa = ps(stk, "paC", [128, 512], F32)
            pob = ps(stk, "pobC", [128, 512], F32)
            P.op("act", I("copy", out=identb[:], in_=identf[:]), r=[("identf",)], w=[("identb",)])
            P.op("pool", I("memset", S[:], 0.0), w=[("S",)])
            P.dma("sp", woA[:, :, :], woAv, w=[("woA",)])
            P.dma("sp", woH[:, :, :], woHv, w=[("woH",)])
            load_gb(gb, 2)
            cstf, cstb = d["cstf"], d["cstb"]
            wcnt = [0]

            def load_wblock(col0):
                bi = wcnt[0] % 2
                wcnt[0] += 1
                P.dma("sp", wi[:, bi, :, :], winv[:, :, col0:col0 + 512], w=[("wi", bi)])
                return bi

            zc = [0]
            for g in range(8):
                main = g >= 4
                gm = g - 4
                tok0 = 512 * g
                load_T(x1s, tok0, 512, xt, tp, xT, identf, "C")
                xr = [("xT", t) for t in range(4)]
                hgrn_gates(d, 512, xT, xr, wi, load_wblock, pz, zc, not main, main,
                           cstf[:, C_RMASK:C_RMASK + 512], 64)
                kendT, kdec, qdec, ecum = d["kendT"], d["kdec"], d["qdec"], d["ecum"]
                bw = load_wblock(W_IH)
                for t in range(4):
                    zz = zc[0] % 2
                    zc[0] += 1
                    for c in range(8):
                        P.op("pe", I("matmul",
                            pz[zz][:, :], xT[:, c, t * 128:(t + 1) * 128], wi[:, bw, c, :],
                            start=(c == 0), stop=(c == 7)), r=xr + [("wi", bw)], w=[("pz", zz)])
                    P.op("act", I("copy", out=vh_tm[:, t, :], in_=pz[zz][:, :]),
                         r=[("pz", zz)], w=[("vhtm",)])
                for t in range(4):
                    for hh in range(4):
                        P.op("pe", I("transpose",
                            tpb[:, hh * 128:(hh + 1) * 128], kendT[:, hh, t * 128:(t + 1) * 128], identb[:, :]),
                            r=[("kendT", hh), ("identb",)], w=[("tpb",)])
                    P.op("act", I("copy", out=kend_tm[:, t, :], in_=tpb[:, 0:512]), r=[("tpb",)], w=[("kendtm",)])
                for ch in range(8):
                    t, hf = ch // 2, ch % 2
                    pr = slice(64 * hf, 64 * hf + 64)
                    for hh in range(4):
                        P.op("pe", I("matmul",
                            pub[:, hh * 128:(hh + 1) * 128], kend_tm[pr, t, hh * 128:(hh + 1) * 128],
                            vh_tm[pr, t, hh * 128:(hh + 1) * 128], start=True, stop=True),
                            r=[("kendtm",), ("vhtm",)], w=[("pub",)])
                    if main:
                        P.op("pool", I("tensor_copy", out=Sbf[:, :, ch, :], in_=S[:, :, :]),
                             r=[("S",)], w=[("Sbf",)])
                    for hh in range(4):
                        P.op("dve", I("scalar_tensor_tensor",
                            out=S[:, hh, :], in0=S[:, hh, :], scalar=ecum[:, hh, ch * 64 + 63:ch * 64 + 64],
                            in1=pub[:, hh * 128:(hh + 1) * 128], op0=ALU.mult, op1=ALU.add),
                            r=[("pub",), ("ecum", hh)], w=[("S",)])
                if not main:
                    continue
                for t in range(4):
                    for hf in range(2):
                        ch = 2 * t + hf
                        pr = slice(64 * hf, 64 * hf + 64)
                        for hh in range(4):
                            P.op("pe", I("matmul",
                                pa[pr, hh * 64:(hh + 1) * 64], kdec[:, hh, ch * 64:(ch + 1) * 64],
                                qdec[:, hh, ch * 64:(ch + 1) * 64], start=True, stop=True),
                                r=[("kdec", hh), ("qdec", hh)], w=[("pa",)])
                    P.op("dve", I("tensor_tensor",
                        out=ATm[:, t, :, :], in0=pa[:, 0:256].rearrange("p (h i) -> p h i", h=4),
                        in1=cstb[:, C_M64:C_M64 + 64].unsqueeze(1).broadcast_to([128, 4, 64]), op=ALU.mult),
                        r=[("pa",), ("cstbh",)], w=[("ATm",)])
                for hh in range(4):
                    for ch in range(8):
                        t, hf = ch // 2, ch % 2
                        pr = slice(64 * hf, 64 * hf + 64)
                        P.op("pe", I("matmul",
                            pob[:, ch * 64:(ch + 1) * 64], Sbf[:, hh, ch, :], qdec[:, hh, ch * 64:(ch + 1) * 64],
                            start=True, stop=False), r=[("Sbf",), ("qdec", hh)], w=[("pob",)])
                        P.op("pe", I("matmul",
                            pob[:, ch * 64:(ch + 1) * 64], vh_tm[pr, t, hh * 128:(hh + 1) * 128], ATm[pr, t, hh, :],
                            start=False, stop=True), r=[("vhtm",), ("ATm",)], w=[("pob",)])
                    hgrn_out(d, 512, hh, pob[:, :], ("pob",), pz, zc)
                P.dma("sp", catAs[:, :, :], cats[:, :, gm * 512:(gm + 1) * 512], w=[("catAs",)])
                catH = d["catH"]

                def wout_mm(t, hf, pdt, pdtok, tt):
                    for h in range(8):
                        P.op("pe", I("matmul",
                            pdt[:tt, :], catAs[:, h, t * 128:t * 128 + tt], woA[:, h, hf * 512:(hf + 1) * 512],
                            start=(h == 0), stop=False), r=[("catAs",), ("woA",)], w=[pdtok])
                    for hh in range(4):
                        P.op("pe", I("matmul",
                            pdt[:tt, :], catH[:, hh, t * 128:t * 128 + tt], woH[:, hh, hf * 512:(hf + 1) * 512],
                            start=False, stop=(hh == 3)), r=[("catH",), ("woH",)], w=[pdtok])

                ln_epilogue(512, 4, x1s, tok0, [(x2s, gm * 512)], None, [(pz[0], ("pz", 0)), (pz[1], ("pz", 1))],
                            yb, gb, stats, mv, rstd, wout_mm)
            P.dma("sp", s_o[:, :, :], S[:, :, :], r=[("S",)])
            P.flush("B2")
        if "B3" not in PHASES:
            return nc
        with ExitStack() as stk:
            T = NS
            xt = sb(stk, "xtS", [128, 2, D], F32)
            xT = sb(stk, "xTS", [128, 8, T], BF16)
            wi = sb(stk, "wiS", [128, 2, 8, 512], BF16)
            d = hgrn_common(stk, "S", T)
            cs = sb(stk, "csS", [128, 2, T], F32)
            t1 = sb(stk, "t1S", [128, 2, T], F32)
            t2 = sb(stk, "t2S", [128, 2, T], F32)
            kfs = sb(stk, "kfsS", [128, 4, T], F32)
            kTs = sb(stk, "kTsS", [128, 4, T], BF16)
            qTs = sb(stk, "qTsS", [128, 4, T], BF16)
            knew = sb(stk, "knewS", [T, 512], F32)
            vnew = sb(stk, "vnewS", [T, 512], F32)
            vnb = sb(stk, "vnbS", [4, 2, 8, 65], BF16)
            kc = sb(stk, "kcS", [128, 2, 7, 512], F32)
            vc = sb(stk, "vcS", [128, 1, 7, 512], F32)
            kcT = sb(stk, "kcTS", [128, 4, 7 * 128], BF16)
            vcb = sb(stk, "vcbS", [128, 7, 8, 65], BF16)
            pexp = sb(stk, "pexpS", [128, 7, 8, 4], BF16)
            pTs = sb(stk, "pTsS", [128, 7, 8, 4], BF16)
            pexn = sb(stk, "pexnS", [4, 8, 4], BF16)
            pTn = sb(stk, "pTnS", [4, 8, 4], BF16)
            oS = sb(stk, "oSS", [65, 16, 32], F32)
            rdn = sb(stk, "rdnS", [65, 512], F32)
            onesf = sb(stk, "onesfS", [128, 64], F32)
            oneshS = sb(stk, "oneshS", [128, 64], BF16)
            rhiS = sb(stk, "rhiS", [65, 512], BF16)
            rloS = sb(stk, "rloS", [65, 512], BF16)
            catAs = sb(stk, "catAsS", [64, 8, T], BF16)
            iT = sb(stk, "iTS", [128, 4, T], BF16)
            S0f = sb(stk, "S0fS", [128, 64, 128], F32)
            S0b = sb(stk, "S0bS", [128, 2, 4, 128], BF16)
            ksvs = sb(stk, "ksvsS", [4, 2, 8, 128], BF16)
            ATs = sb(stk, "ATsS", [4, 64, 4], BF16)
            woA = sb(stk, "woAS", [64, 8, D], BF16)
            woH = sb(stk, "woHS", [128, 4, D], BF16)
            yb = sb(stk, "ybS", [128, 1, D], F32)
            gb = sb(stk, "gbS", [128, 2, D], F32)
            stats = sb(stk, "statsS", [128, 1, 2, 6], F32)
            mv = sb(stk, "mvS", [128, 1, 2], F32)
            rstd = sb(stk, "rstdS", [128, 1], F32)
            identf = make_ident(stk, "S")
            identb = d["identb"]
            cstf, cstb = d["cstf"], d["cstb"]
            tp = [ps(stk, "tpS%d" % i, [128, 512], F32) for i in range(2)]
            tpb = ps(stk, "tpbS", [128, 1024], BF16)
            pz = [ps(stk, "pzS%d" % i, [128, 512], F32) for i in range(2)]
            psc = ps(stk, "pscS", [128, 512], F32)
            pos = ps(stk, "posS", [128, 512], F32)
            pub = ps(stk, "pubS", [128, 512], F32)
            P.op("act", I("copy", out=identb[:], in_=identf[:]), r=[("identf",)], w=[("identb",)])
            P.op("pool", I("memset", onesf[:], 1.0), w=[("onesf",)])
            P.op("pool", I("memset", oneshS[:], 1.0), w=[("onesf",)])
            P.op("pool", I("memset", vcb[:, :, :, 64:65], 1.0), w=[("vcbi",)])
            P.op("pool", I("memset", vnb[:, :, :, 64:65], 1.0), w=[("vnbi",)])
            P.dma("sp", woA[:, :, :], woAv, w=[("woA",)])
            P.dma("sp", woH[:, :, :], woHv, w=[("woH",)])
            load_gb(gb, 2)
            stv = st.rearrange("(bh k) v -> k bh v", k=128)
            for i in range(4):
                P.dma("sp", S0f[:, 16 * i:16 * i + 16, :], stv[:, 16 * i:16 * i + 16, :], w=[("S0f", i)])
            P.dma("sp", cs[:, 0, :], rope[0, :, 2 * L:2 * L + T], w=[("cs",)])
            P.dma("sp", cs[:, 1, :], rope[1, :, 2 * L:2 * L + T], w=[("cs",)])
            wcnt = [0]

            def load_wblock(col0):
                bi = wcnt[0] % 2
                wcnt[0] += 1
                P.dma("sp", wi[:, bi, :, :], winv[:, :, col0:col0 + 512], w=[("wi", bi)])
                return bi

            zc = [0]
            load_T(x1s, 2 * L, T, xt, tp, xT, identf, "S")
            xr = [("xT", 0)]

            def rope_proj_s(col, colsw, dst_bf, dst_tok, f32dst):
                bi = load_wblock(col)
                bs = load_wblock(colsw)
                for cq in range(4):
                    za, zb = zc[0] % 2, (zc[0] + 1) % 2
                    zc[0] += 2
                    for (bw, zz) in ((bi, za), (bs, zb)):
                        for c in range(8):
                            P.op("pe", I("matmul",
                                pz[zz][:, :T], wi[:, bw, c, cq * 128:(cq + 1) * 128], xT[:, c, :],
                                start=(c == 0), stop=(c == 7)), r=xr + [("wi", bw)], w=[("pz", zz)])
                    tb = cq % 2
                    P.op("dve", I("tensor_tensor",
                        out=t1[:, tb, :], in0=pz[za][:, :T], in1=cs[:, 0, :], op=ALU.mult),
                        r=[("pz", za), ("cs",)], w=[("t1", tb)])
                    P.op("dve", I("tensor_tensor",
                        out=t2[:, tb, :], in0=pz[zb][:, :T], in1=cs[:, 1, :], op=ALU.mult),
                        r=[("pz", zb), ("cs",)], w=[("t2", tb)])
                    if f32dst is not None:
                        P.op("pool", I("tensor_tensor",
                            out=f32dst[:, cq, :], in0=t1[:, tb, :], in1=t2[:, tb, :], op=ALU.add),
                            r=[("t1", tb), ("t2", tb)], w=[("kfs",)])
                        P.op("act", I("copy", out=dst_bf[:, cq, :], in_=f32dst[:, cq, :]),
                             r=[("kfs",)], w=[dst_tok])
                    else:
                        P.op("pool", I("tensor_tensor",
                            out=dst_bf[:, cq, :], in0=t1[:, tb, :], in1=t2[:, tb, :], op=ALU.add),
                            r=[("t1", tb), ("t2", tb)], w=[dst_tok])

            rope_proj_s(W_KA, W_KS, kTs, ("kTs",), kfs)
            rope_proj_s(W_QA, W_QS, qTs, ("qTs",), None)
            for cq in range(4):
                P.op("pe", I("transpose", tp[0][:T, cq * 128:(cq + 1) * 128], kfs[:, cq, :], identf[:, :]),
                     r=[("kfs",), ("identf",)], w=[("tp", 0)])
            P.op("act", I("copy", out=knew[:, :], in_=tp[0][:T, :]), r=[("tp", 0)], w=[("knew",)])
            bv = load_wblock(W_VA)
            zz = zc[0] % 2
            zc[0] += 1
            for c in range(8):
                P.op("pe", I("matmul", pz[zz][:T, :], xT[:, c, :], wi[:, bv, c, :],
                                                          start=(c == 0), stop=(c == 7)),
                     r=xr + [("wi", bv)], w=[("pz", zz)])
            P.op("act", I("copy", out=vnew[:, :], in_=pz[zz][:T, :]), r=[("pz", zz)], w=[("vnew",)])
            for b in range(NSQ):
                P.dma("sp", nk_o[b, 2044:2048, :], knew[4 * b:4 * b + 4, :], r=[("knew",)])
                P.dma("sp", nv_o[b, 2044:2048, :], vnew[4 * b:4 * b + 4, :], r=[("vnew",)], w=[("nvrow", b)])

            mult = cstb[:, C_MULT:C_MULT + 28].rearrange("p (t s) -> p t s", t=7)
            for b in range(NSQ):
                bf = b % 2
                P.dma("sp", kc[:, bf, 0:4, :], ck[b, 1536:2048, :].rearrange("(t p) n -> p t n", p=128), w=[("kc", bf)])
                P.dma("pool", vnb[:, bf, :, 0:64], nv_o[b, 2044:2048, :].rearrange("s (h e) -> s h e", h=8),
                      r=[("nvrow", b), ("vnbi",)], w=[("vnb", bf)])
                P.dma("sp", vc[:, 0, 0:4, :], cv[b, 1536:2048, :].rearrange("(t p) n -> p t n", p=128), w=[("vc", 0)])
                for t in range(3):
                    for r in range(4):
                        P.dma("sp", kc[32 * r:32 * r + 32, bf, 4 + t, :], ck[b, t * 512 + r:(t + 1) * 512:16, :],
                              w=[("kc", bf)])
                        P.dma("sp", vc[32 * r:32 * r + 32, 0, 4 + t, :], cv[b, t * 512 + r:(t + 1) * 512:16, :],
                              w=[("vc", 0)])
                for c4 in range(4):
                    for tg in range(2):
                        tiles = list(range(4 * tg, min(7, 4 * tg + 4)))
                        bk = (2 * c4 + tg) % 2
                        for j, tl in enumerate(tiles):
                            P.op("pe", I("transpose",
                                tp[bk][:, j * 128:(j + 1) * 128], kc[:, bf, tl, c4 * 128:(c4 + 1) * 128], identf[:, :]),
                                r=[("kc", bf), ("identf",)], w=[("tp", bk)])
                        n = len(tiles) * 128
                        P.op("act", I("copy",
                            out=kcT[:, c4, tg * 512:tg * 512 + n], in_=tp[bk][:, :n]),
                            r=[("tp", bk)], w=[("kcT",)])
                for tl in range(7):
                    P.op("pool", I("tensor_copy",
                        out=vcb[:, tl, :, 0:64], in_=vc[:, 0, tl, :].rearrange("p (h e) -> p h e", h=8)),
                        r=[("vc", 0), ("vcbi",)], w=[("vcb",)])
                for par, bank, btok in ((0, psc, ("psc",)), (1, pub, ("pub",))):
                    pb = par * 64
                    for tl in range(7):
                        for h2 in range(4):
                            c4 = h2
                            P.op("pe", I("matmul",
                                bank[:, (tl * 4 + h2) * 4:(tl * 4 + h2) * 4 + 4], kcT[pb:pb + 64, c4, tl * 128:(tl + 1) * 128],
                                qTs[pb:pb + 64, c4, 4 * b:4 * b + 4], start=True, stop=True),
                                r=[("kcT",), ("qTs",)], w=[btok])
                    for h2 in range(4):
                        c4 = h2
                        P.op("pe", I("matmul",
                            bank[0:4, 112 + h2 * 4:112 + h2 * 4 + 4], kTs[pb:pb + 64, c4, 4 * b:4 * b + 4],
                            qTs[pb:pb + 64, c4, 4 * b:4 * b + 4], start=True, stop=True),
                            r=[("kTs",), ("qTs",)], w=[btok])
                    P.op("act", I("activation", out=pexp[:, :, par:8:2, :],
                                  in_=bank[:, 0:112].rearrange("p (t h s) -> p t h s", t=7, h=4),
                                  func=AF.Exp, scale=0.125), r=[btok], w=[("pexp",)])
                    P.op("act", I("activation", out=pexn[:, par:8:2, :],
                                  in_=bank[0:4, 112:128].rearrange("p (h s) -> p h s", h=4),
                                  func=AF.Exp, scale=0.125), r=[btok], w=[("pexn",)])
                P.op("dve", I("tensor_tensor",
                    out=pTs[:, :, :, :], in0=pexp[:, :, :, :],
                    in1=mult.unsqueeze(2).broadcast_to([128, 7, 8, 4]), op=ALU.mult),
                    r=[("pexp",), ("cstbh",)], w=[("pTs",)])
                P.op("dve", I("tensor_tensor",
                    out=pTn[:, :, :], in0=pexn[:, :, :],
                    in1=cstb[0:4, C_MNEW:C_MNEW + 4].unsqueeze(1).broadcast_to([4, 8, 4]), op=ALU.mult),
                    r=[("pexn",), ("cstbh",)], w=[("pTn",)])
                for h in range(8):
                    for tl in range(7):
                        P.op("pe", I("matmul",
                            pos[0:65, h * 4:h * 4 + 4], vcb[:, tl, h, :], pTs[:, tl, h, :],
                            start=(tl == 0), stop=False), r=[("vcb",), ("pTs",)], w=[("pos",)])
                    P.op("pe", I("matmul",
                        pos[0:65, h * 4:h * 4 + 4], vnb[0:4, bf, h, :], pTn[0:4, h, :], start=False, stop=True),
                        r=[("vnb", bf), ("pTn",)], w=[("pos",)])
                P.op("act", I("copy", out=oS[:, b, :], in_=pos[0:65, 0:32]), r=[("pos",)], w=[("oS",)])
            oSf = oS[:, :, :].rearrange("p b x -> p (b x)")
            P.op("dve", I("reciprocal", out=rdn[64:65, :], in_=oSf[64:65, :]), r=[("oS",)], w=[("rdn",)])
            P.op("dve", I("tensor_copy", out=rhiS[64:65, :], in_=rdn[64:65, :]), r=[("rdn",)], w=[("rhiS",)])
            P.op("dve", I("tensor_tensor", out=rloS[64:65, :], in0=rdn[64:65, :], in1=rhiS[64:65, :], op=ALU.subtract),
                 r=[("rdn",), ("rhiS",)], w=[("rloS",)])
            P.op("pe", I("matmul", pz[0][:64, :], oneshS[64:65, 0:64], rhiS[64:65, :], start=True, stop=False),
                 r=[("rhiS",), ("onesf",)], w=[("pz", 0)])
            P.op("pe", I("matmul", pz[0][:64, :], oneshS[64:65, 0:64], rloS[64:65, :], start=False, stop=True),
                 r=[("rloS",), ("onesf",)], w=[("pz", 0)])
            P.op("dve", I("tensor_tensor",
                out=catAs[:, :, :].rearrange("p h (b s) -> p b h s", s=4),
                in0=oS[0:64, :, :].rearrange("p b (h s) -> p b h s", s=4),
                in1=pz[0][:64, :].rearrange("p (b h s) -> p b h s", h=8, s=4), op=ALU.mult),
                r=[("pz", 0), ("oS",)], w=[("catAs",)])

            hgrn_gates(d, T, xT, xr, wi, load_wblock, pz, zc, False, True, cstf[:, C_RMASKS:C_RMASKS + T], 4)
            kendT, kdec, qdec, ecum = d["kendT"], d["kdec"], d["qdec"], d["ecum"]
            bw = load_wblock(W_IH)
            for hh in range(4):
                zz = zc[0] % 2
                zc[0] += 1
                for c in range(8):
                    P.op("pe", I("matmul",
                        pz[zz][:, :T], wi[:, bw, c, hh * 128:(hh + 1) * 128], xT[:, c, :],
                        start=(c == 0), stop=(c == 7)), r=xr + [("wi", bw)], w=[("pz", zz)])
                P.op("act", I("copy", out=iT[:, hh, :], in_=pz[zz][:, :T]),
                     r=[("pz", zz)], w=[("iT",)])
            for b in range(NSQ):
                for hh in range(4):
                    P.op("pe", I("matmul",
                        psc[0:4, (b * 4 + hh) * 4:(b * 4 + hh) * 4 + 4], kdec[:, hh, 4 * b:4 * b + 4],
                        qdec[:, hh, 4 * b:4 * b + 4], start=True, stop=True),
                        r=[("kdec", hh), ("qdec", hh)], w=[("psc",)])
            P.op("dve", I("tensor_tensor",
                out=ATs[:, :, :], in0=psc[0:4, 0:256].rearrange("p (x i) -> p x i", i=4),
                in1=cstb[0:4, C_MS4:C_MS4 + 4].unsqueeze(1).broadcast_to([4, 64, 4]), op=ALU.mult),
                r=[("psc",), ("cstbh",)], w=[("ATs",)])
            for b in range(NSQ):
                bf = b % 2
                for hh in range(4):
                    P.op("pe", I("transpose",
                        tpb[0:4, hh * 128:(hh + 1) * 128], kendT[:, hh, 4 * b:4 * b + 4], identb[:, :]),
                        r=[("kendT", hh), ("identb",)], w=[("tpb",)])
                for hh in range(4):
                    P.op("pe", I("transpose",
                        tpb[0:4, (4 + hh) * 128:(5 + hh) * 128], iT[:, hh, 4 * b:4 * b + 4], identb[:, :]),
                        r=[("iT",), ("identb",)], w=[("tpb",)])
                P.op("act", I("copy", out=ksvs[:, bf, :, :].rearrange("p x n -> p (x n)"), in_=tpb[0:4, :]),
                     r=[("tpb",)], w=[("ksvs", bf)])
                P.op("pool", I("tensor_copy", out=S0b[:, bf, :, :], in_=S0f[:, 4 * b:4 * b + 4, :]),
                     r=[("S0f", b // 4)], w=[("S0b", bf)])
                for hh in range(4):
                    bh = 4 * b + hh
                    P.op("pe", I("matmul",
                        pos[:, bh * 4:bh * 4 + 4], S0b[:, bf, hh, :], qdec[:, hh, 4 * b:4 * b + 4],
                        start=True, stop=False), r=[("S0b", bf), ("qdec", hh)], w=[("pos",)])
                    P.op("pe", I("matmul",
                        pos[:, bh * 4:bh * 4 + 4], ksvs[0:4, bf, 4 + hh, :], ATs[0:4, bh, :],
                        start=False, stop=True), r=[("ksvs", bf), ("ATs",)], w=[("pos",)])
                for hh in range(4):
                    P.op("pe", I("matmul",
                        pub[:, hh * 128:(hh + 1) * 128], ksvs[0:4, bf, hh, :], ksvs[0:4, bf, 4 + hh, :],
                        start=True, stop=True), r=[("ksvs", bf)], w=[("pub",)])
                for hh in range(4):
                    bh = 4 * b + hh
                    P.op("dve", I("scalar_tensor_tensor",
                        out=S0f[:, bh, :], in0=S0f[:, bh, :], scalar=ecum[:, hh, 4 * b + 3:4 * b + 4],
                        in1=pub[:, hh * 128:(hh + 1) * 128], op0=ALU.mult, op1=ALU.add),
                        r=[("pub",), ("ecum", hh), ("S0b", bf)], w=[("S0f", b // 4)])
            for i in range(4):
                P.dma("sp", ns_o[:, 16 * i:16 * i + 16, :], S0f[:, 16 * i:16 * i + 16, :], r=[("S0f", i)])
            for hh in range(4):
                hgrn_out(d, T, hh, pos[:, 0:256].rearrange("p (b h s) -> p b h s", h=4, s=4)[:, :, hh, :],
                         ("pos",), pz, zc)
            catH = d["catH"]

            def wout_mm(t, hf, pdt, pdtok, tt):
                for h in range(8):
                    P.op("pe", I("matmul",
                        pdt[:tt, :], catAs[:, h, :], woA[:, h, hf * 512:(hf + 1) * 512],
                        start=(h == 0), stop=False), r=[("catAs",), ("woA",)], w=[pdtok])
                for hh in range(4):
                    P.op("pe", I("matmul",
                        pdt[:tt, :], catH[:, hh, :], woH[:, hh, hf * 512:(hf + 1) * 512],
                        start=False, stop=(hh == 3)), r=[("catH",), ("woH",)], w=[pdtok])

            ln_epilogue(T, 1, x1s, 2 * L, [(x2s, L)], None, [(pz[0], ("pz", 0)), (pz[1], ("pz", 1))],
                        yb, gb, stats, mv, rstd, wout_mm)
            P.flush("B3")

        if "C" not in PHASES:
            return nc
        groupsC = [(512 * g, 512) for g in range(4)] + [(L, NS)]
        ffn_phase("C2", x2s, groupsC, lambda gi: [(y_o, groupsC[gi][0])], w2g, w2u, w2d, 4)
    return nc


def _rope_tables(positions):
    half = 32
    inv = (np.float32(10000.0) ** (-np.arange(half, dtype=np.float32) / np.float32(half))).astype(np.float32)
    ang = positions.astype(np.float32)[None, :] * inv[:, None]
    cos = np.cos(ang).astype(np.float32)
    sin = np.sin(ang).astype(np.float32)
    cosT = np.tile(cos, (4, 1))
    sinS = np.concatenate([-sin, sin, -sin, sin], axis=0)
    return cosT, sinS


def _consts(half_flag, hg_lower_bound, hg_norm_g):
    c = np.zeros((128, NCST), np.float32)
    k = np.arange(128)[:, None]
    q = np.arange(128)[None, :]
    c[:, C_MCUR:C_MCUR + 128] = (k <= q)
    c[:, C_MPREV:C_MPREV + 128] = (k >= q)
    c[:, C_MPREVH:C_MPREVH + 128] = (k >= q) * float(half_flag)
    c[:, C_M64:C_M64 + 64] = ((np.arange(128)[:, None] % 64) <= np.arange(64)[None, :])
    c[:, C_RMASK:C_RMASK + 512] = (np.arange(512) % 64 != 0)[None, :]
    c[:, C_RMASKS:C_RMASKS + 64] = (np.arange(64) % 4 != 0)[None, :]
    c[:, C_FLAG] = float(half_flag)
    lb = hg_lower_bound.reshape(2, 4, 128)
    c[:, C_LB:C_LB + 8] = lb.transpose(2, 0, 1).reshape(128, 8)
    c[:, C_GN:C_GN + 4] = hg_norm_g.reshape(4, 128).T
    p = np.arange(128)
    mult = np.zeros((128, 7, 4), np.float32)
    for t in range(7):
        if t < 4:
            row = 1536 + 128 * t + p
        else:
            row = 16 * ((t - 4) * 32 + p % 32) + p // 32
        for s_ in range(4):
            mult[:, t, s_] = ((row >= 1920 + s_).astype(np.float32) + ((row >= 1536) & (row % 4 == s_))
                              + (row % 16 == s_))
    c[:, C_MULT:C_MULT + 28] = mult.reshape(128, 28)
    sp = np.arange(4)[:, None]
    sq = np.arange(4)[None, :]
    c[0:4, C_MNEW:C_MNEW + 4] = (sp < sq) * 1.0 + (sp == sq) * 3.0
    c[0:4, C_MS4:C_MS4 + 4] = (sp <= sq)
    return c


_PERM = None


def _swap_perm():
    idx = np.arange(512).reshape(8, 2, 32)
    return idx[:, ::-1, :].reshape(512)


def kernel(x_prompt, x_sample, cache_k, cache_v, state_hgrn,
           ffn1_w_gate, ffn1_w_up, ffn1_w_down, ln1_g, ln1_b,
           w_in, hg_lower_bound, hg_norm_g, w_out, ln2_g, ln2_b,
           ffn2_w_gate, ffn2_w_up, ffn2_w_down, ln3_g, ln3_b):
    f = lambda a: np.ascontiguousarray(np.asarray(a, dtype=np.float32))
    x_prompt, x_sample = f(x_prompt), f(x_sample)
    cache_k, cache_v, state_hgrn = np.asarray(cache_k), np.asarray(cache_v), np.asarray(state_hgrn)
    win0 = f(w_in)[0]
    perm = _swap_perm()
    win_ext = np.ascontiguousarray(np.concatenate(
        [win0, win0[:, 0:512][:, perm], win0[:, 512:1024][:, perm]], axis=1))
    lnv = np.ascontiguousarray(np.stack([f(ln1_g)[0], f(ln1_b)[0], f(ln2_g)[0], f(ln2_b)[0], f(ln3_g)[0], f(ln3_b)[0]]))
    shared = {
        "w1g": f(ffn1_w_gate)[0], "w1u": f(ffn1_w_up)[0], "w1d": f(ffn1_w_down)[0],
        "w2g": f(ffn2_w_gate)[0], "w2u": f(ffn2_w_up)[0], "w2d": f(ffn2_w_down)[0],
        "win": win_ext, "wout": f(w_out)[0], "lnv": lnv,
    }
    hlb, hng = f(hg_lower_bound), f(hg_norm_g)[0]
    in_maps = []
    for c in range(NCORES):
        b, half = c // 2, c % 2
        xs = x_sample[16 * c:16 * c + 16].reshape(NS, D)
        if half == 0:
            halo = np.zeros((L, D), np.float32)
            pos_h = np.zeros(L, np.int64)
        else:
            halo = x_prompt[b, 0:L]
            pos_h = np.arange(0, L)
        xin = np.concatenate([halo, x_prompt[b, half * L:(half + 1) * L], xs], axis=0)
        pos = np.concatenate([pos_h, np.arange(half * L, (half + 1) * L), 8192 + (np.arange(NS) % 4)])
        cosT, sinS = _rope_tables(pos)
        m = {}
        for kk, vv in shared.items():
            if kk == "lnv":
                m[kk] = vv
            else:
                m[kk] = np.concatenate([vv, np.full((1, vv.shape[1]), float(c), np.float32)], axis=0)
        m["xin"] = np.ascontiguousarray(xin)
        m["ck"] = np.ascontiguousarray(cache_k[0, 16 * c:16 * c + 16].reshape(16, 2048, 512), dtype=np.float32)
        m["cv"] = np.ascontiguousarray(cache_v[0, 16 * c:16 * c + 16].reshape(16, 2048, 512), dtype=np.float32)
        m["st"] = np.ascontiguousarray(state_hgrn[0, 16 * c:16 * c + 16].reshape(64 * 128, 128), dtype=np.float32)
        m["cst"] = _consts(half, hlb, hng)
        m["rope"] = np.ascontiguousarray(np.stack([cosT, sinS]))
        in_maps.append(m)
    import os
    if os.environ.get("KDBG"):
        ncr = int(os.environ.get("KCORES", "8"))
        nc = build_program(tuple(os.environ["KDBG"].split(",")))
        ims = in_maps[:ncr]
        for m in ims:
            m["ck"], m["cv"] = m["ck"][:NSQ], m["cv"][:NSQ]
        res = run_bass_kernel_spmd(nc, ims, core_ids=list(range(ncr)), trace=bool(os.environ.get("KTRACE")))
        return res, in_maps
    nc = build_program()
    res = run_bass_kernel_spmd(nc, in_maps, core_ids=list(range(NCORES)))
    R = res.results
    y_prompt = np.empty((4, 4096, D), np.float32)
    y_sample = np.empty((128, 4, D), np.float32)
    nkp = np.empty((1, 4, 2048, 8, 64), np.float32)
    nvp = np.empty((1, 4, 2048, 8, 64), np.float32)
    nsp = np.empty((1, 4, 4, 128, 128), np.float32)
    nks = np.empty((1, 128, 2048, 8, 64), np.float32)
    nvs = np.empty((1, 128, 2048, 8, 64), np.float32)
    nss = np.empty((1, 128, 4, 128, 128), np.float32)
    for c in range(NCORES):
        b, half = c // 2, c % 2
        r = R[c]
        y_prompt[b, half * L:(half + 1) * L] = r["y"][0:L]
        y_sample[16 * c:16 * c + 16] = r["y"][L:L + NS].reshape(16, 4, D)
        if half == 1:
            nkp[0, b] = r["kTo"].T.reshape(2048, 8, 64)
            nvp[0, b] = r["vo"].reshape(2048, 8, 64)
            nsp[0, b] = r["so"].transpose(1, 0, 2)
        nks[0, 16 * c:16 * c + 16] = r["nk"].reshape(16, 2048, 8, 64)
        nvs[0, 16 * c:16 * c + 16] = r["nv"].reshape(16, 2048, 8, 64)
        nss[0, 16 * c:16 * c + 16] = r["ns"].reshape(128, 16, 4, 128).transpose(1, 2, 0, 3)
    return (y_prompt, y_sample, nkp, nvp, nsp, nks, nvs, nss)
```
